# Optimizing a Trainium2 kernel written in Bass

```python
import math
import jax, jax.numpy as jnp
from jax import lax
import numpy as np

D_MODEL = 1024
BATCH = 8
SEQ = 2048
DEPTH = 2

MIX_W = D_MODEL // 2
N_BRANCH = 3
HG_KEY = 128
HG_HEADS = MIX_W // HG_KEY
HG_VAL = MIX_W // HG_HEADS
HG_CHUNK = 64
FOX_HEAD_DIM = 64
FOX_HEADS = MIX_W // FOX_HEAD_DIM
FOX_BLOCK = 128
S5_GROUP_CH = 16
S5_GROUPS = MIX_W // S5_GROUP_CH
S5_STATE = 64
S5_DT_MIN = 1e-3
S5_DT_MAX = 1e-1
FFN_HIDDEN = -(-8 * D_MODEL // (3 * 256)) * 256
DEEPNORM_ALPHA = (2 * DEPTH) ** 0.25
DEEPNORM_BETA = (8 * DEPTH) ** -0.25
LN_EPS = 1e-5
RMS_EPS = 1e-6
IN_WIDTHS = (HG_HEADS * HG_KEY, HG_HEADS * HG_KEY, MIX_W, MIX_W,
             MIX_W, MIX_W, MIX_W, FOX_HEADS,
             MIX_W,
             N_BRANCH * D_MODEL)
IN_TOTAL = sum(IN_WIDTHS)

kernel_name = 'hybrid_hgrn2_fox_s5_deepnorm'


def _split_points():
    return [int(v) for v in np.cumsum(IN_WIDTHS)[:-1]]


def layer_norm(x, g, b):
    xf = x.astype(jnp.float32)
    mu = jnp.mean(xf, axis=-1, keepdims=True)
    var = jnp.mean(jnp.square(xf - mu), axis=-1, keepdims=True)
    return ((xf - mu) * lax.rsqrt(var + LN_EPS) * g + b).astype(x.dtype)


def hgrn2_branch(q, f_logit, i, gate, lb, norm_w):
    f32 = jnp.float32
    bsz, seq_len, _ = q.shape
    n_chunks = seq_len // HG_CHUNK
    f = lb + (1.0 - lb) * jax.nn.sigmoid(f_logit.astype(f32))
    k = 1.0 - f

    def chunks(t, width):
        t = t.astype(f32).reshape(bsz, n_chunks, HG_CHUNK, HG_HEADS, width)
        return t.transpose(1, 0, 3, 2, 4)

    qc = chunks(q, HG_KEY)
    kc = chunks(k, HG_KEY)
    vc = chunks(i, HG_VAL)
    gc = jnp.cumsum(chunks(jnp.log(f), HG_KEY), axis=3)
    causal = jnp.tril(jnp.ones((HG_CHUNK, HG_CHUNK), dtype=bool))[:, :, None]

    def step(state, xs):
        qb, kb, vb, gb = xs
        g_last = gb[:, :, -1, :]
        o_inter = jnp.einsum('bhtk,bhkv->bhtv', qb * jnp.exp(gb), state)
        rel = gb[:, :, :, None, :] - gb[:, :, None, :, :]
        decay = jnp.exp(jnp.where(causal, rel, -jnp.inf))
        scores = jnp.einsum('bhtsk,bhsk->bhts', qb[:, :, :, None, :] * decay, kb)
        o_intra = jnp.einsum('bhts,bhsv->bhtv', scores, vb)
        k_to_end = kb * jnp.exp(g_last[:, :, None, :] - gb)
        new_state = (jnp.exp(g_last)[..., None] * state
                     + jnp.einsum('bhsk,bhsv->bhkv', k_to_end, vb))
        return new_state, o_inter + o_intra

    state0 = jnp.zeros((bsz, HG_HEADS, HG_KEY, HG_VAL), f32)
    _, o = lax.scan(step, state0, (qc, kc, vc, gc))
    o = o.transpose(1, 0, 3, 2, 4).reshape(bsz, seq_len, HG_HEADS, HG_VAL)
    o = o * lax.rsqrt(jnp.mean(jnp.square(o), axis=-1, keepdims=True) + RMS_EPS) * norm_w.astype(f32)
    g = gate.astype(f32).reshape(bsz, seq_len, HG_HEADS, HG_VAL)
    return (o * jax.nn.silu(g)).reshape(bsz, seq_len, MIX_W)


def fox_branch(q, k, v, f_logit):
    f32 = jnp.float32
    bsz, seq_len, _ = q.shape
    shp = (bsz, seq_len, FOX_HEADS, FOX_HEAD_DIM)
    q = q.reshape(shp)
    k = k.reshape(shp)
    v = v.reshape(shp)
    cum = jnp.cumsum(jax.nn.log_sigmoid(f_logit.astype(f32)), axis=1).transpose(0, 2, 1)
    scale = FOX_HEAD_DIM ** -0.5
    outs = []
    for blk in range(seq_len // FOX_BLOCK):
        t0 = blk * FOX_BLOCK
        t1 = t0 + FOX_BLOCK
        logits = jnp.einsum('bthd,bshd->bhts', q[:, t0:t1], k[:, :t1]).astype(f32) * scale
        logits = logits + cum[:, :, t0:t1, None] - cum[:, :, None, :t1]
        mask = (t0 + jnp.arange(FOX_BLOCK))[:, None] >= jnp.arange(t1)[None, :]
        probs = jax.nn.softmax(jnp.where(mask, logits, -jnp.inf), axis=-1)
        outs.append(jnp.einsum('bhts,bshd->bthd', probs.astype(v.dtype), v[:, :t1]))
    return jnp.concatenate(outs, axis=1).reshape(bsz, seq_len, MIX_W)


def s5_branch(u, lam_re, lam_im, log_step, b_re, b_im, c_re, c_im, d, w_glu):
    f32 = jnp.float32
    bsz, seq_len, _ = u.shape
    uc = u.astype(f32).reshape(bsz, seq_len, S5_GROUPS, S5_GROUP_CH)
    lam = lax.complex(lam_re.astype(f32), lam_im.astype(f32))
    dt = jnp.exp(log_step.astype(f32))[:, None]
    lam_bar = jnp.exp(lam * dt)
    b_bar = ((lam_bar - 1.0) / lam)[:, :, None] * lax.complex(b_re.astype(f32), b_im.astype(f32))
    c_mat = lax.complex(c_re.astype(f32), c_im.astype(f32))
    bu = jnp.einsum('gph,blgh->lbgp', b_bar, uc.astype(jnp.complex64))
    a = jnp.broadcast_to(lam_bar, (seq_len, 1, S5_GROUPS, S5_STATE))

    def combine(e1, e2):
        a1, s1 = e1
        a2, s2 = e2
        return a1 * a2, a2 * s1 + s2

    _, states = lax.associative_scan(combine, (a, bu), axis=0)
    y = jnp.einsum('ghp,lbgp->blgh', c_mat, states).real + d.astype(f32) * uc
    y = jax.nn.gelu(y.reshape(bsz, seq_len, MIX_W))
    return y * jax.nn.sigmoid(y @ w_glu.astype(f32))


def mixer_sublayer(h, w_in, lb, hg_norm_w, fox_b_f, lam_re, lam_im, log_step, b_re, b_im,
                   c_re, c_im, s5_d, s5_w_glu, w_br_a, w_br_b, w_br_c, w_out):
    dt = h.dtype
    bsz, seq_len, _ = h.shape
    proj = jnp.einsum('bld,de->ble', h, w_in)
    (hg_q, hg_f, hg_i, hg_g, fx_q, fx_k, fx_v, fx_f, s5_u, gate_logits) = jnp.split(
        proj, _split_points(), axis=-1)
    y_a = hgrn2_branch(hg_q, hg_f, hg_i, hg_g, lb, hg_norm_w).astype(dt)
    y_b = fox_branch(fx_q, fx_k, fx_v, fx_f + fox_b_f).astype(dt)
    y_c = s5_branch(s5_u, lam_re, lam_im, log_step, b_re, b_im, c_re, c_im, s5_d, s5_w_glu).astype(dt)
    gates = jax.nn.sigmoid(gate_logits.astype(jnp.float32)).astype(dt).reshape(
        bsz, seq_len, N_BRANCH, D_MODEL)
    merged = (gates[:, :, 0] * (y_a @ w_br_a)
              + gates[:, :, 1] * (y_b @ w_br_b)
              + gates[:, :, 2] * (y_c @ w_br_c))
    return merged @ w_out


def swiglu_ffn(h, w_gate, w_up, w_down):
    return (jax.nn.silu(h @ w_gate) * (h @ w_up)) @ w_down


def _normal(k, shape, scale):
    return jax.random.normal(k, shape, jnp.float32) * scale


def setup_inputs(seed: int = 0) -> dict:
    key = jax.random.key(seed)
    ks = jax.random.split(key, 25)
    G, P, Hc = S5_GROUPS, S5_STATE, S5_GROUP_CH
    return {
        'x': _normal(ks[0], (BATCH, SEQ, D_MODEL), 1.0),
        'w_in': _normal(ks[1], (DEPTH, D_MODEL, IN_TOTAL), D_MODEL ** -0.5),
        'hg_lower_bounds': _normal(ks[2], (DEPTH, HG_HEADS * HG_KEY), 0.1),
        'hg_norm_w': 1.0 + _normal(ks[3], (DEPTH, HG_VAL), 0.02),
        'fox_b_f': _normal(ks[4], (DEPTH, FOX_HEADS), 0.1),
        's5_lambda_re': -0.5 + _normal(ks[5], (DEPTH, G, P), 0.01),
        's5_lambda_im': jnp.pi * jnp.arange(P, dtype=jnp.float32) + _normal(ks[6], (DEPTH, G, P), 0.01),
        's5_log_step': jax.random.uniform(ks[7], (DEPTH, G), jnp.float32,
                                          math.log(S5_DT_MIN), math.log(S5_DT_MAX)),
        's5_b_re': _normal(ks[8], (DEPTH, G, P, Hc), (2.0 * Hc) ** -0.5),
        's5_b_im': _normal(ks[9], (DEPTH, G, P, Hc), (2.0 * Hc) ** -0.5),
        's5_c_re': _normal(ks[10], (DEPTH, G, Hc, P), P ** -0.5),
        's5_c_im': _normal(ks[11], (DEPTH, G, Hc, P), P ** -0.5),
        's5_d': _normal(ks[12], (DEPTH, G, Hc), 1.0),
        's5_w_glu': _normal(ks[13], (DEPTH, MIX_W, MIX_W), MIX_W ** -0.5),
        'w_br_a': _normal(ks[14], (DEPTH, MIX_W, D_MODEL), MIX_W ** -0.5),
        'w_br_b': _normal(ks[15], (DEPTH, MIX_W, D_MODEL), MIX_W ** -0.5),
        'w_br_c': _normal(ks[16], (DEPTH, MIX_W, D_MODEL), MIX_W ** -0.5),
        'w_out': _normal(ks[17], (DEPTH, D_MODEL, D_MODEL), D_MODEL ** -0.5 * DEEPNORM_BETA),
        'ln1_g': 1.0 + _normal(ks[18], (DEPTH, D_MODEL), 0.02),
        'ln1_b': _normal(ks[19], (DEPTH, D_MODEL), 0.02),
        'w_ffn_gate': _normal(ks[20], (DEPTH, D_MODEL, FFN_HIDDEN), D_MODEL ** -0.5),
        'w_ffn_up': _normal(ks[21], (DEPTH, D_MODEL, FFN_HIDDEN), D_MODEL ** -0.5),
        'w_ffn_down': _normal(ks[22], (DEPTH, FFN_HIDDEN, D_MODEL), FFN_HIDDEN ** -0.5 * DEEPNORM_BETA),
        'ln2_g': 1.0 + _normal(ks[23], (DEPTH, D_MODEL), 0.02),
        'ln2_b': _normal(ks[24], (DEPTH, D_MODEL), 0.02),
    }


def reference(x, w_in, hg_lower_bounds, hg_norm_w, fox_b_f, s5_lambda_re, s5_lambda_im, s5_log_step,
              s5_b_re, s5_b_im, s5_c_re, s5_c_im, s5_d, s5_w_glu, w_br_a, w_br_b, w_br_c, w_out,
              ln1_g, ln1_b, w_ffn_gate, w_ffn_up, w_ffn_down, ln2_g, ln2_b):
    sm = jax.nn.softmax(hg_lower_bounds.astype(jnp.float32), axis=0)
    lower_bounds = jnp.cumsum(sm, axis=0) - sm[0:1]
    h = x
    for l in range(DEPTH):
        mix = mixer_sublayer(h, w_in[l], lower_bounds[l], hg_norm_w[l], fox_b_f[l],
                             s5_lambda_re[l], s5_lambda_im[l], s5_log_step[l], s5_b_re[l], s5_b_im[l],
                             s5_c_re[l], s5_c_im[l], s5_d[l], s5_w_glu[l],
                             w_br_a[l], w_br_b[l], w_br_c[l], w_out[l])
        h = layer_norm(DEEPNORM_ALPHA * h + mix, ln1_g[l], ln1_b[l])
        ffn = swiglu_ffn(h, w_ffn_gate[l], w_ffn_up[l], w_ffn_down[l])
        h = layer_norm(DEEPNORM_ALPHA * h + ffn, ln2_g[l], ln2_b[l])
    return h
```

```python
import math
import numpy as np
import concourse.bass as bass
import concourse.mybir as mybir
from concourse.bass_utils import run_bass_kernel_spmd

F32 = mybir.dt.float32
BF16 = mybir.dt.bfloat16
I32 = mybir.dt.int32
AF = mybir.ActivationFunctionType
ALU = mybir.AluOpType
AX = mybir.AxisListType

L = 2048
D = 1024
KT = 8
NTT = 16
DEPTH = 2
FFN = 2816
NFT = 22
ALPHA = (2 * DEPTH) ** 0.25
LN_EPS = 1e-5
RMS_EPS = 1e-6
C_HQ, C_HF, C_HI, C_HG = 0, 512, 1024, 1536
C_FQ, C_FK, C_FV, C_FF = 2048, 2560, 3072, 3584
C_SU = 3592
C_GT = 4104
TWO_PI = 2.0 * math.pi

INPUT_SPECS = [
    ("x", [L, D]), ("w_in", [2, 1024, 7176]), ("hg_lower_bounds", [2, 512]), ("hg_norm_w", [2, 128]),
    ("fox_b_f", [2, 8]), ("s5_lambda_re", [2, 32, 64]), ("s5_lambda_im", [2, 32, 64]),
    ("s5_log_step", [2, 32]), ("s5_b_re", [2, 32, 64, 16]), ("s5_b_im", [2, 32, 64, 16]),
    ("s5_c_re", [2, 32, 16, 64]), ("s5_c_im", [2, 32, 16, 64]), ("s5_d", [2, 32, 16]),
    ("s5_w_glu", [2, 512, 512]), ("w_br_a", [2, 512, 1024]), ("w_br_b", [2, 512, 1024]),
    ("w_br_c", [2, 512, 1024]), ("w_out", [2, 1024, 1024]), ("ln1_g", [2, 1024]), ("ln1_b", [2, 1024]),
    ("w_ffn_gate", [2, 1024, 2816]), ("w_ffn_up", [2, 1024, 2816]), ("w_ffn_down", [2, 2816, 1024]),
    ("ln2_g", [2, 1024]), ("ln2_b", [2, 1024]),
]


class Buf:
    __slots__ = ("name", "w", "r")

    def __init__(self, name):
        self.name = name
        self.w = None
        self.r = {}


class KB:
    NDS = 12

    def __init__(self, nc):
        self.nc = nc
        self.E = {"pe": nc.tensor, "act": nc.scalar, "dve": nc.vector, "pool": nc.gpsimd, "sp": nc.sync}
        self.sem = {}
        self.cnt = {}
        self.seen = {e: {} for e in self.E}
        for e in self.E:
            self.sem[e] = nc.alloc_semaphore("s_" + e)
            self.cnt[e] = 0
        self.dq = {}
        for q in ("sp", "pool"):
            keys = []
            for i in range(self.NDS):
                k = "d_%s%d" % (q, i)
                self.sem[k] = nc.alloc_semaphore(k)
                self.cnt[k] = 0
                keys.append(k)
            self.dq[q] = [keys, 0]
        self.nins = 0

    def _wait(self, eng, deps):
        need = {}
        for (s, v) in deps:
            if v > need.get(s, 0):
                need[s] = v
        seen = self.seen[eng]
        for s, v in need.items():
            if eng == "pe" and s == "pe":
                continue
            if seen.get(s, 0) < v:
                self.E[eng].wait_ge(self.sem[s], v)
                seen[s] = v

    @staticmethod
    def _deps(reads, writes):
        deps = []
        for b in reads:
            if b.w is not None:
                deps.append(b.w)
        for b in writes:
            if b.w is not None:
                deps.append(b.w)
            deps.extend(b.r.items())
        return deps

    @staticmethod
    def _commit(ev, reads, writes):
        s, v = ev
        for b in writes:
            b.w = ev
            b.r = {}
        for b in reads:
            if b in writes:
                continue
            if b.r.get(s, 0) < v:
                b.r[s] = v

    def op(self, eng, fn, reads=(), writes=()):
        self._wait(eng, self._deps(reads, writes))
        ins = fn(self.E[eng])
        self.cnt[eng] += 1
        ins.then_inc(self.sem[eng], 1)
        self._commit((eng, self.cnt[eng]), reads, writes)
        self.nins += 1

    def dma(self, q, out, in_, reads=(), writes=(), slow=False):
        keys, i = self.dq[q]
        k = keys[i % len(keys)]
        self.dq[q][1] = i + 1
        deps = self._deps(reads, writes)
        if self.cnt[k] > 0:
            deps.append((k, self.cnt[k]))
        self._wait(q, deps)
        if slow:
            ins = self.E[q].dma_start(out=out, in_=in_, allow_slow_non_contiguous=True)
        else:
            ins = self.E[q].dma_start(out=out, in_=in_)
        self.cnt[k] += 16
        ins.then_inc(self.sem[k], 16)
        self._commit((k, self.cnt[k]), reads, writes)
        self.nins += 1

    def barrier(self, engines=("pe", "act", "dve", "pool", "sp")):
        deps = [(s, v) for s, v in self.cnt.items() if v > 0]
        for e in engines:
            self._wait(e, deps)


class Arena:
    def __init__(self, nc, nbytes):
        self.n = nbytes // 2
        self.t = nc.alloc_sbuf_tensor("arena", [128, self.n], BF16)
        self.ptr = {}
        self.base = {}
        self.lim = {}

    def region(self, name, start, end):
        self.base[name] = start
        self.ptr[name] = start
        self.lim[name] = end

    def reset(self, name):
        self.ptr[name] = self.base[name]

    def alloc(self, region, shape, dtype, parts=128):
        esz = 4 if dtype in (F32, I32) else 2
        n = 1
        for s in shape:
            n *= s
        nb16 = (n * esz + 1) // 2
        nb16 = (nb16 + 15) // 16 * 16
        off = self.ptr[region]
        assert off + nb16 <= self.lim[region], "arena region %s overflow: %d + %d > %d" % (
            region, off, nb16, self.lim[region])
        self.ptr[region] = off + nb16
        v = self.t[0:parts, off:off + (n * esz) // 2]
        if esz == 4:
            v = v.bitcast(dtype)
        if len(shape) == 2:
            v = v.rearrange("p (a b) -> p a b", a=shape[0])
        elif len(shape) == 3:
            v = v.rearrange("p (a b c) -> p a b c", a=shape[0], b=shape[1])
        elif len(shape) == 4:
            v = v.rearrange("p (a b c d) -> p a b c d", a=shape[0], b=shape[1], c=shape[2])
        return v


def build(n_layers=DEPTH, dbg=False):
    nc = bass.Bass("TRN2", target_bir_lowering=False)
    kb = KB(nc)
    dr = {}
    for name, shp in INPUT_SPECS:
        dr[name] = nc.dram_tensor(name, shp, F32, kind="ExternalInput").ap()
    out_d = nc.dram_tensor("out", [L, D], F32, kind="ExternalOutput").ap()
    hscr = nc.dram_tensor("hscr", [L, D], F32, kind="Internal").ap()
    cumscr = nc.dram_tensor("cumscr", [8, L], F32, kind="Internal").ap()
    b_hscr = Buf("hscr")
    b_cumscr = Buf("cumscr")
    b_out = Buf("out")
    dbg_out = {}

    def dbg_tensor(name, shape):
        t = nc.dram_tensor(name, shape, F32, kind="ExternalOutput").ap()
        dbg_out[name] = t
        return t

    total = nc.sbuf_bytes_remaining - 1024
    total = total // 64 * 64
    ar = Arena(nc, total)
    NEL = ar.n
    P_END = 16384 + 3 * 4096 + 6144
    ar.region("P", 0, P_END)
    Y0 = P_END
    Y_END = Y0 + 3 * 8192
    ar.region("Y", Y0, Y_END)
    ar.region("R", Y_END, NEL)

    banks = []
    bankbufs = []
    for i in range(8):
        banks.append(nc.alloc_psum_tensor("pb%d" % i, [128, 512], F32)[:])
        bankbufs.append(Buf("pb%d" % i))
    rot = {"list": list(range(8)), "i": 0}

    def bank():
        lst = rot["list"]
        i = lst[rot["i"] % len(lst)]
        rot["i"] += 1
        return banks[i], bankbufs[i]

    hT = ar.alloc("P", [KT, L], BF16)
    b_hT = [[Buf("hT%d_%d" % (k, t)) for t in range(4)] for k in range(KT)]

    def hT_bufs(tb=None):
        if tb is None:
            return [b for row in b_hT for b in row]
        return [b_hT[k][tb] for k in range(KT)]

    wslab = [ar.alloc("P", [KT, 512], BF16) for _ in range(3)]
    b_wslab = [Buf("wslab%d" % i) for i in range(3)]
    ident_bf = ar.alloc("P", [128], BF16)
    maskU_bf = ar.alloc("P", [128], BF16)
    ones_bf = ar.alloc("P", [128], BF16)
    ident_f = ar.alloc("P", [128], F32)
    lng = ar.alloc("P", [D], F32)
    lnb = ar.alloc("P", [D], F32)
    b_lng, b_lnb = Buf("lng"), Buf("lnb")
    kk = ar.alloc("P", [33], F32)
    kk_i = ar.alloc("P", [33], I32)
    rowmask = ar.alloc("P", [8], F32)
    negmask = ar.alloc("P", [128], F32)
    eps_ln = ar.alloc("P", [1], F32)
    eps_rms = ar.alloc("P", [1], F32)
    b_const = Buf("const")

    ws = {"req": [], "issued": 0, "next": 0}

    def ws_request(fn):
        ws["req"].append(fn)

    def ws_get(k=1):
        n = ws["next"]
        while ws["issued"] < min(len(ws["req"]), n + 3):
            i = ws["issued"]
            ws["req"][i](wslab[i % 3], b_wslab[i % 3])
            ws["issued"] += 1
        ws["next"] = n + k
        outl = [(wslab[(n + j) % 3], b_wslab[(n + j) % 3]) for j in range(k)]
        return outl[0] if k == 1 else outl

    def wload(dst, bdst, src):
        kb.dma("pool", dst, src, writes=[bdst])

    def setup_consts():
        P = kb.op
        P("pool", lambda e: e.memset(ones_bf, 1.0), writes=[b_const])
        P("pool", lambda e: e.affine_select(out=ident_bf, in_=ones_bf, pattern=[[1, 128]],
                                            compare_op=ALU.is_equal, fill=0.0, base=0, channel_multiplier=-1),
          reads=[b_const], writes=[b_const])
        P("pool", lambda e: e.affine_select(out=maskU_bf, in_=ones_bf, pattern=[[1, 128]],
                                            compare_op=ALU.is_ge, fill=0.0, base=0, channel_multiplier=-1),
          reads=[b_const], writes=[b_const])
        P("dve", lambda e: e.tensor_copy(ident_f, ident_bf), reads=[b_const], writes=[b_const])
        P("dve", lambda e: e.tensor_scalar(negmask, maskU_bf, 30000.0, -30000.0, ALU.mult, ALU.add),
          reads=[b_const], writes=[b_const])
        P("pool", lambda e: e.iota(kk_i, pattern=[[1, 33]], base=0, channel_multiplier=0), writes=[b_const])
        P("pool", lambda e: e.memset(eps_ln, LN_EPS), reads=[b_const], writes=[b_const])
        P("pool", lambda e: e.memset(eps_rms, RMS_EPS), reads=[b_const], writes=[b_const])
        P("dve", lambda e: e.tensor_copy(kk, kk_i), reads=[b_const], writes=[b_const])
        P("pool", lambda e: e.memset(rowmask, 1.0), reads=[b_const], writes=[b_const])
        P("pool", lambda e: e.affine_select(out=rowmask, in_=rowmask, pattern=[[-16, 8]],
                                            compare_op=ALU.is_ge, fill=0.0, base=0, channel_multiplier=1),
          reads=[b_const], writes=[b_const])
        P("pool", lambda e: e.affine_select(out=rowmask, in_=rowmask, pattern=[[16, 8]],
                                            compare_op=ALU.is_ge, fill=0.0, base=15, channel_multiplier=-1),
          reads=[b_const], writes=[b_const])

    def proj_fm(slab, bslab, c0, ncols, tb, kts=KT, rhs_src=None, rhs_bufs=None):
        pb, bb = bank()
        for kt in range(kts):
            if rhs_src is None:
                rhs = hT[:, kt, tb * 512:(tb + 1) * 512]
                rb = [b_hT[kt][tb]]
            else:
                rhs = rhs_src[:, kt, tb * 512:(tb + 1) * 512]
                rb = rhs_bufs
            kb.op("pe", lambda e, kt=kt, rhs=rhs: e.matmul(pb[0:ncols, :], slab[:, kt, c0:c0 + ncols], rhs,
                                                           start=(kt == 0), stop=(kt == kts - 1)),
                  reads=[bslab] + rb, writes=[bb])
        return pb, bb

    def w_in_cols(l, c0, n):
        return dr["w_in"][l, :, c0:c0 + n].rearrange("(kt p) e -> p kt e", p=128)

    def transpose_to_hT(src_bf, bsrc, tt):
        pb, bb = bank()
        pv = pb[:].bitcast(BF16)
        for kt in range(KT):
            kb.op("pe", lambda e, kt=kt: e.transpose(pv[:, kt * 128:(kt + 1) * 128],
                                                     src_bf[:, kt * 128:(kt + 1) * 128], ident_bf),
                  reads=[bsrc, b_const], writes=[bb])
        kb.op("act", lambda e: e.copy(hT[:, :, tt * 128:(tt + 1) * 128],
                                      pv.rearrange("p (k t) -> p k t", k=KT)),
              reads=[bb], writes=[b_hT[k][tt // 4] for k in range(KT)])

    def layer_norm_tile(xs, bxs, g_t, b_t, out_f32, bout, stat, bstat):
        kb.op("dve", lambda e: e.bn_stats(stat[:, 0:6], xs[:, 0:512]), reads=[bxs], writes=[bstat])
        kb.op("dve", lambda e: e.bn_stats(stat[:, 6:12], xs[:, 512:1024]), reads=[bxs, bstat], writes=[bstat])
        kb.op("dve", lambda e: e.bn_aggr(stat[:, 12:14], stat[:, 0:12]), reads=[bstat], writes=[bstat])
        kb.op("act", lambda e: e.activation(out=stat[:, 14:15], in_=stat[:, 13:14], func=AF.Sqrt, bias=eps_ln[:, 0:1]),
              reads=[bstat, b_const], writes=[bstat])
        kb.op("dve", lambda e: e.reciprocal(stat[:, 14:15], stat[:, 14:15]), reads=[bstat], writes=[bstat])
        kb.op("dve", lambda e: e.scalar_tensor_tensor(stat[:, 15:16], stat[:, 12:13], -1.0, stat[:, 14:15],
                                                      ALU.mult, ALU.mult), reads=[bstat], writes=[bstat])
        kb.op("act", lambda e: e.activation(out=xs, in_=xs, func=AF.Identity, bias=stat[:, 15:16],
                                            scale=stat[:, 14:15]), reads=[bxs, bstat], writes=[bxs])
        kb.op("dve", lambda e: e.tensor_tensor(xs, xs, g_t, ALU.mult), reads=[bxs, b_lng], writes=[bxs])
        kb.op("dve", lambda e: e.tensor_tensor(out_f32, xs, b_t, ALU.add), reads=[bxs, b_lnb], writes=[bout])

    def load_input():
        ar.reset("R")
        xs = [ar.alloc("R", [D], F32) for _ in range(2)]
        xb = [ar.alloc("R", [D], BF16) for _ in range(2)]
        bxs = [Buf("xs0"), Buf("xs1")]
        bxb = [Buf("xb0"), Buf("xb1")]
        for tt in range(NTT):
            i = tt % 2
            kb.dma("sp", xs[i], dr["x"][tt * 128:(tt + 1) * 128, :], writes=[bxs[i]])
            kb.op("dve", lambda e, i=i: e.tensor_copy(xb[i], xs[i]), reads=[bxs[i]], writes=[bxb[i]])
            transpose_to_hT(xb[i], bxb[i], tt)

    def hgrn2(l, yT_a, b_yT_a):
        ar.reset("R")
        R = lambda shape, dt, parts=128: ar.alloc("R", shape, dt, parts)
        lbraw = R([2, 4], F32)
        lb = R([4], F32)
        oml = R([4], F32)
        normw = R([1], F32)
        b_sm = Buf("hg_small")
        scanmask = R([L], F32)
        A = R([L], F32)
        B = R([L], F32)
        G = R([L], F32)
        bA, bB, bG = Buf("A"), Buf("B"), Buf("G")
        qg = R([L], BF16)
        kg = R([L], BF16)
        kte = R([L], BF16)
        bqg, bkg, bkte = Buf("qg"), Buf("kg"), Buf("kte")
        vtok = R([32, 128], BF16)
        kteT = R([32, 128], BF16)
        sTm = R([32, 64], BF16)
        states_bf = R([32, 128], BF16)
        st_pp = [R([128], F32), R([128], F32)]
        egl = R([32], F32)
        bvtok, bkteT, bsTm, bstbf, begl = Buf("vtok"), Buf("kteT"), Buf("sTm"), Buf("stbf"), Buf("egl")
        bstpp = [Buf("stpp0"), Buf("stpp1")]

        kb.dma("sp", lbraw, dr["hg_lower_bounds"].rearrange("l (h k) -> k l h", k=128), writes=[b_sm], slow=True)
        kb.dma("sp", normw, dr["hg_norm_w"][l].rearrange("(p o) -> p o", o=1), writes=[b_sm])
        if l == 0:
            kb.op("dve", lambda e: e.memset(lb, 0.0), reads=[b_sm], writes=[b_sm])
        else:
            kb.op("dve", lambda e: e.tensor_tensor(lb, lbraw[:, 1, :], lbraw[:, 0, :], ALU.subtract),
                  reads=[b_sm], writes=[b_sm])
            kb.op("act", lambda e: e.activation(out=lb, in_=lb, func=AF.Sigmoid), reads=[b_sm], writes=[b_sm])
        kb.op("dve", lambda e: e.tensor_scalar(oml, lb, -1.0, 1.0, ALU.mult, ALU.add), reads=[b_sm], writes=[b_sm])
        kb.op("pool", lambda e: e.memset(scanmask, 1.0), writes=[b_sm])
        kb.op("pool", lambda e: e.memset(scanmask.rearrange("p (c s) -> p c s", s=64)[:, :, 0:1], 0.0),
              reads=[b_sm], writes=[b_sm])

        for h in range(4):
            def req(dst, bdst, h=h):
                for j, c0 in enumerate((C_HQ, C_HF, C_HI, C_HG)):
                    kb.dma("pool", dst[:, :, j * 128:(j + 1) * 128], w_in_cols(l, c0 + h * 128, 128), writes=[bdst])
            ws_request(req)
        for h in range(4):
            slab, bslab = ws_get()
            G3 = G.rearrange("p (c s) -> p c s", s=64)
            B3 = B.rearrange("p (c s) -> p c s", s=64)
            for tb in range(4):
                pb, bb = proj_fm(slab, bslab, 128, 128, tb)
                kb.op("act", lambda e, tb=tb, pb=pb: e.activation(out=A[:, tb * 512:(tb + 1) * 512], in_=pb,
                                                                  func=AF.Sigmoid), reads=[bb], writes=[bA])
            kb.op("dve", lambda e, h=h: e.tensor_scalar(A, A, oml[:, h:h + 1], lb[:, h:h + 1], ALU.mult, ALU.add),
                  reads=[bA, b_sm], writes=[bA])
            kb.op("act", lambda e: e.activation(out=B, in_=A, func=AF.Ln), reads=[bA], writes=[bB])
            kb.op("dve", lambda e: e.tensor_scalar(A, A, -1.0, 1.0, ALU.mult, ALU.add), reads=[bA], writes=[bA])
            kb.op("dve", lambda e: e.tensor_tensor_scan(G, scanmask, B, 0.0, ALU.mult, ALU.add),
                  reads=[bB, b_sm], writes=[bG])
            kb.op("act", lambda e: e.activation(out=B, in_=G, func=AF.Exp), reads=[bG], writes=[bB])
            for tb in range(4):
                pb, bb = proj_fm(slab, bslab, 0, 128, tb)
                kb.op("dve", lambda e, tb=tb, pb=pb: e.tensor_tensor(qg[:, tb * 512:(tb + 1) * 512], pb,
                                                                     B[:, tb * 512:(tb + 1) * 512], ALU.mult),
                      reads=[bb, bB], writes=[bqg])
            kb.op("act", lambda e: e.activation(out=B, in_=G, func=AF.Exp, scale=-1.0), reads=[bG], writes=[bB])
            kb.op("dve", lambda e: e.tensor_tensor(kg, A, B, ALU.mult), reads=[bA, bB], writes=[bkg])
            kb.op("dve", lambda e: e.tensor_tensor(B3, G3[:, :, 63:64].to_broadcast([128, 32, 64]), G3,
                                                   ALU.subtract), reads=[bG], writes=[bB])
            kb.op("act", lambda e: e.activation(out=B, in_=B, func=AF.Exp), reads=[bB], writes=[bB])
            kb.op("dve", lambda e: e.tensor_tensor(kte, A, B, ALU.mult), reads=[bA, bB], writes=[bkte])
            kb.op("act", lambda e: e.activation(out=egl, in_=G3[:, :, 63], func=AF.Exp), reads=[bG], writes=[begl])
            for c4 in range(8):
                pb, bb = bank()
                for cc in range(4):
                    c = c4 * 4 + cc
                    for kt in range(KT):
                        kb.op("pe", lambda e, kt=kt, c=c, cc=cc, pb=pb: e.matmul(
                            pb[0:64, cc * 128:(cc + 1) * 128], hT[:, kt, c * 64:(c + 1) * 64],
                            slab[:, kt, 256:384], start=(kt == 0), stop=(kt == KT - 1)),
                            reads=[bslab, b_hT[kt][c // 8]], writes=[bb])
                kb.op("act", lambda e, c4=c4, pb=pb: e.copy(vtok[0:64, c4 * 4:(c4 + 1) * 4, :],
                                                            pb[0:64, :].rearrange("p (a b) -> p a b", a=4)),
                      reads=[bb], writes=[bvtok])
            for tb in range(4):
                pb, bb = proj_fm(slab, bslab, 384, 128, tb)
                kb.op("act", lambda e, tb=tb, pb=pb: e.activation(out=A[:, tb * 512:(tb + 1) * 512], in_=pb,
                                                                  func=AF.Silu), reads=[bb], writes=[bA])
            for c8 in range(4):
                pb, bb = bank()
                for cc in range(8):
                    c = c8 * 8 + cc
                    kb.op("pe", lambda e, c=c, cc=cc, pb=pb: e.matmul(
                        pb[0:64, cc * 64:(cc + 1) * 64], kg[:, c * 64:(c + 1) * 64], qg[:, c * 64:(c + 1) * 64],
                        start=True, stop=True), reads=[bkg, bqg], writes=[bb])
                kb.op("dve", lambda e, c8=c8, pb=pb: e.tensor_tensor(
                    sTm[0:64, c8 * 8:(c8 + 1) * 8, :], pb[0:64, :].rearrange("p (a b) -> p a b", a=8),
                    maskU_bf[0:64, 0:64].unsqueeze(1).to_broadcast([64, 8, 64]), ALU.mult),
                    reads=[bb, b_const], writes=[bsTm])
            for c8 in range(4):
                pb, bb = bank()
                pv = pb[:].bitcast(BF16)
                for cc in range(8):
                    c = c8 * 8 + cc
                    kb.op("pe", lambda e, c=c, cc=cc, pv=pv: e.transpose(
                        pv[0:64, cc * 128:(cc + 1) * 128], kte[:, c * 64:(c + 1) * 64], ident_bf),
                        reads=[bkte, b_const], writes=[bb])
                kb.op("act", lambda e, c8=c8, pv=pv: e.copy(kteT[0:64, c8 * 8:(c8 + 1) * 8, :],
                                                            pv[0:64, :].rearrange("p (a b) -> p a b", a=8)),
                      reads=[bb], writes=[bkteT])
            for c4 in range(8):
                pb, bb = bank()
                for cc in range(4):
                    c = c4 * 4 + cc
                    kb.op("pe", lambda e, c=c, cc=cc, pb=pb: e.matmul(
                        pb[:, cc * 128:(cc + 1) * 128], kteT[0:64, c, :], vtok[0:64, c, :], start=True, stop=True),
                        reads=[bkteT, bvtok], writes=[bb])
                for cc in range(4):
                    c = c4 * 4 + cc
                    cur, prv = st_pp[c % 2], st_pp[(c + 1) % 2]
                    bcur, bprv = bstpp[c % 2], bstpp[(c + 1) % 2]
                    if c == 0:
                        kb.op("dve", lambda e, pb=pb, cur=cur: e.tensor_copy(cur, pb[:, 0:128]),
                              reads=[bb], writes=[bcur])
                    else:
                        kb.op("dve", lambda e, pb=pb, cur=cur, prv=prv, c=c, cc=cc: e.scalar_tensor_tensor(
                            cur, prv, egl[:, c:c + 1], pb[:, cc * 128:(cc + 1) * 128], ALU.mult, ALU.add),
                            reads=[bb, bprv, begl], writes=[bcur])
                    kb.op("act", lambda e, cur=cur, c=c: e.copy(states_bf[:, c, :], cur), reads=[bcur],
                          writes=[bstbf])
            for c8 in range(4):
                pb, bb = bank()
                for cc in range(8):
                    c = c8 * 8 + cc
                    if c >= 1:
                        kb.op("pe", lambda e, c=c, cc=cc, pb=pb: e.matmul(
                            pb[:, cc * 64:(cc + 1) * 64], states_bf[:, c - 1, :], qg[:, c * 64:(c + 1) * 64],
                            start=True, stop=False), reads=[bstbf, bqg], writes=[bb])
                    kb.op("pe", lambda e, c=c, cc=cc, pb=pb: e.matmul(
                        pb[:, cc * 64:(cc + 1) * 64], vtok[0:64, c, :], sTm[0:64, c, :],
                        start=(c == 0), stop=True), reads=[bvtok, bsTm], writes=[bb])
                kb.op("act", lambda e, c8=c8, pb=pb: e.copy(B[:, c8 * 512:(c8 + 1) * 512], pb), reads=[bb],
                      writes=[bB])
                kb.op("act", lambda e, c8=c8, pb=pb: e.activation(out=kg[:, c8 * 512:(c8 + 1) * 512], in_=pb,
                                                                  func=AF.Square), reads=[bb], writes=[bkg])
            for tb in range(4):
                pb, bb = bank()
                kb.op("pe", lambda e, tb=tb, pb=pb: e.matmul(pb, ones_bf, kg[:, tb * 512:(tb + 1) * 512],
                                                             start=True, stop=True), reads=[bkg, b_const],
                      writes=[bb])
                kb.op("act", lambda e, tb=tb, pb=pb: e.activation(out=G[:, tb * 512:(tb + 1) * 512], in_=pb,
                                                                  func=AF.Sqrt, bias=eps_rms[:, 0:1],
                                                                  scale=1.0 / 128.0),
                      reads=[bb, b_const], writes=[bG])
            kb.op("dve", lambda e: e.reciprocal(G, G), reads=[bG], writes=[bG])
            kb.op("dve", lambda e: e.tensor_tensor(B, B, G, ALU.mult), reads=[bB, bG], writes=[bB])
            kb.op("dve", lambda e, h=h: e.scalar_tensor_tensor(yT_a[:, h, :], B, normw[:, 0:1], A, ALU.mult,
                                                               ALU.mult), reads=[bB, bA, b_sm], writes=[b_yT_a])

    def fox(l, yT_b, b_yT_b):
        ar.reset("R")
        R = lambda shape, dt, parts=128: ar.alloc("R", shape, dt, parts)
        qT = R([L], BF16)
        kT = R([L], BF16)
        vaug = R([NTT, 2, 65], BF16)
        bqT, bkT, bvaug = Buf("qT"), Buf("kT"), Buf("vaug")
        cum = R([L], F32, 8)
        cum2 = R([L], F32, 8)
        bcum = Buf("cum")
        bfb = R([1], F32, 8)
        ncum = R([NTT, 8], F32)
        bncum = Buf("ncum")
        cumbc = [R([L], F32), R([L], F32)]
        bcumbc = [Buf("cumbc0"), Buf("cumbc1")]
        tmp = [R([512], F32), R([512], F32)]
        btmp = [Buf("tmp0"), Buf("tmp1")]
        pT = [R([512], BF16), R([512], BF16), R([512], BF16)]
        bpT = [Buf("pT0"), Buf("pT1"), Buf("pT2")]
        yb = R([NTT, 512], BF16)
        byb = Buf("yb")
        rl = R([NTT], F32)
        brl = Buf("rl")
        ones8 = R([L], F32, 8)

        def req_f(dst, bdst):
            kb.dma("pool", dst[:, :, 0:8], w_in_cols(l, C_FF, 8), writes=[bdst])
        ws_request(req_f)
        for hp in range(4):
            def req(dst, bdst, hp=hp):
                for j, c0 in enumerate((C_FQ, C_FK, C_FV)):
                    kb.dma("pool", dst[:, :, j * 128:(j + 1) * 128], w_in_cols(l, c0 + hp * 128, 128), writes=[bdst])
            ws_request(req)
        slab, bslab = ws_get()
        kb.dma("sp", bfb, dr["fox_b_f"][l].rearrange("(p o) -> p o", o=1), writes=[bcum])
        kb.op("dve", lambda e: e.memset(ones8, 1.0), writes=[bcum])
        for tb in range(4):
            pb, bb = proj_fm(slab, bslab, 0, 8, tb)
            kb.op("act", lambda e, tb=tb, pb=pb: e.activation(out=cum[:, tb * 512:(tb + 1) * 512], in_=pb[0:8, :],
                                                              func=AF.Sigmoid, bias=bfb[:, 0:1]),
                  reads=[bb, bcum], writes=[bcum])
        kb.op("act", lambda e: e.activation(out=cum, in_=cum, func=AF.Ln), reads=[bcum], writes=[bcum])
        kb.op("dve", lambda e: e.tensor_tensor_scan(cum2, ones8, cum, 0.0, ALU.mult, ALU.add),
              reads=[bcum], writes=[bcum])
        kb.dma("sp", cumscr, cum2, reads=[bcum], writes=[b_cumscr])
        if dbg and l == 0:
            kb.dma("sp", dbg_tensor("dbg_cum", [8, L]), cum2, reads=[bcum], writes=[Buf("dbgcum")])
        pb, bb = bank()
        for S in range(NTT):
            kb.op("pe", lambda e, S=S, pb=pb: e.transpose(pb[:, S * 8:(S + 1) * 8], cum2[:, S * 128:(S + 1) * 128],
                                                          ident_f[0:8, 0:8]), reads=[bcum, b_const], writes=[bb])
        kb.op("dve", lambda e, pb=pb: e.tensor_scalar(ncum, pb[:, 0:128].rearrange("p (a b) -> p a b", a=NTT), -1.0,
                                                      None, ALU.mult), reads=[bb], writes=[bncum])
        kb.op("pool", lambda e: e.memset(vaug[:, :, :, 64:65], 1.0), writes=[bvaug])

        rot["list"] = [0, 1, 2, 3, 4]
        rot["i"] = 0

        def oacc(T):
            bi = 5 + T // 6
            j = T % 6
            return banks[bi][:, j * 65:(j + 1) * 65], bankbufs[bi]

        gi = 0
        for hp in range(4):
            slab, bslab = ws_get()
            for tb in range(4):
                pb, bb = proj_fm(slab, bslab, 0, 128, tb)
                kb.op("act", lambda e, tb=tb, pb=pb: e.activation(out=qT[:, tb * 512:(tb + 1) * 512], in_=pb,
                                                                  func=AF.Copy, scale=0.125), reads=[bb],
                      writes=[bqT])
                pb, bb = proj_fm(slab, bslab, 128, 128, tb)
                kb.op("act", lambda e, tb=tb, pb=pb: e.copy(kT[:, tb * 512:(tb + 1) * 512], pb), reads=[bb],
                      writes=[bkT])
            for t4 in range(4):
                pb, bb = bank()
                for tq in range(4):
                    tt = t4 * 4 + tq
                    for kt in range(KT):
                        kb.op("pe", lambda e, kt=kt, tt=tt, tq=tq, pb=pb: e.matmul(
                            pb[:, tq * 128:(tq + 1) * 128], hT[:, kt, tt * 128:(tt + 1) * 128],
                            slab[:, kt, 256:384], start=(kt == 0), stop=(kt == KT - 1)),
                            reads=[bslab, b_hT[kt][tt // 4]], writes=[bb])
                kb.op("act", lambda e, t4=t4, pb=pb: e.copy(
                    vaug[:, t4 * 4:(t4 + 1) * 4, :, 0:64],
                    pb[:].rearrange("p (a h d) -> p a h d", a=4, h=2)), reads=[bb], writes=[bvaug])
            for hh in range(2):
                h = hp * 2 + hh
                hb = hh * 64
                cb, bcb = cumbc[h % 2], bcumbc[h % 2]
                kb.dma("sp", cb, cumscr[h].partition_broadcast(128), reads=[b_cumscr], writes=[bcb])
                for S in range(NTT):
                    t0 = S * 128
                    first = True
                    while t0 < L:
                        n = min(512, L - t0)
                        pb, bb = bank()
                        kb.op("pe", lambda e, S=S, t0=t0, n=n, pb=pb: e.matmul(
                            pb[:, 0:n], kT[hb:hb + 64, S * 128:(S + 1) * 128], qT[hb:hb + 64, t0:t0 + n],
                            start=True, stop=True), reads=[bkT, bqT], writes=[bb])
                        tm, btm = tmp[gi % 2], btmp[gi % 2]
                        pt, bpt = pT[gi % 3], bpT[gi % 3]
                        gi += 1
                        kb.op("dve", lambda e, t0=t0, n=n, pb=pb, tm=tm, cb=cb, S=S, h=h: e.scalar_tensor_tensor(
                            tm[:, 0:n], cb[:, t0:t0 + n], ncum[:, S, h:h + 1], pb[:, 0:n], ALU.add, ALU.add),
                            reads=[bb, bcb, bncum], writes=[btm])
                        if first:
                            kb.op("dve", lambda e, tm=tm: e.tensor_tensor(tm[:, 0:128], tm[:, 0:128], negmask,
                                                                          ALU.add),
                                  reads=[btm, b_const], writes=[btm])
                            first = False
                        kb.op("act", lambda e, n=n, tm=tm, pt=pt: e.activation(
                            out=pt[:, 0:n], in_=tm[:, 0:n], func=AF.Exp), reads=[btm], writes=[bpt])
                        for j in range(n // 128):
                            T = t0 // 128 + j
                            oa, boa = oacc(T)
                            kb.op("pe", lambda e, S=S, T=T, j=j, pt=pt, oa=oa: e.matmul(
                                oa, pt[:, j * 128:(j + 1) * 128], vaug[:, S, hh, :],
                                start=(S == 0 and T % 6 == 0), stop=(S == T), skip_group_check=True),
                                reads=[bpt, bvaug], writes=[boa])
                        t0 += n
                for bi, (Ta, Tn) in enumerate(((0, 6), (6, 6), (12, 4))):
                    pbk, bbk = banks[5 + bi], bankbufs[5 + bi]
                    v3 = pbk[:, 0:Tn * 65].rearrange("p (a b) -> p a b", b=65)
                    kb.op("dve", lambda e, Ta=Ta, Tn=Tn, v3=v3: e.reciprocal(rl[:, Ta:Ta + Tn], v3[:, :, 64]),
                          reads=[bbk], writes=[brl])
                    kb.op("dve", lambda e, Ta=Ta, Tn=Tn, v3=v3, h=h: e.tensor_tensor(
                        yb[:, Ta:Ta + Tn, h * 64:(h + 1) * 64], v3[:, :, 0:64],
                        rl[:, Ta:Ta + Tn].unsqueeze(2).to_broadcast([128, Tn, 64]), ALU.mult),
                        reads=[bbk, brl], writes=[byb])
        rot["list"] = list(range(8))
        for tt in range(NTT):
            pb, bb = bank()
            pv = pb[:].bitcast(BF16)
            for ct in range(4):
                kb.op("pe", lambda e, ct=ct, tt=tt, pv=pv: e.transpose(pv[:, ct * 128:(ct + 1) * 128],
                                                                       yb[:, tt, ct * 128:(ct + 1) * 128], ident_bf),
                      reads=[byb, b_const], writes=[bb])
            kb.op("act", lambda e, tt=tt, pv=pv: e.copy(yT_b[:, :, tt * 128:(tt + 1) * 128],
                                                        pv[:, 0:512].rearrange("p (c t) -> p c t", c=4)),
                  reads=[bb], writes=[b_yT_b])

    def s5(l, yT_c, b_yT_c):
        ar.reset("R")
        R = lambda shape, dt, parts=128: ar.alloc("R", shape, dt, parts)
        V = kb.op
        lo, hi = slice(0, 64), slice(64, 128)
        lre = R([8], F32)
        lim = R([8], F32)
        dts = R([8], F32)
        sm = R([8, 8], F32)
        bsm = Buf("s5small")
        Lre = R([8, 33], F32)
        Lim = R([8, 33], F32)
        LA = R([8, 33], F32)
        LB = R([8, 33], F32)
        LA2 = R([8, 33], F32)
        LB2 = R([8, 33], F32)
        ang = R([8, 33], F32)
        mag = R([8, 33], F32)
        angi = R([8, 33], I32)
        angf = R([8, 33], F32)
        bL = Buf("Ltab")
        Bre = R([8, 16], F32)
        Bim = R([8, 16], F32)
        Bbr = R([8, 16], F32)
        Bbi = R([8, 16], F32)
        bB_ = Buf("Btab")
        Cnat = R([2, 2, 64], F32)
        CA = R([8, 16], F32)
        CB = R([8, 16], F32)
        Cst = R([8, 16], BF16)
        bC = Buf("Ctab")
        dcol = R([1], F32)
        t1 = R([4, 128], F32)
        t2 = R([4, 128], F32)
        bt = Buf("s5t")
        BLd = R([33, 128], BF16)
        bBLd = Buf("BLd")
        pad = R([4, 8, 128], BF16)
        bpad = Buf("pad")
        Ktab = R([32, 128], BF16)
        bK = Buf("Ktab")
        uc = R([L], BF16)
        buc = Buf("uc")
        Abf = R([8, 96], BF16)
        bAbf = Buf("Abf")
        g2 = R([512], F32)
        bg = Buf("gelu")
        sig = R([4, 512], F32)
        bsig = Buf("sig")
        mark = ar.ptr["R"]
        Wd = R([32, 128], BF16)
        Wds = R([32, 128], BF16)
        um = R([L], BF16)
        bWd = bum = bX = Buf("s5alias")
        ar.ptr["R"] = mark
        XA = [R([8, 96], F32), R([8, 96], F32)]
        XB = [R([8, 96], F32), R([8, 96], F32)]
        xt1 = R([8, 64], F32)
        xt2 = R([8, 64], F32)
        cpow = R([4, 8], F32)
        Gd = BLd
        bGd = bBLd

        def req_u(dst, bdst):
            kb.dma("pool", dst[:, :, 0:512], w_in_cols(l, C_SU, 512), writes=[bdst])
        ws_request(req_u)

        def req_glu(dst, bdst):
            kb.dma("pool", dst[:, 0:4, 0:512], dr["s5_w_glu"][l].rearrange("(kt p) e -> p kt e", p=128),
                   writes=[bdst])
        ws_request(req_glu)
        slab, bslab = ws_get()

        def padded(Tpad):
            return bass.AP(Tpad.tensor, Tpad.offset, [list(Tpad.ap[0]), [8 * 128, 4], [144, 8], [1, 16]])

        for ct in range(4):
            g0 = ct * 8
            for tb in range(4):
                pb, bb = proj_fm(slab, bslab, ct * 128, 128, tb)
                V("act", lambda e, tb=tb, pb=pb: e.copy(uc[:, tb * 512:(tb + 1) * 512], pb), reads=[bb],
                  writes=[buc])
            for half in range(2):
                hs = slice(half * 64, half * 64 + 64)
                kb.dma("sp", lre[hs, :], dr["s5_lambda_re"][l, g0:g0 + 8, :].rearrange("g p -> p g"),
                       writes=[bsm], slow=True)
                kb.dma("sp", lim[hs, :], dr["s5_lambda_im"][l, g0:g0 + 8, :].rearrange("g p -> p g"),
                       writes=[bsm], slow=True)
                kb.dma("sp", Bre[hs, :, :], dr["s5_b_re"][l, g0:g0 + 8].rearrange("g p h -> p g h"),
                       writes=[bB_])
                kb.dma("sp", Bim[hs, :, :], dr["s5_b_im"][l, g0:g0 + 8].rearrange("g p h -> p g h"),
                       writes=[bB_])
            kb.dma("sp", dts, dr["s5_log_step"][l, g0:g0 + 8].partition_broadcast(128), writes=[bsm])
            for ri, nm in enumerate(("s5_c_re", "s5_c_im")):
                for dup in range(2):
                    kb.dma("sp", Cnat[:, ri, dup, :],
                           dr[nm][l, g0:g0 + 8].rearrange("g h p -> (g h) p"), writes=[bC])
            kb.dma("sp", dcol, dr["s5_d"][l, g0:g0 + 8].rearrange("g (h o) -> (g h) o", o=1), writes=[bC])
            S = lambda i: sm[:, i, :]
            V("act", lambda e: e.activation(out=dts, in_=dts, func=AF.Exp), reads=[bsm], writes=[bsm])
            V("dve", lambda e: e.tensor_tensor(S(0), lre, dts, ALU.mult), reads=[bsm], writes=[bsm])
            V("dve", lambda e: e.tensor_tensor(S(1), lim, dts, ALU.mult), reads=[bsm], writes=[bsm])
            kkb = kk.unsqueeze(1).to_broadcast([128, 8, 33])
            V("dve", lambda e: e.tensor_tensor(mag, S(0).unsqueeze(2).to_broadcast([128, 8, 33]), kkb, ALU.mult),
              reads=[bsm, b_const], writes=[bL])
            V("act", lambda e: e.activation(out=mag, in_=mag, func=AF.Exp), reads=[bL], writes=[bL])
            V("dve", lambda e: e.tensor_tensor(ang, S(1).unsqueeze(2).to_broadcast([128, 8, 33]), kkb, ALU.mult),
              reads=[bsm, b_const], writes=[bL])
            def sin_of(dst, shift):
                C1 = 6.28125
                C2 = TWO_PI - C1
                V("dve", lambda e: e.tensor_scalar(dst, ang, shift, 1.0 / TWO_PI, ALU.add, ALU.mult),
                  reads=[bL], writes=[bL])
                V("dve", lambda e: e.tensor_copy(angi, dst), reads=[bL], writes=[bL])
                V("dve", lambda e: e.tensor_copy(angf, angi), reads=[bL], writes=[bL])
                V("dve", lambda e: e.tensor_scalar(dst, ang, shift, None, ALU.add), reads=[bL], writes=[bL])
                V("dve", lambda e: e.scalar_tensor_tensor(dst, angf, -C1, dst, ALU.mult, ALU.add), reads=[bL],
                  writes=[bL])
                V("dve", lambda e: e.scalar_tensor_tensor(dst, angf, -C2, dst, ALU.mult, ALU.add), reads=[bL],
                  writes=[bL])
                V("dve", lambda e: e.tensor_scalar(angf, dst, math.pi, -TWO_PI, ALU.is_gt, ALU.mult), reads=[bL],
                  writes=[bL])
                V("dve", lambda e: e.tensor_tensor(dst, dst, angf, ALU.add), reads=[bL], writes=[bL])
                V("dve", lambda e: e.tensor_scalar(angf, dst, -math.pi, TWO_PI, ALU.is_lt, ALU.mult), reads=[bL],
                  writes=[bL])
                V("dve", lambda e: e.tensor_tensor(dst, dst, angf, ALU.add), reads=[bL], writes=[bL])
                V("act", lambda e: e.activation(out=dst, in_=dst, func=AF.Sin), reads=[bL], writes=[bL])
            sin_of(Lim, 0.0)
            sin_of(Lre, 0.5 * math.pi)
            V("dve", lambda e: e.tensor_tensor(Lre, Lre, mag, ALU.mult), reads=[bL], writes=[bL])
            V("dve", lambda e: e.tensor_tensor(Lim, Lim, mag, ALU.mult), reads=[bL], writes=[bL])
            V("dve", lambda e: e.tensor_copy(LA[lo], Lre[lo]), reads=[bL], writes=[bL])
            V("dve", lambda e: e.tensor_copy(LA[hi], Lim[hi]), reads=[bL], writes=[bL])
            V("dve", lambda e: e.tensor_scalar(LB[lo], Lim[lo], -1.0, None, ALU.mult), reads=[bL], writes=[bL])
            V("dve", lambda e: e.tensor_copy(LB[hi], Lre[hi]), reads=[bL], writes=[bL])
            V("dve", lambda e: e.tensor_copy(LA2[lo], Lre[lo]), reads=[bL], writes=[bL])
            V("dve", lambda e: e.tensor_scalar(LA2[hi], Lim[hi], -1.0, None, ALU.mult), reads=[bL], writes=[bL])
            V("dve", lambda e: e.tensor_scalar(LB2[lo], Lim[lo], -1.0, None, ALU.mult), reads=[bL], writes=[bL])
            V("dve", lambda e: e.tensor_scalar(LB2[hi], Lre[hi], -1.0, None, ALU.mult), reads=[bL], writes=[bL])
            V("dve", lambda e: e.tensor_scalar(S(2), Lre[:, :, 1], -1.0, None, ALU.add), reads=[bL, bsm],
              writes=[bsm])
            V("dve", lambda e: e.tensor_tensor(S(3), lre, lre, ALU.mult), reads=[bsm], writes=[bsm])
            V("dve", lambda e: e.tensor_tensor(S(4), lim, lim, ALU.mult), reads=[bsm], writes=[bsm])
            V("dve", lambda e: e.tensor_tensor(S(3), S(3), S(4), ALU.add), reads=[bsm], writes=[bsm])
            V("dve", lambda e: e.reciprocal(S(3), S(3)), reads=[bsm], writes=[bsm])
            V("dve", lambda e: e.tensor_tensor(S(4), S(2), lre, ALU.mult), reads=[bsm], writes=[bsm])
            V("dve", lambda e: e.tensor_tensor(S(5), Lim[:, :, 1], lim, ALU.mult), reads=[bsm, bL], writes=[bsm])
            V("dve", lambda e: e.tensor_tensor(S(4), S(4), S(5), ALU.add), reads=[bsm], writes=[bsm])
            V("dve", lambda e: e.tensor_tensor(S(6), S(4), S(3), ALU.mult), reads=[bsm], writes=[bsm])
            V("dve", lambda e: e.tensor_tensor(S(4), Lim[:, :, 1], lre, ALU.mult), reads=[bsm, bL], writes=[bsm])
            V("dve", lambda e: e.tensor_tensor(S(5), S(2), lim, ALU.mult), reads=[bsm], writes=[bsm])
            V("dve", lambda e: e.tensor_tensor(S(4), S(4), S(5), ALU.subtract), reads=[bsm], writes=[bsm])
            V("dve", lambda e: e.tensor_tensor(S(7), S(4), S(3), ALU.mult), reads=[bsm], writes=[bsm])
            crb = S(6).unsqueeze(2).to_broadcast([128, 8, 16])
            cib = S(7).unsqueeze(2).to_broadcast([128, 8, 16])
            V("dve", lambda e: e.tensor_tensor(Bbr, Bre, crb, ALU.mult), reads=[bsm, bB_], writes=[bB_])
            V("dve", lambda e: e.tensor_tensor(Bbi, Bim, cib, ALU.mult), reads=[bsm, bB_], writes=[bB_])
            V("dve", lambda e: e.tensor_tensor(Bbr, Bbr, Bbi, ALU.subtract), reads=[bB_], writes=[bB_])
            V("dve", lambda e: e.tensor_tensor(Bbi, Bim, crb, ALU.mult), reads=[bsm, bB_], writes=[bB_])
            V("dve", lambda e: e.tensor_tensor(Bre, Bre, cib, ALU.mult), reads=[bsm, bB_], writes=[bB_])
            V("dve", lambda e: e.tensor_tensor(Bbi, Bbi, Bre, ALU.add), reads=[bB_], writes=[bB_])
            pb, bb = bank()
            for ri in range(2):
                V("pe", lambda e, ri=ri, pb=pb: e.transpose(
                    pb[:, ri * 128:(ri + 1) * 128], Cnat[:, ri, :, :].rearrange("p a b -> p (a b)"), ident_f),
                    reads=[bC, b_const], writes=[bb])
            V("dve", lambda e, pb=pb: e.tensor_copy(CA, pb[:, 0:128].rearrange("p (g h) -> p g h", g=8)),
              reads=[bb], writes=[bC])
            V("dve", lambda e, pb=pb: e.tensor_copy(CB, pb[:, 128:256].rearrange("p (g h) -> p g h", g=8)),
              reads=[bb], writes=[bC])
            V("dve", lambda e: e.tensor_copy(Cst[lo], CA[lo]), reads=[bC], writes=[bC])
            V("dve", lambda e: e.tensor_scalar(Cst[hi], CB[hi], -1.0, None, ALU.mult), reads=[bC], writes=[bC])

            def dense_tab(dst, bdst, Ta, Tb, La, Lb_, bsrc):
                for k0 in range(0, 33, 4):
                    kn = min(4, 33 - k0)

                    def bc_tab(T):
                        return T.unsqueeze(1).to_broadcast([128, kn, 8, 16])

                    def bc_pow(Lt):
                        return Lt[:, :, k0:k0 + kn].rearrange("p g k -> p k g").unsqueeze(3).to_broadcast(
                            [128, kn, 8, 16])
                    t1v = t1[:, 0:kn, :].rearrange("p k (g h) -> p k g h", g=8)
                    t2v = t2[:, 0:kn, :].rearrange("p k (g h) -> p k g h", g=8)
                    V("dve", lambda e: e.tensor_tensor(t1v, bc_tab(Ta), bc_pow(La), ALU.mult),
                      reads=[bsrc, bL, bt], writes=[bt])
                    V("dve", lambda e: e.tensor_tensor(t2v, bc_tab(Tb), bc_pow(Lb_), ALU.mult),
                      reads=[bsrc, bL, bt], writes=[bt])
                    V("dve", lambda e: e.tensor_tensor(dst[:, k0:k0 + kn, :], t1[:, 0:kn, :], t2[:, 0:kn, :],
                                                       ALU.add), reads=[bt], writes=[bdst])

            dense_tab(BLd, bBLd, Bbr, Bbi, LA, LB, bB_)
            V("pool", lambda e: e.memset(pad, 0.0), writes=[bpad])
            for d4 in range(8):
                V("dve", lambda e, d4=d4: e.tensor_copy(
                    padded(pad), BLd[:, d4 * 4:(d4 + 1) * 4, :].rearrange("p k (g h) -> p k g h", g=8)),
                    reads=[bBLd], writes=[bpad])
                pb, bb = bank()
                for dd in range(4):
                    for gl in range(8):
                        V("pe", lambda e, dd=dd, gl=gl, pb=pb: e.matmul(
                            pb[:, dd * 128 + gl * 16: dd * 128 + (gl + 1) * 16], pad[:, dd, gl, :],
                            Cst[:, gl, :], start=True, stop=True), reads=[bpad, bC], writes=[bb])
                if d4 == 0:
                    V("dve", lambda e, pb=pb: e.scalar_tensor_tensor(Ktab[:, 0, :], ident_f, dcol[:, 0:1],
                                                                     pb[:, 0:128], ALU.mult, ALU.add),
                      reads=[bb, bC, b_const], writes=[bK])
                    V("act", lambda e, pb=pb: e.copy(Ktab[:, 1:4, :],
                                                     pb[:, 128:512].rearrange("p (a b) -> p a b", a=3)),
                      reads=[bb], writes=[bK])
                else:
                    V("act", lambda e, pb=pb, d4=d4: e.copy(Ktab[:, d4 * 4:d4 * 4 + 4, :],
                                                            pb[:].rearrange("p (a b) -> p a b", a=4)),
                      reads=[bb], writes=[bK])
            for s8 in range(4):
                pb, bb = bank()
                pv = pb[:].bitcast(BF16)
                for ss in range(8):
                    sg = s8 * 8 + ss
                    V("pe", lambda e, sg=sg, ss=ss, pv=pv: e.transpose(pv[:, ss * 128:(ss + 1) * 128],
                                                                       BLd[:, 31 - sg, :], ident_bf),
                      reads=[bBLd, b_const], writes=[bb])
                pv3 = pv.rearrange("p (a b) -> p a b", a=8)
                V("act", lambda e, s8=s8, pv3=pv3: e.copy(Wd[:, s8 * 8:(s8 + 1) * 8, :], pv3), reads=[bb],
                  writes=[bWd])
                V("dve", lambda e, s8=s8, pv3=pv3: e.tensor_copy(Wds[:, s8 * 8:(s8 + 1) * 8, 0:64],
                                                                 pv3[:, :, 64:128]), reads=[bb], writes=[bWd])
                V("dve", lambda e, s8=s8, pv3=pv3: e.tensor_copy(Wds[:, s8 * 8:(s8 + 1) * 8, 64:128],
                                                                 pv3[:, :, 0:64]), reads=[bb], writes=[bWd])
            pbA, bbA = bank()
            pbB, bbB = bank()
            u3 = um.rearrange("p (c s) -> p c s", s=32)
            for gl in range(8):
                V("dve", lambda e, gl=gl: e.tensor_scalar(um, uc, rowmask[:, gl:gl + 1], None, ALU.mult),
                  reads=[buc, b_const], writes=[bum])
                for sg in range(32):
                    V("pe", lambda e, gl=gl, sg=sg: e.matmul(pbA[:, gl * 64:(gl + 1) * 64], Wd[:, sg, :],
                                                             u3[:, :, sg], start=(sg == 0), stop=(sg == 31)),
                      reads=[bWd, bum], writes=[bbA])
                for sg in range(32):
                    V("pe", lambda e, gl=gl, sg=sg: e.matmul(pbB[:, gl * 64:(gl + 1) * 64], Wds[:, sg, :],
                                                             u3[:, :, sg], start=(sg == 0), stop=(sg == 31)),
                      reads=[bWd, bum], writes=[bbB])
            V("pool", lambda e: e.memset(XA[0][:, :, 0:32], 0.0), writes=[bX])
            V("pool", lambda e: e.memset(XA[1][:, :, 0:32], 0.0), reads=[bX], writes=[bX])
            V("pool", lambda e: e.memset(XB[0][:, :, 0:32], 0.0), reads=[bX], writes=[bX])
            V("pool", lambda e: e.memset(XB[1][:, :, 0:32], 0.0), reads=[bX], writes=[bX])
            V("dve", lambda e: e.tensor_copy(XA[0][:, :, 32:96], pbA[:].rearrange("p (g c) -> p g c", g=8)),
              reads=[bbA, bX], writes=[bX])
            V("dve", lambda e: e.tensor_copy(XB[0][:, :, 32:96], pbB[:].rearrange("p (g c) -> p g c", g=8)),
              reads=[bbB, bX], writes=[bX])
            V("dve", lambda e: e.tensor_copy(cpow[:, 0, :], Lre[:, :, 32]), reads=[bL, bX], writes=[bX])
            V("dve", lambda e: e.tensor_scalar(cpow[lo, 1, :], Lim[lo, :, 32], -1.0, None, ALU.mult),
              reads=[bL, bX], writes=[bX])
            V("dve", lambda e: e.tensor_copy(cpow[hi, 1, :], Lim[hi, :, 32]), reads=[bL, bX], writes=[bX])
            cur = 0
            for m in range(6):
                s_ = 1 << m
                A0, B0, A1, B1 = XA[cur], XB[cur], XA[1 - cur], XB[1 - cur]
                cab = cpow[:, 0, :].unsqueeze(2).to_broadcast([128, 8, 64])
                cbb = cpow[:, 1, :].unsqueeze(2).to_broadcast([128, 8, 64])
                As, Bs = A0[:, :, 32 - s_:96 - s_], B0[:, :, 32 - s_:96 - s_]
                V("dve", lambda e, As=As, cab=cab: e.tensor_tensor(xt1, As, cab, ALU.mult), reads=[bX], writes=[bX])
                V("dve", lambda e, Bs=Bs, cbb=cbb: e.tensor_tensor(xt2, Bs, cbb, ALU.mult), reads=[bX], writes=[bX])
                V("dve", lambda e: e.tensor_tensor(xt1, xt1, xt2, ALU.add), reads=[bX], writes=[bX])
                V("dve", lambda e, A0=A0, A1=A1: e.tensor_tensor(A1[:, :, 32:96], A0[:, :, 32:96], xt1, ALU.add),
                  reads=[bX], writes=[bX])
                V("dve", lambda e, Bs=Bs, cab=cab: e.tensor_tensor(xt1, Bs, cab, ALU.mult), reads=[bX], writes=[bX])
                V("dve", lambda e, As=As, cbb=cbb: e.tensor_tensor(xt2, As, cbb, ALU.mult), reads=[bX], writes=[bX])
                V("dve", lambda e: e.tensor_tensor(xt1, xt1, xt2, ALU.subtract), reads=[bX], writes=[bX])
                V("dve", lambda e, B0=B0, B1=B1: e.tensor_tensor(B1[:, :, 32:96], B0[:, :, 32:96], xt1, ALU.add),
                  reads=[bX], writes=[bX])
                cur = 1 - cur
                if m < 5:
                    V("dve", lambda e: e.tensor_tensor(cpow[:, 2, :], cpow[:, 0, :], cpow[:, 0, :], ALU.mult),
                      reads=[bX], writes=[bX])
                    V("dve", lambda e: e.tensor_tensor(cpow[:, 3, :], cpow[:, 1, :], cpow[:, 1, :], ALU.mult),
                      reads=[bX], writes=[bX])
                    V("dve", lambda e: e.scalar_tensor_tensor(cpow[:, 1, :], cpow[:, 0, :], 2.0, cpow[:, 1, :],
                                                              ALU.mult, ALU.mult), reads=[bX], writes=[bX])
                    V("dve", lambda e: e.tensor_tensor(cpow[:, 0, :], cpow[:, 2, :], cpow[:, 3, :], ALU.subtract),
                      reads=[bX], writes=[bX])
            V("act", lambda e, cur=cur: e.copy(Abf, XA[cur]), reads=[bX], writes=[bAbf])
            dense_tab(Gd, bGd, CA, CB, LA2, LB2, bC)
            rot["list"] = [0, 1, 2, 3]
            rot["i"] = 0
            accb = [4, 5, 6, 7]
            u3c = uc.rearrange("p (c s) -> p s c", s=32)
            for bi in range(4):
                pb, bb = banks[accb[bi]], bankbufs[accb[bi]]
                pb3 = pb[:].rearrange("p (s c) -> p s c", c=64)
                for d in range(0, bi * 8 + 8):
                    tl0 = max(bi * 8, d)
                    nt = bi * 8 + 8 - tl0
                    V("pe", lambda e, d=d, tl0=tl0, nt=nt, pb3=pb3, bi=bi: e.matmul(
                        pb3[:, tl0 - bi * 8:tl0 - bi * 8 + nt, :], Ktab[:, d, :], u3c[:, tl0 - d:tl0 - d + nt, :],
                        start=(d == 0), stop=False), reads=[bK, buc], writes=[bb])
            for k4 in range(8):
                V("dve", lambda e, k4=k4: e.tensor_copy(
                    padded(pad), Gd[:, 1 + k4 * 4:1 + (k4 + 1) * 4, :].rearrange("p k (g h) -> p k g h", g=8)),
                    reads=[bGd], writes=[bpad])
                pb, bb = banks[accb[k4 // 2]], bankbufs[accb[k4 // 2]]
                for kl in range(4):
                    col = ((k4 % 2) * 4 + kl) * 64
                    for gl in range(8):
                        V("pe", lambda e, kl=kl, gl=gl, pb=pb, col=col: e.matmul(
                            pb[:, col:col + 64], pad[:, kl, gl, :], Abf[:, gl, 31:95],
                            start=False, stop=(gl == 7)), reads=[bpad, bAbf], writes=[bb])
            yc3 = yT_c[:, ct, :].rearrange("p (c s) -> p s c", s=32)
            for bi in range(4):
                pb, bb = banks[accb[bi]], bankbufs[accb[bi]]
                V("act", lambda e, pb=pb: e.activation(out=g2, in_=pb, func=AF.Square), reads=[bb], writes=[bg])
                V("dve", lambda e: e.tensor_scalar(g2, g2, 0.044715, 1.0, ALU.mult, ALU.add), reads=[bg],
                  writes=[bg])
                V("dve", lambda e, pb=pb: e.tensor_tensor(g2, g2, pb, ALU.mult), reads=[bg, bb], writes=[bg])
                V("act", lambda e: e.activation(out=g2, in_=g2, func=AF.Sigmoid,
                                                scale=2.0 * math.sqrt(2.0 / math.pi)), reads=[bg], writes=[bg])
                V("dve", lambda e, bi=bi, pb=pb: e.tensor_tensor(
                    yc3[:, bi * 8:(bi + 1) * 8, :], g2.rearrange("p (s c) -> p s c", c=64),
                    pb[:].rearrange("p (s c) -> p s c", c=64), ALU.mult), reads=[bg, bb], writes=[b_yT_c])
            rot["list"] = list(range(8))
            rot["i"] = 0
        slab, bslab = ws_get()
        for tb in range(4):
            for ct in range(4):
                pb, bb = proj_fm(slab, bslab, ct * 128, 128, tb, kts=4, rhs_src=yT_c, rhs_bufs=[b_yT_c])
                V("act", lambda e, ct=ct, pb=pb: e.activation(out=sig[:, ct, :], in_=pb, func=AF.Sigmoid),
                  reads=[bb], writes=[bsig])
            V("dve", lambda e, tb=tb: e.tensor_tensor(yT_c[:, :, tb * 512:(tb + 1) * 512],
                                                      yT_c[:, :, tb * 512:(tb + 1) * 512], sig, ALU.mult),
              reads=[bsig, b_yT_c], writes=[b_yT_c])

    def merge(l, yT, b_yT, mergedT, bmerged):
        sg = [ar.alloc("R", [512], F32) for _ in range(3)]
        bsg = [Buf("sg%d" % i) for i in range(3)]
        acc = ar.alloc("R", [512], F32)
        bacc = Buf("macc")
        brn = ("w_br_a", "w_br_b", "w_br_c")
        for et in range(KT):
            def req_g(dst, bdst, et=et):
                for i in range(3):
                    kb.dma("pool", dst[:, :, i * 128:(i + 1) * 128],
                           w_in_cols(l, C_GT + i * 1024 + et * 128, 128), writes=[bdst])
            ws_request(req_g)

            def req_b(dst, bdst, et=et):
                for i in range(3):
                    kb.dma("pool", dst[:, 0:4, i * 128:(i + 1) * 128],
                           dr[brn[i]][l, :, et * 128:(et + 1) * 128].rearrange("(kt p) e -> p kt e", p=128),
                           writes=[bdst])
            ws_request(req_b)
        for et in range(KT):
            (sg_slab, bsg_slab), (sb_slab, bsb_slab) = ws_get(2)
            for tb in range(4):
                for i in range(3):
                    pb, bb = proj_fm(sg_slab, bsg_slab, i * 128, 128, tb)
                    kb.op("act", lambda e, i=i, pb=pb: e.activation(out=sg[i], in_=pb, func=AF.Sigmoid),
                          reads=[bb], writes=[bsg[i]])
                for i in range(3):
                    pb, bb = proj_fm(sb_slab, bsb_slab, i * 128, 128, tb, kts=4, rhs_src=yT[i], rhs_bufs=[b_yT[i]])
                    if i == 0:
                        kb.op("dve", lambda e, pb=pb: e.tensor_tensor(acc, sg[0], pb, ALU.mult),
                              reads=[bb, bsg[0]], writes=[bacc])
                    else:
                        kb.op("dve", lambda e, pb=pb, i=i: e.tensor_tensor(sg[i], sg[i], pb, ALU.mult),
                              reads=[bb, bsg[i]], writes=[bsg[i]])
                        if i == 1:
                            kb.op("dve", lambda e: e.tensor_tensor(acc, acc, sg[1], ALU.add),
                                  reads=[bacc, bsg[1]], writes=[bacc])
                        else:
                            kb.op("dve", lambda e, et=et, tb=tb: e.tensor_tensor(
                                mergedT[:, et, tb * 512:(tb + 1) * 512], acc, sg[2], ALU.add),
                                reads=[bacc, bsg[2]], writes=[bmerged])

    def tail(l, mergedT, bmerged, res_src, b_res, dst, b_dst, last):
        ar.reset("Y")
        Yr = lambda shape, dt: ar.alloc("Y", shape, dt)
        Rr = lambda shape, dt: ar.alloc("R", shape, dt)
        h1 = Yr([4, D], F32)
        bh1 = [Buf("h1_%d" % i) for i in range(4)]
        h1T = Yr([KT, 512], BF16)
        bh1T = Buf("h1T")
        aT = Yr([NFT, 512], BF16)
        baT = Buf("aT")
        xs = [Rr([D], F32), Rr([D], F32)]
        bxs = [Buf("txs0"), Buf("txs1")]
        xb = [Rr([D], BF16), Rr([D], BF16)]
        bxb = [Buf("txb0"), Buf("txb1")]
        rs = [Rr([D], F32), Rr([D], F32)]
        brs = [Buf("rs0"), Buf("rs1")]
        stat = [Rr([16], F32), Rr([16], F32)]
        bstat = [Buf("st0"), Buf("st1")]
        sgt = [Rr([512], F32), Rr([512], F32)]
        bsgt = [Buf("sgt0"), Buf("sgt1")]
        ho = [Rr([D], F32), Rr([D], F32)]
        bho = [Buf("ho0"), Buf("ho1")]

        def load_ln(gname, bname):
            kb.dma("sp", lng, dr[gname][l].partition_broadcast(128), writes=[b_lng])
            kb.dma("sp", lnb, dr[bname][l].partition_broadcast(128), writes=[b_lnb])

        for tb in range(4):
            for half in range(2):
                ws_request(lambda dst, bdst, half=half: kb.dma(
                    "pool", dst[:, :, 0:512],
                    dr["w_out"][l, :, half * 512:(half + 1) * 512].rearrange("(kt p) e -> p kt e", p=128),
                    writes=[bdst]))
            for f4 in range(6):
                nf = 4 if f4 < 5 else 2
                for nm in ("w_ffn_gate", "w_ffn_up"):
                    ws_request(lambda dst, bdst, f4=f4, nf=nf, nm=nm: kb.dma(
                        "pool", dst[:, :, 0:nf * 128],
                        dr[nm][l, :, f4 * 512:f4 * 512 + nf * 128].rearrange("(kt p) e -> p kt e", p=128),
                        writes=[bdst]))
            for half in range(2):
                for k3 in range(3):
                    nk = 8 if k3 < 2 else 6
                    ws_request(lambda dst, bdst, half=half, k3=k3, nk=nk: kb.dma(
                        "pool", dst[:, 0:nk, 0:512],
                        dr["w_ffn_down"][l, k3 * 1024:k3 * 1024 + nk * 128, half * 512:(half + 1) * 512].rearrange(
                            "(kt p) e -> p kt e", p=128), writes=[bdst]))
        ti = 0
        for tb in range(4):
            load_ln("ln1_g", "ln1_b")
            wsl = ws_get(2)
            for tq in range(4):
                tt = tb * 4 + tq
                i = ti % 2
                ti += 1
                kb.dma("sp", rs[i], res_src[tt * 128:(tt + 1) * 128, :], reads=[b_res], writes=[brs[i]])
                for half in range(2):
                    slab, bslab = wsl[half]
                    pb, bb = bank()
                    for kt in range(KT):
                        kb.op("pe", lambda e, kt=kt, tt=tt, pb=pb, slab=slab: e.matmul(
                            pb, mergedT[:, kt, tt * 128:(tt + 1) * 128], slab[:, kt, 0:512], start=(kt == 0),
                            stop=(kt == KT - 1)), reads=[bslab, bmerged], writes=[bb])
                    kb.op("dve", lambda e, half=half, pb=pb, i=i: e.scalar_tensor_tensor(
                        xs[i][:, half * 512:(half + 1) * 512], rs[i][:, half * 512:(half + 1) * 512], ALPHA, pb,
                        ALU.mult, ALU.add), reads=[bb, brs[i]], writes=[bxs[i]])
                layer_norm_tile(xs[i], bxs[i], lng, lnb, h1[:, tq, :], bh1[tq], stat[i], bstat[i])
                kb.op("act", lambda e, i=i, tq=tq: e.copy(xb[i], h1[:, tq, :]), reads=[bh1[tq]], writes=[bxb[i]])
                pb, bb = bank()
                pv = pb[:].bitcast(BF16)
                for kt in range(KT):
                    kb.op("pe", lambda e, kt=kt, pv=pv, i=i: e.transpose(pv[:, kt * 128:(kt + 1) * 128],
                                                                         xb[i][:, kt * 128:(kt + 1) * 128],
                                                                         ident_bf),
                          reads=[bxb[i], b_const], writes=[bb])
                kb.op("act", lambda e, tq=tq, pv=pv: e.copy(h1T[:, :, tq * 128:(tq + 1) * 128],
                                                            pv.rearrange("p (k t) -> p k t", k=KT)),
                      reads=[bb], writes=[bh1T])
            for f4 in range(6):
                nf = 4 if f4 < 5 else 2
                (sg_, bsg_), (su_, bsu_) = ws_get(2)
                for fl in range(nf):
                    ft = f4 * 4 + fl
                    i = ft % 2
                    pbg, bbg = bank()
                    for kt in range(KT):
                        kb.op("pe", lambda e, kt=kt, fl=fl, pbg=pbg, sg_=sg_: e.matmul(
                            pbg, sg_[:, kt, fl * 128:(fl + 1) * 128], h1T[:, kt, :], start=(kt == 0),
                            stop=(kt == KT - 1)), reads=[bsg_, bh1T], writes=[bbg])
                    pbu, bbu = bank()
                    for kt in range(KT):
                        kb.op("pe", lambda e, kt=kt, fl=fl, pbu=pbu, su_=su_: e.matmul(
                            pbu, su_[:, kt, fl * 128:(fl + 1) * 128], h1T[:, kt, :], start=(kt == 0),
                            stop=(kt == KT - 1)), reads=[bsu_, bh1T], writes=[bbu])
                    kb.op("act", lambda e, i=i, pbg=pbg: e.activation(out=sgt[i], in_=pbg, func=AF.Silu),
                          reads=[bbg], writes=[bsgt[i]])
                    kb.op("dve", lambda e, i=i, ft=ft, pbu=pbu: e.tensor_tensor(aT[:, ft, :], sgt[i], pbu, ALU.mult),
                          reads=[bbu, bsgt[i]], writes=[baT])
            load_ln("ln2_g", "ln2_b")
            for half in range(2):
                dsl = ws_get(3)
                for tq in range(4):
                    pb, bb = bank()
                    for k3 in range(3):
                        nk = 8 if k3 < 2 else 6
                        slab, bslab = dsl[k3]
                        for kk_ in range(nk):
                            ft = k3 * 8 + kk_
                            kb.op("pe", lambda e, ft=ft, kk_=kk_, tq=tq, pb=pb, slab=slab: e.matmul(
                                pb, aT[:, ft, tq * 128:(tq + 1) * 128], slab[:, kk_, 0:512], start=(ft == 0),
                                stop=(ft == NFT - 1)), reads=[bslab, baT], writes=[bb])
                    kb.op("dve", lambda e, half=half, pb=pb, tq=tq: e.scalar_tensor_tensor(
                        h1[:, tq, half * 512:(half + 1) * 512], h1[:, tq, half * 512:(half + 1) * 512], ALPHA, pb,
                        ALU.mult, ALU.add), reads=[bb, bh1[tq]], writes=[bh1[tq]])
            for tq in range(4):
                tt = tb * 4 + tq
                i = ti % 2
                ti += 1
                layer_norm_tile(h1[:, tq, :], bh1[tq], lng, lnb, ho[i], bho[i], stat[i], bstat[i])
                kb.dma("sp", dst[tt * 128:(tt + 1) * 128, :], ho[i], reads=[bho[i]], writes=[b_dst])
                if not last:
                    kb.op("act", lambda e, i=i: e.copy(xb[i], ho[i]), reads=[bho[i]], writes=[bxb[i]])
                    transpose_to_hT(xb[i], bxb[i], tt)

    setup_consts()
    load_input()
    yT = [ar.alloc("Y", [4, L], BF16) for _ in range(3)]
    b_yT = [Buf("yT%d" % i) for i in range(3)]
    for l in range(n_layers):
        kb.barrier()
        hgrn2(l, yT[0], b_yT[0])
        kb.barrier()
        fox(l, yT[1], b_yT[1])
        kb.barrier()
        s5(l, yT[2], b_yT[2])
        kb.barrier()
        if dbg and l == 0:
            for i, nm in enumerate(("dbg_ya", "dbg_yb", "dbg_yc")):
                dt_ = dbg_tensor(nm, [128, 4 * L])
                ar.reset("R")
                tmpf = ar.alloc("R", [4 * L], F32)
                bt_ = Buf("dbgt")
                kb.op("dve", lambda e, i=i: e.tensor_copy(tmpf, yT[i].rearrange("p a b -> p (a b)")),
                      reads=[b_yT[i]], writes=[bt_])
                kb.dma("sp", dt_, tmpf, reads=[bt_], writes=[Buf("dbgo")])
                kb.barrier()
        ar.reset("R")
        mergedT = ar.alloc("R", [KT, L], BF16)
        bmerged = Buf("merged")
        merge(l, yT, b_yT, mergedT, bmerged)
        kb.barrier()
        last = (l == n_layers - 1)
        res_src = dr["x"] if l == 0 else hscr
        b_res = Buf("xres") if l == 0 else b_hscr
        tail(l, mergedT, bmerged, res_src, b_res, out_d if last else hscr, b_out if last else b_hscr, last)
        ar.reset("Y")
        yT = [ar.alloc("Y", [4, L], BF16) for _ in range(3)]
    kb.barrier(engines=("sp",))
    return nc, kb, list(dbg_out.keys())


_NC_CACHE = {}


def kernel(**inputs):
    key = "main"
    if key not in _NC_CACHE:
        _NC_CACHE[key] = build(DEPTH, dbg=False)
    nc, kb, _ = _NC_CACHE[key]
    x = np.ascontiguousarray(np.asarray(inputs["x"], dtype=np.float32))
    shared = {}
    for name, shp in INPUT_SPECS:
        if name == "x":
            continue
        shared[name] = np.ascontiguousarray(np.asarray(inputs[name], dtype=np.float32))
    in_maps = []
    for b in range(8):
        m = dict(shared)
        m["x"] = x[b]
        in_maps.append(m)
    res = run_bass_kernel_spmd(nc, in_maps, core_ids=list(range(8)))
    out = np.stack([np.asarray(res.results[b]["out"], dtype=np.float32) for b in range(8)], axis=0)
    return out
```

```python
import math
import numpy as np
import concourse.bass as bass
import concourse.mybir as mybir
from concourse.bass_utils import run_bass_kernel_spmd

F32 = mybir.dt.float32
BF16 = mybir.dt.bfloat16
I32 = mybir.dt.int32
AF = mybir.ActivationFunctionType
ALU = mybir.AluOpType
AX = mybir.AxisListType

L = 2048
D = 1024
KT = 8
NTT = 16
DEPTH = 2
FFN = 2816
NFT = 22
ALPHA = (2 * DEPTH) ** 0.25
LN_EPS = 1e-5
RMS_EPS = 1e-6
C_HQ, C_HF, C_HI, C_HG = 0, 512, 1024, 1536
C_FQ, C_FK, C_FV, C_FF = 2048, 2560, 3072, 3584
C_SU = 3592
C_GT = 4104
TWO_PI = 2.0 * math.pi

INPUT_SPECS = [
    ("x", [L, D]), ("w_in", [2, 1024, 7176]), ("hg_lower_bounds", [2, 512]), ("hg_norm_w", [2, 128]),
    ("fox_b_f", [2, 8]), ("s5_lambda_re", [2, 32, 64]), ("s5_lambda_im", [2, 32, 64]),
    ("s5_log_step", [2, 32]), ("s5_b_re", [2, 32, 64, 16]), ("s5_b_im", [2, 32, 64, 16]),
    ("s5_c_re", [2, 32, 16, 64]), ("s5_c_im", [2, 32, 16, 64]), ("s5_d", [2, 32, 16]),
    ("s5_w_glu", [2, 512, 512]), ("w_br_a", [2, 512, 1024]), ("w_br_b", [2, 512, 1024]),
    ("w_br_c", [2, 512, 1024]), ("w_out", [2, 1024, 1024]), ("ln1_g", [2, 1024]), ("ln1_b", [2, 1024]),
    ("w_ffn_gate", [2, 1024, 2816]), ("w_ffn_up", [2, 1024, 2816]), ("w_ffn_down", [2, 2816, 1024]),
    ("ln2_g", [2, 1024]), ("ln2_b", [2, 1024]),
]


PACKED = ("w_in", "s5_w_glu", "w_br_a", "w_br_b", "w_br_c", "w_out", "w_ffn_gate", "w_ffn_up", "w_ffn_down")


class Buf:
    __slots__ = ("name", "w", "r")

    def __init__(self, name):
        self.name = name
        self.w = None
        self.r = {}


class KB:
    NDS = 12

    def __init__(self, nc):
        self.nc = nc
        self.E = {"pe": nc.tensor, "act": nc.scalar, "dve": nc.vector, "pool": nc.gpsimd, "sp": nc.sync}
        self.sem = {}
        self.cnt = {}
        self.seen = {e: {} for e in self.E}
        for e in self.E:
            self.sem[e] = nc.alloc_semaphore("s_" + e)
            self.cnt[e] = 0
        self.dq = {}
        for q in ("sp", "pool"):
            keys = []
            for i in range(self.NDS):
                k = "d_%s%d" % (q, i)
                self.sem[k] = nc.alloc_semaphore(k)
                self.cnt[k] = 0
                keys.append(k)
            self.dq[q] = [keys, 0]
        self.nins = 0

    def _wait(self, eng, deps):
        need = {}
        for (s, v) in deps:
            if v > need.get(s, 0):
                need[s] = v
        seen = self.seen[eng]
        for s, v in need.items():
            if eng == "pe" and s == "pe":
                continue
            if seen.get(s, 0) < v:
                self.E[eng].wait_ge(self.sem[s], v)
                seen[s] = v

    @staticmethod
    def _deps(reads, writes):
        deps = []
        for b in reads:
            if b.w is not None:
                deps.append(b.w)
        for b in writes:
            if b.w is not None:
                deps.append(b.w)
            deps.extend(b.r.items())
        return deps

    @staticmethod
    def _commit(ev, reads, writes):
        s, v = ev
        for b in writes:
            b.w = ev
            b.r = {}
        for b in reads:
            if b in writes:
                continue
            if b.r.get(s, 0) < v:
                b.r[s] = v

    def op(self, eng, fn, reads=(), writes=()):
        self._wait(eng, self._deps(reads, writes))
        ins = fn(self.E[eng])
        self.cnt[eng] += 1
        ins.then_inc(self.sem[eng], 1)
        self._commit((eng, self.cnt[eng]), reads, writes)
        self.nins += 1

    def dma(self, q, out, in_, reads=(), writes=(), slow=False):
        keys, i = self.dq[q]
        k = keys[i % len(keys)]
        self.dq[q][1] = i + 1
        deps = self._deps(reads, writes)
        if self.cnt[k] > 0:
            deps.append((k, self.cnt[k]))
        self._wait(q, deps)
        if slow:
            ins = self.E[q].dma_start(out=out, in_=in_, allow_slow_non_contiguous=True)
        else:
            ins = self.E[q].dma_start(out=out, in_=in_)
        self.cnt[k] += 16
        ins.then_inc(self.sem[k], 16)
        self._commit((k, self.cnt[k]), reads, writes)
        self.nins += 1

    def barrier(self, engines=("pe", "act", "dve", "pool", "sp")):
        deps = [(s, v) for s, v in self.cnt.items() if v > 0]
        for e in engines:
            self._wait(e, deps)


class Arena:
    def __init__(self, nc, nbytes):
        self.n = nbytes // 2
        self.t = nc.alloc_sbuf_tensor("arena", [128, self.n], BF16)
        self.ptr = {}
        self.base = {}
        self.lim = {}

    def region(self, name, start, end):
        self.base[name] = start
        self.ptr[name] = start
        self.lim[name] = end

    def reset(self, name):
        self.ptr[name] = self.base[name]

    def alloc(self, region, shape, dtype, parts=128):
        esz = 4 if dtype in (F32, I32) else 2
        n = 1
        for s in shape:
            n *= s
        nb16 = (n * esz + 1) // 2
        nb16 = (nb16 + 15) // 16 * 16
        off = self.ptr[region]
        assert off + nb16 <= self.lim[region], "arena region %s overflow: %d + %d > %d" % (
            region, off, nb16, self.lim[region])
        self.ptr[region] = off + nb16
        v = self.t[0:parts, off:off + (n * esz) // 2]
        if esz == 4:
            v = v.bitcast(dtype)
        if len(shape) == 2:
            v = v.rearrange("p (a b) -> p a b", a=shape[0])
        elif len(shape) == 3:
            v = v.rearrange("p (a b c) -> p a b c", a=shape[0], b=shape[1])
        elif len(shape) == 4:
            v = v.rearrange("p (a b c d) -> p a b c d", a=shape[0], b=shape[1], c=shape[2])
        return v


def build(n_layers=DEPTH, dbg=False, wp_size=None):
    nc = bass.Bass("TRN2", target_bir_lowering=False)
    kb = KB(nc)
    dr = {}
    plan = {"size": 0, "off": {}, "order": []}
    wp = nc.dram_tensor("wpack", [wp_size if wp_size else 48 * 1024 * 1024], F32, kind="ExternalInput").ap()
    for name, shp in INPUT_SPECS:
        if name in PACKED:
            continue
        dr[name] = nc.dram_tensor(name, shp, F32, kind="ExternalInput").ap()
    out_d = nc.dram_tensor("out", [L, D], F32, kind="ExternalOutput").ap()
    hscr = nc.dram_tensor("hscr", [L, D], F32, kind="Internal").ap()
    cumscr = nc.dram_tensor("cumscr", [8, L], F32, kind="Internal").ap()
    b_hscr = Buf("hscr")
    b_cumscr = Buf("cumscr")
    b_out = Buf("out")
    dbg_out = {}

    def dbg_tensor(name, shape):
        t = nc.dram_tensor(name, shape, F32, kind="ExternalOutput").ap()
        dbg_out[name] = t
        return t

    total = nc.sbuf_bytes_remaining - 1024
    total = total // 64 * 64
    ar = Arena(nc, total)
    NEL = ar.n
    P_END = 16384 + 3 * 4096 + 6144
    ar.region("P", 0, P_END)
    Y0 = P_END
    Y_END = Y0 + 3 * 8192
    ar.region("Y", Y0, Y_END)
    ar.region("R", Y_END, NEL)

    banks = []
    bankbufs = []
    for i in range(8):
        banks.append(nc.alloc_psum_tensor("pb%d" % i, [128, 512], F32)[:])
        bankbufs.append(Buf("pb%d" % i))
    rot = {"list": list(range(8)), "i": 0}

    def bank():
        lst = rot["list"]
        i = lst[rot["i"] % len(lst)]
        rot["i"] += 1
        return banks[i], bankbufs[i]

    hT = ar.alloc("P", [KT, L], BF16)
    b_hT = [[Buf("hT%d_%d" % (k, t)) for t in range(4)] for k in range(KT)]

    def hT_bufs(tb=None):
        if tb is None:
            return [b for row in b_hT for b in row]
        return [b_hT[k][tb] for k in range(KT)]

    wslab = [ar.alloc("P", [KT * 512], BF16) for _ in range(3)]
    b_wslab = [Buf("wslab%d" % i) for i in range(3)]
    ident_bf = ar.alloc("P", [128], BF16)
    maskU_bf = ar.alloc("P", [128], BF16)
    ones_bf = ar.alloc("P", [128], BF16)
    ident_f = ar.alloc("P", [128], F32)
    lng = ar.alloc("P", [D], F32)
    lnb = ar.alloc("P", [D], F32)
    b_lng, b_lnb = Buf("lng"), Buf("lnb")
    kk = ar.alloc("P", [33], F32)
    kk_i = ar.alloc("P", [33], I32)
    rowmask = ar.alloc("P", [8], F32)
    negmask = ar.alloc("P", [128], F32)
    eps_ln = ar.alloc("P", [1], F32)
    eps_rms = ar.alloc("P", [1], F32)
    b_const = Buf("const")

    ws = {"req": [], "issued": 0, "next": 0}

    def ws_request(name, l, nk, ncols, pieces):
        names = name if isinstance(name, tuple) else (name,) * len(pieces)
        key = (names, l, nk, ncols, tuple(pieces))
        if key not in plan["off"]:
            plan["off"][key] = plan["size"]
            plan["size"] += 128 * nk * ncols
            plan["order"].append(key)
        off = plan["off"][key]
        n = nk * ncols

        def fn(dst, bdst):
            kb.dma("pool", dst[:, 0:n], wp[off:off + 128 * n].rearrange("(p f) -> p f", p=128), writes=[bdst])
        ws["req"].append((fn, nk, ncols))

    def ws_get(k=1):
        n = ws["next"]
        while ws["issued"] < min(len(ws["req"]), n + 3):
            i = ws["issued"]
            ws["req"][i][0](wslab[i % 3], b_wslab[i % 3])
            ws["issued"] += 1
        ws["next"] = n + k
        outl = []
        for j in range(k):
            _, nk, ncols = ws["req"][n + j]
            v = wslab[(n + j) % 3][:, 0:nk * ncols].rearrange("p (k c) -> p k c", k=nk)
            outl.append((v, b_wslab[(n + j) % 3]))
        return outl[0] if k == 1 else outl

    def wload(dst, bdst, src):
        kb.dma("pool", dst, src, writes=[bdst])

    def setup_consts():
        P = kb.op
        P("pool", lambda e: e.memset(ones_bf, 1.0), writes=[b_const])
        P("pool", lambda e: e.affine_select(out=ident_bf, in_=ones_bf, pattern=[[1, 128]],
                                            compare_op=ALU.is_equal, fill=0.0, base=0, channel_multiplier=-1),
          reads=[b_const], writes=[b_const])
        P("pool", lambda e: e.affine_select(out=maskU_bf, in_=ones_bf, pattern=[[1, 128]],
                                            compare_op=ALU.is_ge, fill=0.0, base=0, channel_multiplier=-1),
          reads=[b_const], writes=[b_const])
        P("dve", lambda e: e.tensor_copy(ident_f, ident_bf), reads=[b_const], writes=[b_const])
        P("dve", lambda e: e.tensor_scalar(negmask, maskU_bf, 30000.0, -30000.0, ALU.mult, ALU.add),
          reads=[b_const], writes=[b_const])
        P("pool", lambda e: e.iota(kk_i, pattern=[[1, 33]], base=0, channel_multiplier=0), writes=[b_const])
        P("pool", lambda e: e.memset(eps_ln, LN_EPS), reads=[b_const], writes=[b_const])
        P("pool", lambda e: e.memset(eps_rms, RMS_EPS), reads=[b_const], writes=[b_const])
        P("dve", lambda e: e.tensor_copy(kk, kk_i), reads=[b_const], writes=[b_const])
        P("pool", lambda e: e.memset(rowmask, 1.0), reads=[b_const], writes=[b_const])
        P("pool", lambda e: e.affine_select(out=rowmask, in_=rowmask, pattern=[[-16, 8]],
                                            compare_op=ALU.is_ge, fill=0.0, base=0, channel_multiplier=1),
          reads=[b_const], writes=[b_const])
        P("pool", lambda e: e.affine_select(out=rowmask, in_=rowmask, pattern=[[16, 8]],
                                            compare_op=ALU.is_ge, fill=0.0, base=15, channel_multiplier=-1),
          reads=[b_const], writes=[b_const])

    def proj_fm(slab, bslab, c0, ncols, tb, kts=KT, rhs_src=None, rhs_bufs=None):
        pb, bb = bank()
        for kt in range(kts):
            if rhs_src is None:
                rhs = hT[:, kt, tb * 512:(tb + 1) * 512]
                rb = [b_hT[kt][tb]]
            else:
                rhs = rhs_src[:, kt, tb * 512:(tb + 1) * 512]
                rb = rhs_bufs
            kb.op("pe", lambda e, kt=kt, rhs=rhs: e.matmul(pb[0:ncols, :], slab[:, kt, c0:c0 + ncols], rhs,
                                                           start=(kt == 0), stop=(kt == kts - 1)),
                  reads=[bslab] + rb, writes=[bb])
        return pb, bb

    def w_in_cols(l, c0, n):
        return dr["w_in"][l, :, c0:c0 + n].rearrange("(kt p) e -> p kt e", p=128)

    def transpose_to_hT(src_bf, bsrc, tt):
        pb, bb = bank()
        pv = pb[:].bitcast(BF16)
        for kt in range(KT):
            kb.op("pe", lambda e, kt=kt: e.transpose(pv[:, kt * 128:(kt + 1) * 128],
                                                     src_bf[:, kt * 128:(kt + 1) * 128], ident_bf),
                  reads=[bsrc, b_const], writes=[bb])
        kb.op("act", lambda e: e.copy(hT[:, :, tt * 128:(tt + 1) * 128],
                                      pv.rearrange("p (k t) -> p k t", k=KT)),
              reads=[bb], writes=[b_hT[k][tt // 4] for k in range(KT)])

    def layer_norm_tile(xs, bxs, g_t, b_t, out_f32, bout, stat, bstat):
        kb.op("dve", lambda e: e.bn_stats(stat[:, 0:6], xs[:, 0:512]), reads=[bxs], writes=[bstat])
        kb.op("dve", lambda e: e.bn_stats(stat[:, 6:12], xs[:, 512:1024]), reads=[bxs, bstat], writes=[bstat])
        kb.op("dve", lambda e: e.bn_aggr(stat[:, 12:14], stat[:, 0:12]), reads=[bstat], writes=[bstat])
        kb.op("act", lambda e: e.activation(out=stat[:, 14:15], in_=stat[:, 13:14], func=AF.Sqrt, bias=eps_ln[:, 0:1]),
              reads=[bstat, b_const], writes=[bstat])
        kb.op("dve", lambda e: e.reciprocal(stat[:, 14:15], stat[:, 14:15]), reads=[bstat], writes=[bstat])
        kb.op("dve", lambda e: e.scalar_tensor_tensor(stat[:, 15:16], stat[:, 12:13], -1.0, stat[:, 14:15],
                                                      ALU.mult, ALU.mult), reads=[bstat], writes=[bstat])
        kb.op("act", lambda e: e.activation(out=xs, in_=xs, func=AF.Identity, bias=stat[:, 15:16],
                                            scale=stat[:, 14:15]), reads=[bxs, bstat], writes=[bxs])
        kb.op("dve", lambda e: e.tensor_tensor(xs, xs, g_t, ALU.mult), reads=[bxs, b_lng], writes=[bxs])
        kb.op("dve", lambda e: e.tensor_tensor(out_f32, xs, b_t, ALU.add), reads=[bxs, b_lnb], writes=[bout])

    def load_input():
        ar.reset("R")
        xs = [ar.alloc("R", [D], F32) for _ in range(2)]
        xb = [ar.alloc("R", [D], BF16) for _ in range(2)]
        bxs = [Buf("xs0"), Buf("xs1")]
        bxb = [Buf("xb0"), Buf("xb1")]
        for tt in range(NTT):
            i = tt % 2
            kb.dma("sp", xs[i], dr["x"][tt * 128:(tt + 1) * 128, :], writes=[bxs[i]])
            kb.op("dve", lambda e, i=i: e.tensor_copy(xb[i], xs[i]), reads=[bxs[i]], writes=[bxb[i]])
            transpose_to_hT(xb[i], bxb[i], tt)

    def hgrn2(l, yT_a, b_yT_a):
        ar.reset("R")
        R = lambda shape, dt, parts=128: ar.alloc("R", shape, dt, parts)
        lbraw = R([2, 4], F32)
        lb = R([4], F32)
        oml = R([4], F32)
        normw = R([1], F32)
        b_sm = Buf("hg_small")
        scanmask = R([L], F32)
        A = R([L], F32)
        B = R([L], F32)
        G = R([L], F32)
        bA, bB, bG = Buf("A"), Buf("B"), Buf("G")
        qg = R([L], BF16)
        kg = R([L], BF16)
        kte = R([L], BF16)
        bqg, bkg, bkte = Buf("qg"), Buf("kg"), Buf("kte")
        vtok = R([32, 128], BF16)
        kteT = R([32, 128], BF16)
        sTm = R([32, 64], BF16)
        states_bf = R([32, 128], BF16)
        st_pp = [R([128], F32), R([128], F32)]
        egl = R([32], F32)
        bvtok, bkteT, bsTm, bstbf, begl = Buf("vtok"), Buf("kteT"), Buf("sTm"), Buf("stbf"), Buf("egl")
        bstpp = [Buf("stpp0"), Buf("stpp1")]

        kb.dma("sp", lbraw, dr["hg_lower_bounds"].rearrange("l (h k) -> k l h", k=128), writes=[b_sm], slow=True)
        kb.dma("sp", normw, dr["hg_norm_w"][l].rearrange("(p o) -> p o", o=1), writes=[b_sm])
        if l == 0:
            kb.op("dve", lambda e: e.memset(lb, 0.0), reads=[b_sm], writes=[b_sm])
        else:
            kb.op("dve", lambda e: e.tensor_tensor(lb, lbraw[:, 1, :], lbraw[:, 0, :], ALU.subtract),
                  reads=[b_sm], writes=[b_sm])
            kb.op("act", lambda e: e.activation(out=lb, in_=lb, func=AF.Sigmoid), reads=[b_sm], writes=[b_sm])
        kb.op("dve", lambda e: e.tensor_scalar(oml, lb, -1.0, 1.0, ALU.mult, ALU.add), reads=[b_sm], writes=[b_sm])
        kb.op("pool", lambda e: e.memset(scanmask, 1.0), writes=[b_sm])
        kb.op("pool", lambda e: e.memset(scanmask.rearrange("p (c s) -> p c s", s=64)[:, :, 0:1], 0.0),
              reads=[b_sm], writes=[b_sm])

        for h in range(4):
            ws_request("w_in", l, KT, 512, [(0, c0 + h * 128, 128, j * 128)
                                             for j, c0 in enumerate((C_HQ, C_HF, C_HI, C_HG))])
        for h in range(4):
            slab, bslab = ws_get()
            G3 = G.rearrange("p (c s) -> p c s", s=64)
            B3 = B.rearrange("p (c s) -> p c s", s=64)
            for tb in range(4):
                pb, bb = proj_fm(slab, bslab, 128, 128, tb)
                kb.op("act", lambda e, tb=tb, pb=pb: e.activation(out=A[:, tb * 512:(tb + 1) * 512], in_=pb,
                                                                  func=AF.Sigmoid), reads=[bb], writes=[bA])
            kb.op("dve", lambda e, h=h: e.tensor_scalar(A, A, oml[:, h:h + 1], lb[:, h:h + 1], ALU.mult, ALU.add),
                  reads=[bA, b_sm], writes=[bA])
            kb.op("act", lambda e: e.activation(out=B, in_=A, func=AF.Ln), reads=[bA], writes=[bB])
            kb.op("dve", lambda e: e.tensor_scalar(A, A, -1.0, 1.0, ALU.mult, ALU.add), reads=[bA], writes=[bA])
            kb.op("dve", lambda e: e.tensor_tensor_scan(G, scanmask, B, 0.0, ALU.mult, ALU.add),
                  reads=[bB, b_sm], writes=[bG])
            kb.op("act", lambda e: e.activation(out=B, in_=G, func=AF.Exp), reads=[bG], writes=[bB])
            for tb in range(4):
                pb, bb = proj_fm(slab, bslab, 0, 128, tb)
                kb.op("dve", lambda e, tb=tb, pb=pb: e.tensor_tensor(qg[:, tb * 512:(tb + 1) * 512], pb,
                                                                     B[:, tb * 512:(tb + 1) * 512], ALU.mult),
                      reads=[bb, bB], writes=[bqg])
            kb.op("act", lambda e: e.activation(out=B, in_=G, func=AF.Exp, scale=-1.0), reads=[bG], writes=[bB])
            kb.op("dve", lambda e: e.tensor_tensor(kg, A, B, ALU.mult), reads=[bA, bB], writes=[bkg])
            kb.op("dve", lambda e: e.tensor_tensor(B3, G3[:, :, 63:64].to_broadcast([128, 32, 64]), G3,
                                                   ALU.subtract), reads=[bG], writes=[bB])
            kb.op("act", lambda e: e.activation(out=B, in_=B, func=AF.Exp), reads=[bB], writes=[bB])
            kb.op("dve", lambda e: e.tensor_tensor(kte, A, B, ALU.mult), reads=[bA, bB], writes=[bkte])
            kb.op("act", lambda e: e.activation(out=egl, in_=G3[:, :, 63], func=AF.Exp), reads=[bG], writes=[begl])
            for c4 in range(8):
                pb, bb = bank()
                for cc in range(4):
                    c = c4 * 4 + cc
                    for kt in range(KT):
                        kb.op("pe", lambda e, kt=kt, c=c, cc=cc, pb=pb: e.matmul(
                            pb[0:64, cc * 128:(cc + 1) * 128], hT[:, kt, c * 64:(c + 1) * 64],
                            slab[:, kt, 256:384], start=(kt == 0), stop=(kt == KT - 1)),
                            reads=[bslab, b_hT[kt][c // 8]], writes=[bb])
                kb.op("act", lambda e, c4=c4, pb=pb: e.copy(vtok[0:64, c4 * 4:(c4 + 1) * 4, :],
                                                            pb[0:64, :].rearrange("p (a b) -> p a b", a=4)),
                      reads=[bb], writes=[bvtok])
            for tb in range(4):
                pb, bb = proj_fm(slab, bslab, 384, 128, tb)
                kb.op("act", lambda e, tb=tb, pb=pb: e.activation(out=A[:, tb * 512:(tb + 1) * 512], in_=pb,
                                                                  func=AF.Silu), reads=[bb], writes=[bA])
            for c8 in range(4):
                pb, bb = bank()
                for cc in range(8):
                    c = c8 * 8 + cc
                    kb.op("pe", lambda e, c=c, cc=cc, pb=pb: e.matmul(
                        pb[0:64, cc * 64:(cc + 1) * 64], kg[:, c * 64:(c + 1) * 64], qg[:, c * 64:(c + 1) * 64],
                        start=True, stop=True), reads=[bkg, bqg], writes=[bb])
                kb.op("dve", lambda e, c8=c8, pb=pb: e.tensor_tensor(
                    sTm[0:64, c8 * 8:(c8 + 1) * 8, :], pb[0:64, :].rearrange("p (a b) -> p a b", a=8),
                    maskU_bf[0:64, 0:64].unsqueeze(1).to_broadcast([64, 8, 64]), ALU.mult),
                    reads=[bb, b_const], writes=[bsTm])
            for c8 in range(4):
                pb, bb = bank()
                pv = pb[:].bitcast(BF16)
                for cc in range(8):
                    c = c8 * 8 + cc
                    kb.op("pe", lambda e, c=c, cc=cc, pv=pv: e.transpose(
                        pv[0:64, cc * 128:(cc + 1) * 128], kte[:, c * 64:(c + 1) * 64], ident_bf),
                        reads=[bkte, b_const], writes=[bb])
                kb.op("act", lambda e, c8=c8, pv=pv: e.copy(kteT[0:64, c8 * 8:(c8 + 1) * 8, :],
                                                            pv[0:64, :].rearrange("p (a b) -> p a b", a=8)),
                      reads=[bb], writes=[bkteT])
            for c4 in range(8):
                pb, bb = bank()
                for cc in range(4):
                    c = c4 * 4 + cc
                    kb.op("pe", lambda e, c=c, cc=cc, pb=pb: e.matmul(
                        pb[:, cc * 128:(cc + 1) * 128], kteT[0:64, c, :], vtok[0:64, c, :], start=True, stop=True),
                        reads=[bkteT, bvtok], writes=[bb])
                for cc in range(4):
                    c = c4 * 4 + cc
                    cur, prv = st_pp[c % 2], st_pp[(c + 1) % 2]
                    bcur, bprv = bstpp[c % 2], bstpp[(c + 1) % 2]
                    if c == 0:
                        kb.op("dve", lambda e, pb=pb, cur=cur: e.tensor_copy(cur, pb[:, 0:128]),
                              reads=[bb], writes=[bcur])
                    else:
                        kb.op("dve", lambda e, pb=pb, cur=cur, prv=prv, c=c, cc=cc: e.scalar_tensor_tensor(
                            cur, prv, egl[:, c:c + 1], pb[:, cc * 128:(cc + 1) * 128], ALU.mult, ALU.add),
                            reads=[bb, bprv, begl], writes=[bcur])
                    kb.op("act", lambda e, cur=cur, c=c: e.copy(states_bf[:, c, :], cur), reads=[bcur],
                          writes=[bstbf])
            for c8 in range(4):
                pb, bb = bank()
                for cc in range(8):
                    c = c8 * 8 + cc
                    if c >= 1:
                        kb.op("pe", lambda e, c=c, cc=cc, pb=pb: e.matmul(
                            pb[:, cc * 64:(cc + 1) * 64], states_bf[:, c - 1, :], qg[:, c * 64:(c + 1) * 64],
                            start=True, stop=False), reads=[bstbf, bqg], writes=[bb])
                    kb.op("pe", lambda e, c=c, cc=cc, pb=pb: e.matmul(
                        pb[:, cc * 64:(cc + 1) * 64], vtok[0:64, c, :], sTm[0:64, c, :],
                        start=(c == 0), stop=True), reads=[bvtok, bsTm], writes=[bb])
                kb.op("act", lambda e, c8=c8, pb=pb: e.copy(B[:, c8 * 512:(c8 + 1) * 512], pb), reads=[bb],
                      writes=[bB])
                kb.op("act", lambda e, c8=c8, pb=pb: e.activation(out=kg[:, c8 * 512:(c8 + 1) * 512], in_=pb,
                                                                  func=AF.Square), reads=[bb], writes=[bkg])
            for tb in range(4):
                pb, bb = bank()
                kb.op("pe", lambda e, tb=tb, pb=pb: e.matmul(pb, ones_bf, kg[:, tb * 512:(tb + 1) * 512],
                                                             start=True, stop=True), reads=[bkg, b_const],
                      writes=[bb])
                kb.op("act", lambda e, tb=tb, pb=pb: e.activation(out=G[:, tb * 512:(tb + 1) * 512], in_=pb,
                                                                  func=AF.Sqrt, bias=eps_rms[:, 0:1],
                                                                  scale=1.0 / 128.0),
                      reads=[bb, b_const], writes=[bG])
            kb.op("dve", lambda e: e.reciprocal(G, G), reads=[bG], writes=[bG])
            kb.op("dve", lambda e: e.tensor_tensor(B, B, G, ALU.mult), reads=[bB, bG], writes=[bB])
            kb.op("dve", lambda e, h=h: e.scalar_tensor_tensor(yT_a[:, h, :], B, normw[:, 0:1], A, ALU.mult,
                                                               ALU.mult), reads=[bB, bA, b_sm], writes=[b_yT_a])

    def fox(l, yT_b, b_yT_b):
        ar.reset("R")
        R = lambda shape, dt, parts=128: ar.alloc("R", shape, dt, parts)
        qT = R([L], BF16)
        kT = R([L], BF16)
        vaug = R([NTT, 2, 65], BF16)
        bqT, bkT, bvaug = Buf("qT"), Buf("kT"), Buf("vaug")
        cum = R([L], F32, 8)
        cum2 = R([L], F32, 8)
        bcum = Buf("cum")
        bfb = R([1], F32, 8)
        ncum = R([NTT, 8], F32)
        bncum = Buf("ncum")
        cumbc = [R([L], F32), R([L], F32)]
        bcumbc = [Buf("cumbc0"), Buf("cumbc1")]
        tmp = [R([512], F32), R([512], F32)]
        btmp = [Buf("tmp0"), Buf("tmp1")]
        pT = [R([512], BF16), R([512], BF16), R([512], BF16)]
        bpT = [Buf("pT0"), Buf("pT1"), Buf("pT2")]
        yb = R([NTT, 512], BF16)
        byb = Buf("yb")
        rl = R([NTT], F32)
        brl = Buf("rl")
        ones8 = R([L], F32, 8)

        ws_request("w_in", l, KT, 8, [(0, C_FF, 8, 0)])
        for hp in range(4):
            ws_request("w_in", l, KT, 384, [(0, c0 + hp * 128, 128, j * 128)
                                             for j, c0 in enumerate((C_FQ, C_FK, C_FV))])
        slab, bslab = ws_get()
        kb.dma("sp", bfb, dr["fox_b_f"][l].rearrange("(p o) -> p o", o=1), writes=[bcum])
        kb.op("dve", lambda e: e.memset(ones8, 1.0), writes=[bcum])
        for tb in range(4):
            pb, bb = proj_fm(slab, bslab, 0, 8, tb)
            kb.op("act", lambda e, tb=tb, pb=pb: e.activation(out=cum[:, tb * 512:(tb + 1) * 512], in_=pb[0:8, :],
                                                              func=AF.Sigmoid, bias=bfb[:, 0:1]),
                  reads=[bb, bcum], writes=[bcum])
        kb.op("act", lambda e: e.activation(out=cum, in_=cum, func=AF.Ln), reads=[bcum], writes=[bcum])
        kb.op("dve", lambda e: e.tensor_tensor_scan(cum2, ones8, cum, 0.0, ALU.mult, ALU.add),
              reads=[bcum], writes=[bcum])
        kb.dma("sp", cumscr, cum2, reads=[bcum], writes=[b_cumscr])
        if dbg and l == 0:
            kb.dma("sp", dbg_tensor("dbg_cum", [8, L]), cum2, reads=[bcum], writes=[Buf("dbgcum")])
        pb, bb = bank()
        for S in range(NTT):
            kb.op("pe", lambda e, S=S, pb=pb: e.transpose(pb[:, S * 8:(S + 1) * 8], cum2[:, S * 128:(S + 1) * 128],
                                                          ident_f[0:8, 0:8]), reads=[bcum, b_const], writes=[bb])
        kb.op("dve", lambda e, pb=pb: e.tensor_scalar(ncum, pb[:, 0:128].rearrange("p (a b) -> p a b", a=NTT), -1.0,
                                                      None, ALU.mult), reads=[bb], writes=[bncum])
        kb.op("pool", lambda e: e.memset(vaug[:, :, :, 64:65], 1.0), writes=[bvaug])

        rot["list"] = [0, 1, 2, 3, 4]
        rot["i"] = 0

        def oacc(T):
            bi = 5 + T // 6
            j = T % 6
            return banks[bi][:, j * 65:(j + 1) * 65], bankbufs[bi]

        gi = 0
        for hp in range(4):
            slab, bslab = ws_get()
            for tb in range(4):
                pb, bb = proj_fm(slab, bslab, 0, 128, tb)
                kb.op("act", lambda e, tb=tb, pb=pb: e.activation(out=qT[:, tb * 512:(tb + 1) * 512], in_=pb,
                                                                  func=AF.Copy, scale=0.125), reads=[bb],
                      writes=[bqT])
                pb, bb = proj_fm(slab, bslab, 128, 128, tb)
                kb.op("act", lambda e, tb=tb, pb=pb: e.copy(kT[:, tb * 512:(tb + 1) * 512], pb), reads=[bb],
                      writes=[bkT])
            for t4 in range(4):
                pb, bb = bank()
                for tq in range(4):
                    tt = t4 * 4 + tq
                    for kt in range(KT):
                        kb.op("pe", lambda e, kt=kt, tt=tt, tq=tq, pb=pb: e.matmul(
                            pb[:, tq * 128:(tq + 1) * 128], hT[:, kt, tt * 128:(tt + 1) * 128],
                            slab[:, kt, 256:384], start=(kt == 0), stop=(kt == KT - 1)),
                            reads=[bslab, b_hT[kt][tt // 4]], writes=[bb])
                kb.op("act", lambda e, t4=t4, pb=pb: e.copy(
                    vaug[:, t4 * 4:(t4 + 1) * 4, :, 0:64],
                    pb[:].rearrange("p (a h d) -> p a h d", a=4, h=2)), reads=[bb], writes=[bvaug])
            for hh in range(2):
                h = hp * 2 + hh
                hb = hh * 64
                cb, bcb = cumbc[h % 2], bcumbc[h % 2]
                kb.dma("sp", cb, cumscr[h].partition_broadcast(128), reads=[b_cumscr], writes=[bcb])
                groups = []
                for S in range(NTT):
                    t0 = S * 128
                    first = True
                    while t0 < L:
                        n = min(512, L - t0)
                        groups.append((S, t0, n, first))
                        first = False
                        t0 += n
                LOOK = 2
                sc_banks = {}

                def emit_scores(gidx):
                    S, t0, n, first = groups[gidx]
                    pb, bb = bank()
                    kb.op("pe", lambda e: e.matmul(
                        pb[:, 0:n], kT[hb:hb + 64, S * 128:(S + 1) * 128], qT[hb:hb + 64, t0:t0 + n],
                        start=True, stop=True), reads=[bkT, bqT], writes=[bb])
                    sc_banks[gidx] = (pb, bb)

                for gidx in range(min(LOOK, len(groups))):
                    emit_scores(gidx)
                for gidx, (S, t0, n, first) in enumerate(groups):
                    if gidx + LOOK < len(groups):
                        emit_scores(gidx + LOOK)
                    pb, bb = sc_banks.pop(gidx)
                    tm, btm = tmp[gi % 2], btmp[gi % 2]
                    pt, bpt = pT[gi % 3], bpT[gi % 3]
                    gi += 1
                    kb.op("dve", lambda e: e.scalar_tensor_tensor(
                        tm[:, 0:n], cb[:, t0:t0 + n], ncum[:, S, h:h + 1], pb[:, 0:n], ALU.add, ALU.add),
                        reads=[bb, bcb, bncum], writes=[btm])
                    if first:
                        kb.op("dve", lambda e: e.tensor_tensor(tm[:, 0:128], tm[:, 0:128], negmask, ALU.add),
                              reads=[btm, b_const], writes=[btm])
                    kb.op("act", lambda e: e.activation(out=pt[:, 0:n], in_=tm[:, 0:n], func=AF.Exp),
                          reads=[btm], writes=[bpt])
                    for j in range(n // 128):
                        T = t0 // 128 + j
                        oa, boa = oacc(T)
                        kb.op("pe", lambda e: e.matmul(
                            oa, pt[:, j * 128:(j + 1) * 128], vaug[:, S, hh, :],
                            start=(S == 0 and T % 6 == 0), stop=(S == T), skip_group_check=True),
                            reads=[bpt, bvaug], writes=[boa])
                for bi, (Ta, Tn) in enumerate(((0, 6), (6, 6), (12, 4))):
                    pbk, bbk = banks[5 + bi], bankbufs[5 + bi]
                    v3 = pbk[:, 0:Tn * 65].rearrange("p (a b) -> p a b", b=65)
                    kb.op("dve", lambda e, Ta=Ta, Tn=Tn, v3=v3: e.reciprocal(rl[:, Ta:Ta + Tn], v3[:, :, 64]),
                          reads=[bbk], writes=[brl])
                    kb.op("dve", lambda e, Ta=Ta, Tn=Tn, v3=v3, h=h: e.tensor_tensor(
                        yb[:, Ta:Ta + Tn, h * 64:(h + 1) * 64], v3[:, :, 0:64],
                        rl[:, Ta:Ta + Tn].unsqueeze(2).to_broadcast([128, Tn, 64]), ALU.mult),
                        reads=[bbk, brl], writes=[byb])
        rot["list"] = list(range(8))
        for tt in range(NTT):
            pb, bb = bank()
            pv = pb[:].bitcast(BF16)
            for ct in range(4):
                kb.op("pe", lambda e, ct=ct, tt=tt, pv=pv: e.transpose(pv[:, ct * 128:(ct + 1) * 128],
                                                                       yb[:, tt, ct * 128:(ct + 1) * 128], ident_bf),
                      reads=[byb, b_const], writes=[bb])
            kb.op("act", lambda e, tt=tt, pv=pv: e.copy(yT_b[:, :, tt * 128:(tt + 1) * 128],
                                                        pv[:, 0:512].rearrange("p (c t) -> p c t", c=4)),
                  reads=[bb], writes=[b_yT_b])

    def s5(l, yT_c, b_yT_c):
        ar.reset("R")
        R = lambda shape, dt, parts=128: ar.alloc("R", shape, dt, parts)
        V = kb.op
        lo, hi = slice(0, 64), slice(64, 128)
        lre = R([8], F32)
        lim = R([8], F32)
        dts = R([8], F32)
        sm = R([8, 8], F32)
        bsm = Buf("s5small")
        Lre = R([8, 33], F32)
        Lim = R([8, 33], F32)
        LA = R([8, 33], F32)
        LB = R([8, 33], F32)
        LA2 = R([8, 33], F32)
        LB2 = R([8, 33], F32)
        ang = R([8, 33], F32)
        mag = R([8, 33], F32)
        angi = R([8, 33], I32)
        angf = R([8, 33], F32)
        bL = Buf("Ltab")
        Bre = R([8, 16], F32)
        Bim = R([8, 16], F32)
        Bbr = R([8, 16], F32)
        Bbi = R([8, 16], F32)
        bB_ = Buf("Btab")
        Cnat = R([2, 2, 64], F32)
        CA = R([8, 16], F32)
        CB = R([8, 16], F32)
        Cst = R([8, 16], BF16)
        bC = Buf("Ctab")
        dcol = R([1], F32)
        t1 = R([4, 128], F32)
        t2 = R([4, 128], F32)
        bt = Buf("s5t")
        BLd = R([33, 128], BF16)
        bBLd = Buf("BLd")
        pad = R([4, 8, 128], BF16)
        bpad = Buf("pad")
        Ktab = R([32, 128], BF16)
        bK = Buf("Ktab")
        uc = R([L], BF16)
        buc = Buf("uc")
        Abf = R([8, 96], BF16)
        bAbf = Buf("Abf")
        g2 = R([512], F32)
        bg = Buf("gelu")
        sig = R([4, 512], F32)
        bsig = Buf("sig")
        mark = ar.ptr["R"]
        Wd = R([32, 128], BF16)
        Wds = R([32, 128], BF16)
        um = R([L], BF16)
        bWd = bum = bX = Buf("s5alias")
        ar.ptr["R"] = mark
        XA = [R([8, 96], F32), R([8, 96], F32)]
        XB = [R([8, 96], F32), R([8, 96], F32)]
        xt1 = R([8, 64], F32)
        xt2 = R([8, 64], F32)
        cpow = R([4, 8], F32)
        Gd = BLd
        bGd = bBLd

        ws_request("w_in", l, KT, 512, [(0, C_SU, 512, 0)])
        ws_request("s5_w_glu", l, 4, 512, [(0, 0, 512, 0)])
        slab, bslab = ws_get()

        def padded(Tpad):
            return bass.AP(Tpad.tensor, Tpad.offset, [list(Tpad.ap[0]), [8 * 128, 4], [144, 8], [1, 16]])

        for ct in range(4):
            g0 = ct * 8
            for tb in range(4):
                pb, bb = proj_fm(slab, bslab, ct * 128, 128, tb)
                V("act", lambda e, tb=tb, pb=pb: e.copy(uc[:, tb * 512:(tb + 1) * 512], pb), reads=[bb],
                  writes=[buc])
            for half in range(2):
                hs = slice(half * 64, half * 64 + 64)
                kb.dma("sp", lre[hs, :], dr["s5_lambda_re"][l, g0:g0 + 8, :].rearrange("g p -> p g"),
                       writes=[bsm], slow=True)
                kb.dma("sp", lim[hs, :], dr["s5_lambda_im"][l, g0:g0 + 8, :].rearrange("g p -> p g"),
                       writes=[bsm], slow=True)
                kb.dma("sp", Bre[hs, :, :], dr["s5_b_re"][l, g0:g0 + 8].rearrange("g p h -> p g h"),
                       writes=[bB_])
                kb.dma("sp", Bim[hs, :, :], dr["s5_b_im"][l, g0:g0 + 8].rearrange("g p h -> p g h"),
                       writes=[bB_])
            kb.dma("sp", dts, dr["s5_log_step"][l, g0:g0 + 8].partition_broadcast(128), writes=[bsm])
            for ri, nm in enumerate(("s5_c_re", "s5_c_im")):
                for dup in range(2):
                    kb.dma("sp", Cnat[:, ri, dup, :],
                           dr[nm][l, g0:g0 + 8].rearrange("g h p -> (g h) p"), writes=[bC])
            kb.dma("sp", dcol, dr["s5_d"][l, g0:g0 + 8].rearrange("g (h o) -> (g h) o", o=1), writes=[bC])
            S = lambda i: sm[:, i, :]
            V("act", lambda e: e.activation(out=dts, in_=dts, func=AF.Exp), reads=[bsm], writes=[bsm])
            V("dve", lambda e: e.tensor_tensor(S(0), lre, dts, ALU.mult), reads=[bsm], writes=[bsm])
            V("dve", lambda e: e.tensor_tensor(S(1), lim, dts, ALU.mult), reads=[bsm], writes=[bsm])
            kkb = kk.unsqueeze(1).to_broadcast([128, 8, 33])
            V("dve", lambda e: e.tensor_tensor(mag, S(0).unsqueeze(2).to_broadcast([128, 8, 33]), kkb, ALU.mult),
              reads=[bsm, b_const], writes=[bL])
            V("act", lambda e: e.activation(out=mag, in_=mag, func=AF.Exp), reads=[bL], writes=[bL])
            V("dve", lambda e: e.tensor_tensor(ang, S(1).unsqueeze(2).to_broadcast([128, 8, 33]), kkb, ALU.mult),
              reads=[bsm, b_const], writes=[bL])
            def sin_of(dst, shift):
                C1 = 6.28125
                C2 = TWO_PI - C1
                V("dve", lambda e: e.tensor_scalar(dst, ang, shift, 1.0 / TWO_PI, ALU.add, ALU.mult),
                  reads=[bL], writes=[bL])
                V("dve", lambda e: e.tensor_copy(angi, dst), reads=[bL], writes=[bL])
                V("dve", lambda e: e.tensor_copy(angf, angi), reads=[bL], writes=[bL])
                V("dve", lambda e: e.tensor_scalar(dst, ang, shift, None, ALU.add), reads=[bL], writes=[bL])
                V("dve", lambda e: e.scalar_tensor_tensor(dst, angf, -C1, dst, ALU.mult, ALU.add), reads=[bL],
                  writes=[bL])
                V("dve", lambda e: e.scalar_tensor_tensor(dst, angf, -C2, dst, ALU.mult, ALU.add), reads=[bL],
                  writes=[bL])
                V("dve", lambda e: e.tensor_scalar(angf, dst, math.pi, -TWO_PI, ALU.is_gt, ALU.mult), reads=[bL],
                  writes=[bL])
                V("dve", lambda e: e.tensor_tensor(dst, dst, angf, ALU.add), reads=[bL], writes=[bL])
                V("dve", lambda e: e.tensor_scalar(angf, dst, -math.pi, TWO_PI, ALU.is_lt, ALU.mult), reads=[bL],
                  writes=[bL])
                V("dve", lambda e: e.tensor_tensor(dst, dst, angf, ALU.add), reads=[bL], writes=[bL])
                V("act", lambda e: e.activation(out=dst, in_=dst, func=AF.Sin), reads=[bL], writes=[bL])
            sin_of(Lim, 0.0)
            sin_of(Lre, 0.5 * math.pi)
            V("dve", lambda e: e.tensor_tensor(Lre, Lre, mag, ALU.mult), reads=[bL], writes=[bL])
            V("dve", lambda e: e.tensor_tensor(Lim, Lim, mag, ALU.mult), reads=[bL], writes=[bL])
            V("dve", lambda e: e.tensor_copy(LA[lo], Lre[lo]), reads=[bL], writes=[bL])
            V("dve", lambda e: e.tensor_copy(LA[hi], Lim[hi]), reads=[bL], writes=[bL])
            V("dve", lambda e: e.tensor_scalar(LB[lo], Lim[lo], -1.0, None, ALU.mult), reads=[bL], writes=[bL])
            V("dve", lambda e: e.tensor_copy(LB[hi], Lre[hi]), reads=[bL], writes=[bL])
            V("dve", lambda e: e.tensor_copy(LA2[lo], Lre[lo]), reads=[bL], writes=[bL])
            V("dve", lambda e: e.tensor_scalar(LA2[hi], Lim[hi], -1.0, None, ALU.mult), reads=[bL], writes=[bL])
            V("dve", lambda e: e.tensor_scalar(LB2[lo], Lim[lo], -1.0, None, ALU.mult), reads=[bL], writes=[bL])
            V("dve", lambda e: e.tensor_scalar(LB2[hi], Lre[hi], -1.0, None, ALU.mult), reads=[bL], writes=[bL])
            V("dve", lambda e: e.tensor_scalar(S(2), Lre[:, :, 1], -1.0, None, ALU.add), reads=[bL, bsm],
              writes=[bsm])
            V("dve", lambda e: e.tensor_tensor(S(3), lre, lre, ALU.mult), reads=[bsm], writes=[bsm])
            V("dve", lambda e: e.tensor_tensor(S(4), lim, lim, ALU.mult), reads=[bsm], writes=[bsm])
            V("dve", lambda e: e.tensor_tensor(S(3), S(3), S(4), ALU.add), reads=[bsm], writes=[bsm])
            V("dve", lambda e: e.reciprocal(S(3), S(3)), reads=[bsm], writes=[bsm])
            V("dve", lambda e: e.tensor_tensor(S(4), S(2), lre, ALU.mult), reads=[bsm], writes=[bsm])
            V("dve", lambda e: e.tensor_tensor(S(5), Lim[:, :, 1], lim, ALU.mult), reads=[bsm, bL], writes=[bsm])
            V("dve", lambda e: e.tensor_tensor(S(4), S(4), S(5), ALU.add), reads=[bsm], writes=[bsm])
            V("dve", lambda e: e.tensor_tensor(S(6), S(4), S(3), ALU.mult), reads=[bsm], writes=[bsm])
            V("dve", lambda e: e.tensor_tensor(S(4), Lim[:, :, 1], lre, ALU.mult), reads=[bsm, bL], writes=[bsm])
            V("dve", lambda e: e.tensor_tensor(S(5), S(2), lim, ALU.mult), reads=[bsm], writes=[bsm])
            V("dve", lambda e: e.tensor_tensor(S(4), S(4), S(5), ALU.subtract), reads=[bsm], writes=[bsm])
            V("dve", lambda e: e.tensor_tensor(S(7), S(4), S(3), ALU.mult), reads=[bsm], writes=[bsm])
            crb = S(6).unsqueeze(2).to_broadcast([128, 8, 16])
            cib = S(7).unsqueeze(2).to_broadcast([128, 8, 16])
            V("dve", lambda e: e.tensor_tensor(Bbr, Bre, crb, ALU.mult), reads=[bsm, bB_], writes=[bB_])
            V("dve", lambda e: e.tensor_tensor(Bbi, Bim, cib, ALU.mult), reads=[bsm, bB_], writes=[bB_])
            V("dve", lambda e: e.tensor_tensor(Bbr, Bbr, Bbi, ALU.subtract), reads=[bB_], writes=[bB_])
            V("dve", lambda e: e.tensor_tensor(Bbi, Bim, crb, ALU.mult), reads=[bsm, bB_], writes=[bB_])
            V("dve", lambda e: e.tensor_tensor(Bre, Bre, cib, ALU.mult), reads=[bsm, bB_], writes=[bB_])
            V("dve", lambda e: e.tensor_tensor(Bbi, Bbi, Bre, ALU.add), reads=[bB_], writes=[bB_])
            pb, bb = bank()
            for ri in range(2):
                V("pe", lambda e, ri=ri, pb=pb: e.transpose(
                    pb[:, ri * 128:(ri + 1) * 128], Cnat[:, ri, :, :].rearrange("p a b -> p (a b)"), ident_f),
                    reads=[bC, b_const], writes=[bb])
            V("dve", lambda e, pb=pb: e.tensor_copy(CA, pb[:, 0:128].rearrange("p (g h) -> p g h", g=8)),
              reads=[bb], writes=[bC])
            V("dve", lambda e, pb=pb: e.tensor_copy(CB, pb[:, 128:256].rearrange("p (g h) -> p g h", g=8)),
              reads=[bb], writes=[bC])
            V("dve", lambda e: e.tensor_copy(Cst[lo], CA[lo]), reads=[bC], writes=[bC])
            V("dve", lambda e: e.tensor_scalar(Cst[hi], CB[hi], -1.0, None, ALU.mult), reads=[bC], writes=[bC])

            def dense_tab(dst, bdst, Ta, Tb, La, Lb_, bsrc):
                for k0 in range(0, 33, 4):
                    kn = min(4, 33 - k0)

                    def bc_tab(T):
                        return T.unsqueeze(1).to_broadcast([128, kn, 8, 16])

                    def bc_pow(Lt):
                        return Lt[:, :, k0:k0 + kn].rearrange("p g k -> p k g").unsqueeze(3).to_broadcast(
                            [128, kn, 8, 16])
                    t1v = t1[:, 0:kn, :].rearrange("p k (g h) -> p k g h", g=8)
                    t2v = t2[:, 0:kn, :].rearrange("p k (g h) -> p k g h", g=8)
                    V("dve", lambda e: e.tensor_tensor(t1v, bc_tab(Ta), bc_pow(La), ALU.mult),
                      reads=[bsrc, bL, bt], writes=[bt])
                    V("dve", lambda e: e.tensor_tensor(t2v, bc_tab(Tb), bc_pow(Lb_), ALU.mult),
                      reads=[bsrc, bL, bt], writes=[bt])
                    V("dve", lambda e: e.tensor_tensor(dst[:, k0:k0 + kn, :], t1[:, 0:kn, :], t2[:, 0:kn, :],
                                                       ALU.add), reads=[bt], writes=[bdst])

            dense_tab(BLd, bBLd, Bbr, Bbi, LA, LB, bB_)
            V("pool", lambda e: e.memset(pad, 0.0), writes=[bpad])
            for d4 in range(8):
                V("dve", lambda e, d4=d4: e.tensor_copy(
                    padded(pad), BLd[:, d4 * 4:(d4 + 1) * 4, :].rearrange("p k (g h) -> p k g h", g=8)),
                    reads=[bBLd], writes=[bpad])
                pb, bb = bank()
                for dd in range(4):
                    for gl in range(8):
                        V("pe", lambda e, dd=dd, gl=gl, pb=pb: e.matmul(
                            pb[:, dd * 128 + gl * 16: dd * 128 + (gl + 1) * 16], pad[:, dd, gl, :],
                            Cst[:, gl, :], start=True, stop=True), reads=[bpad, bC], writes=[bb])
                if d4 == 0:
                    V("dve", lambda e, pb=pb: e.scalar_tensor_tensor(Ktab[:, 0, :], ident_f, dcol[:, 0:1],
                                                                     pb[:, 0:128], ALU.mult, ALU.add),
                      reads=[bb, bC, b_const], writes=[bK])
                    V("act", lambda e, pb=pb: e.copy(Ktab[:, 1:4, :],
                                                     pb[:, 128:512].rearrange("p (a b) -> p a b", a=3)),
                      reads=[bb], writes=[bK])
                else:
                    V("act", lambda e, pb=pb, d4=d4: e.copy(Ktab[:, d4 * 4:d4 * 4 + 4, :],
                                                            pb[:].rearrange("p (a b) -> p a b", a=4)),
                      reads=[bb], writes=[bK])
            for s8 in range(4):
                pb, bb = bank()
                pv = pb[:].bitcast(BF16)
                for ss in range(8):
                    sg = s8 * 8 + ss
                    V("pe", lambda e, sg=sg, ss=ss, pv=pv: e.transpose(pv[:, ss * 128:(ss + 1) * 128],
                                                                       BLd[:, 31 - sg, :], ident_bf),
                      reads=[bBLd, b_const], writes=[bb])
                pv3 = pv.rearrange("p (a b) -> p a b", a=8)
                V("act", lambda e, s8=s8, pv3=pv3: e.copy(Wd[:, s8 * 8:(s8 + 1) * 8, :], pv3), reads=[bb],
                  writes=[bWd])
                V("dve", lambda e, s8=s8, pv3=pv3: e.tensor_copy(Wds[:, s8 * 8:(s8 + 1) * 8, 0:64],
                                                                 pv3[:, :, 64:128]), reads=[bb], writes=[bWd])
                V("dve", lambda e, s8=s8, pv3=pv3: e.tensor_copy(Wds[:, s8 * 8:(s8 + 1) * 8, 64:128],
                                                                 pv3[:, :, 0:64]), reads=[bb], writes=[bWd])
            pbA, bbA = bank()
            pbB, bbB = bank()
            u3 = um.rearrange("p (c s) -> p c s", s=32)
            for gl in range(8):
                V("dve", lambda e, gl=gl: e.tensor_scalar(um, uc, rowmask[:, gl:gl + 1], None, ALU.mult),
                  reads=[buc, b_const], writes=[bum])
                for sg in range(32):
                    V("pe", lambda e, gl=gl, sg=sg: e.matmul(pbA[:, gl * 64:(gl + 1) * 64], Wd[:, sg, :],
                                                             u3[:, :, sg], start=(sg == 0), stop=(sg == 31)),
                      reads=[bWd, bum], writes=[bbA])
                for sg in range(32):
                    V("pe", lambda e, gl=gl, sg=sg: e.matmul(pbB[:, gl * 64:(gl + 1) * 64], Wds[:, sg, :],
                                                             u3[:, :, sg], start=(sg == 0), stop=(sg == 31)),
                      reads=[bWd, bum], writes=[bbB])
            V("pool", lambda e: e.memset(XA[0][:, :, 0:32], 0.0), writes=[bX])
            V("pool", lambda e: e.memset(XA[1][:, :, 0:32], 0.0), reads=[bX], writes=[bX])
            V("pool", lambda e: e.memset(XB[0][:, :, 0:32], 0.0), reads=[bX], writes=[bX])
            V("pool", lambda e: e.memset(XB[1][:, :, 0:32], 0.0), reads=[bX], writes=[bX])
            V("dve", lambda e: e.tensor_copy(XA[0][:, :, 32:96], pbA[:].rearrange("p (g c) -> p g c", g=8)),
              reads=[bbA, bX], writes=[bX])
            V("dve", lambda e: e.tensor_copy(XB[0][:, :, 32:96], pbB[:].rearrange("p (g c) -> p g c", g=8)),
              reads=[bbB, bX], writes=[bX])
            V("dve", lambda e: e.tensor_copy(cpow[:, 0, :], Lre[:, :, 32]), reads=[bL, bX], writes=[bX])
            V("dve", lambda e: e.tensor_scalar(cpow[lo, 1, :], Lim[lo, :, 32], -1.0, None, ALU.mult),
              reads=[bL, bX], writes=[bX])
            V("dve", lambda e: e.tensor_copy(cpow[hi, 1, :], Lim[hi, :, 32]), reads=[bL, bX], writes=[bX])
            cur = 0
            for m in range(6):
                s_ = 1 << m
                A0, B0, A1, B1 = XA[cur], XB[cur], XA[1 - cur], XB[1 - cur]
                cab = cpow[:, 0, :].unsqueeze(2).to_broadcast([128, 8, 64])
                cbb = cpow[:, 1, :].unsqueeze(2).to_broadcast([128, 8, 64])
                As, Bs = A0[:, :, 32 - s_:96 - s_], B0[:, :, 32 - s_:96 - s_]
                V("dve", lambda e, As=As, cab=cab: e.tensor_tensor(xt1, As, cab, ALU.mult), reads=[bX], writes=[bX])
                V("dve", lambda e, Bs=Bs, cbb=cbb: e.tensor_tensor(xt2, Bs, cbb, ALU.mult), reads=[bX], writes=[bX])
                V("dve", lambda e: e.tensor_tensor(xt1, xt1, xt2, ALU.add), reads=[bX], writes=[bX])
                V("dve", lambda e, A0=A0, A1=A1: e.tensor_tensor(A1[:, :, 32:96], A0[:, :, 32:96], xt1, ALU.add),
                  reads=[bX], writes=[bX])
                V("dve", lambda e, Bs=Bs, cab=cab: e.tensor_tensor(xt1, Bs, cab, ALU.mult), reads=[bX], writes=[bX])
                V("dve", lambda e, As=As, cbb=cbb: e.tensor_tensor(xt2, As, cbb, ALU.mult), reads=[bX], writes=[bX])
                V("dve", lambda e: e.tensor_tensor(xt1, xt1, xt2, ALU.subtract), reads=[bX], writes=[bX])
                V("dve", lambda e, B0=B0, B1=B1: e.tensor_tensor(B1[:, :, 32:96], B0[:, :, 32:96], xt1, ALU.add),
                  reads=[bX], writes=[bX])
                cur = 1 - cur
                if m < 5:
                    V("dve", lambda e: e.tensor_tensor(cpow[:, 2, :], cpow[:, 0, :], cpow[:, 0, :], ALU.mult),
                      reads=[bX], writes=[bX])
                    V("dve", lambda e: e.tensor_tensor(cpow[:, 3, :], cpow[:, 1, :], cpow[:, 1, :], ALU.mult),
                      reads=[bX], writes=[bX])
                    V("dve", lambda e: e.scalar_tensor_tensor(cpow[:, 1, :], cpow[:, 0, :], 2.0, cpow[:, 1, :],
                                                              ALU.mult, ALU.mult), reads=[bX], writes=[bX])
                    V("dve", lambda e: e.tensor_tensor(cpow[:, 0, :], cpow[:, 2, :], cpow[:, 3, :], ALU.subtract),
                      reads=[bX], writes=[bX])
            V("act", lambda e, cur=cur: e.copy(Abf, XA[cur]), reads=[bX], writes=[bAbf])
            dense_tab(Gd, bGd, CA, CB, LA2, LB2, bC)
            rot["list"] = [0, 1, 2, 3]
            rot["i"] = 0
            accb = [4, 5, 6, 7]
            u3c = uc.rearrange("p (c s) -> p s c", s=32)
            for bi in range(4):
                pb, bb = banks[accb[bi]], bankbufs[accb[bi]]
                pb3 = pb[:].rearrange("p (s c) -> p s c", c=64)
                for d in range(0, bi * 8 + 8):
                    tl0 = max(bi * 8, d)
                    nt = bi * 8 + 8 - tl0
                    V("pe", lambda e, d=d, tl0=tl0, nt=nt, pb3=pb3, bi=bi: e.matmul(
                        pb3[:, tl0 - bi * 8:tl0 - bi * 8 + nt, :], Ktab[:, d, :], u3c[:, tl0 - d:tl0 - d + nt, :],
                        start=(d == 0), stop=False), reads=[bK, buc], writes=[bb])
            for k4 in range(8):
                V("dve", lambda e, k4=k4: e.tensor_copy(
                    padded(pad), Gd[:, 1 + k4 * 4:1 + (k4 + 1) * 4, :].rearrange("p k (g h) -> p k g h", g=8)),
                    reads=[bGd], writes=[bpad])
                pb, bb = banks[accb[k4 // 2]], bankbufs[accb[k4 // 2]]
                for kl in range(4):
                    col = ((k4 % 2) * 4 + kl) * 64
                    for gl in range(8):
                        V("pe", lambda e, kl=kl, gl=gl, pb=pb, col=col: e.matmul(
                            pb[:, col:col + 64], pad[:, kl, gl, :], Abf[:, gl, 31:95],
                            start=False, stop=(gl == 7)), reads=[bpad, bAbf], writes=[bb])
            yc3 = yT_c[:, ct, :].rearrange("p (c s) -> p s c", s=32)
            for bi in range(4):
                pb, bb = banks[accb[bi]], bankbufs[accb[bi]]
                V("act", lambda e, pb=pb: e.activation(out=g2, in_=pb, func=AF.Square), reads=[bb], writes=[bg])
                V("dve", lambda e: e.tensor_scalar(g2, g2, 0.044715, 1.0, ALU.mult, ALU.add), reads=[bg],
                  writes=[bg])
                V("dve", lambda e, pb=pb: e.tensor_tensor(g2, g2, pb, ALU.mult), reads=[bg, bb], writes=[bg])
                V("act", lambda e: e.activation(out=g2, in_=g2, func=AF.Sigmoid,
                                                scale=2.0 * math.sqrt(2.0 / math.pi)), reads=[bg], writes=[bg])
                V("dve", lambda e, bi=bi, pb=pb: e.tensor_tensor(
                    yc3[:, bi * 8:(bi + 1) * 8, :], g2.rearrange("p (s c) -> p s c", c=64),
                    pb[:].rearrange("p (s c) -> p s c", c=64), ALU.mult), reads=[bg, bb], writes=[b_yT_c])
            rot["list"] = list(range(8))
            rot["i"] = 0
        slab, bslab = ws_get()
        for tb in range(4):
            for ct in range(4):
                pb, bb = proj_fm(slab, bslab, ct * 128, 128, tb, kts=4, rhs_src=yT_c, rhs_bufs=[b_yT_c])
                V("act", lambda e, ct=ct, pb=pb: e.activation(out=sig[:, ct, :], in_=pb, func=AF.Sigmoid),
                  reads=[bb], writes=[bsig])
            V("dve", lambda e, tb=tb: e.tensor_tensor(yT_c[:, :, tb * 512:(tb + 1) * 512],
                                                      yT_c[:, :, tb * 512:(tb + 1) * 512], sig, ALU.mult),
              reads=[bsig, b_yT_c], writes=[b_yT_c])

    def merge(l, yT, b_yT, mergedT, bmerged):
        sg = [ar.alloc("R", [512], F32) for _ in range(3)]
        bsg = [Buf("sg%d" % i) for i in range(3)]
        acc = ar.alloc("R", [512], F32)
        bacc = Buf("macc")
        brn = ("w_br_a", "w_br_b", "w_br_c")
        for et in range(KT):
            ws_request("w_in", l, KT, 384, [(0, C_GT + i * 1024 + et * 128, 128, i * 128) for i in range(3)])
            ws_request(brn, l, 4, 384, [(0, et * 128, 128, i * 128) for i in range(3)])
        for et in range(KT):
            (sg_slab, bsg_slab), (sb_slab, bsb_slab) = ws_get(2)
            for tb in range(4):
                for i in range(3):
                    pb, bb = proj_fm(sg_slab, bsg_slab, i * 128, 128, tb)
                    kb.op("act", lambda e, i=i, pb=pb: e.activation(out=sg[i], in_=pb, func=AF.Sigmoid),
                          reads=[bb], writes=[bsg[i]])
                for i in range(3):
                    pb, bb = proj_fm(sb_slab, bsb_slab, i * 128, 128, tb, kts=4, rhs_src=yT[i], rhs_bufs=[b_yT[i]])
                    if i == 0:
                        kb.op("dve", lambda e, pb=pb: e.tensor_tensor(acc, sg[0], pb, ALU.mult),
                              reads=[bb, bsg[0]], writes=[bacc])
                    else:
                        kb.op("dve", lambda e, pb=pb, i=i: e.tensor_tensor(sg[i], sg[i], pb, ALU.mult),
                              reads=[bb, bsg[i]], writes=[bsg[i]])
                        if i == 1:
                            kb.op("dve", lambda e: e.tensor_tensor(acc, acc, sg[1], ALU.add),
                                  reads=[bacc, bsg[1]], writes=[bacc])
                        else:
                            kb.op("dve", lambda e, et=et, tb=tb: e.tensor_tensor(
                                mergedT[:, et, tb * 512:(tb + 1) * 512], acc, sg[2], ALU.add),
                                reads=[bacc, bsg[2]], writes=[bmerged])

    def tail(l, mergedT, bmerged, res_src, b_res, dst, b_dst, last):
        ar.reset("Y")
        Yr = lambda shape, dt: ar.alloc("Y", shape, dt)
        Rr = lambda shape, dt: ar.alloc("R", shape, dt)
        h1 = Yr([4, D], F32)
        bh1 = [Buf("h1_%d" % i) for i in range(4)]
        h1T = Yr([KT, 512], BF16)
        bh1T = Buf("h1T")
        aT = Yr([NFT, 512], BF16)
        baT = Buf("aT")
        xs = [Rr([D], F32) for _ in range(4)]
        bxs = [Buf("txs%d" % i) for i in range(4)]
        rs = [Rr([D], F32) for _ in range(4)]
        brs = [Buf("rs%d" % i) for i in range(4)]
        xb = [Rr([D], BF16), Rr([D], BF16)]
        bxb = [Buf("txb0"), Buf("txb1")]
        stat = [Rr([16], F32), Rr([16], F32)]
        bstat = [Buf("st0"), Buf("st1")]
        sgt = [Rr([512], F32) for _ in range(4)]
        bsgt = [Buf("sgt%d" % i) for i in range(4)]
        ho = [Rr([D], F32), Rr([D], F32)]
        bho = [Buf("ho0"), Buf("ho1")]

        def load_ln(gname, bname):
            kb.dma("sp", lng, dr[gname][l].partition_broadcast(128), writes=[b_lng])
            kb.dma("sp", lnb, dr[bname][l].partition_broadcast(128), writes=[b_lnb])

        for tb in range(4):
            for half in range(2):
                ws_request("w_out", l, KT, 512, [(0, half * 512, 512, 0)])
            for f4 in range(6):
                nf = 4 if f4 < 5 else 2
                for nm in ("w_ffn_gate", "w_ffn_up"):
                    ws_request(nm, l, KT, nf * 128, [(0, f4 * 512, nf * 128, 0)])
            for half in range(2):
                for k3 in range(3):
                    nk = 8 if k3 < 2 else 6
                    ws_request("w_ffn_down", l, nk, 512, [(k3 * 1024, half * 512, 512, 0)])
        ti = 0
        for tb in range(4):
            load_ln("ln1_g", "ln1_b")
            for tq in range(4):
                tt = tb * 4 + tq
                kb.dma("sp", rs[tq], res_src[tt * 128:(tt + 1) * 128, :], reads=[b_res], writes=[brs[tq]])
            for half in range(2):
                slab, bslab = ws_get()
                for tq in range(4):
                    tt = tb * 4 + tq
                    pb, bb = bank()
                    for kt in range(KT):
                        kb.op("pe", lambda e: e.matmul(
                            pb, mergedT[:, kt, tt * 128:(tt + 1) * 128], slab[:, kt, 0:512], start=(kt == 0),
                            stop=(kt == KT - 1)), reads=[bslab, bmerged], writes=[bb])
                    kb.op("dve", lambda e: e.scalar_tensor_tensor(
                        xs[tq][:, half * 512:(half + 1) * 512], rs[tq][:, half * 512:(half + 1) * 512], ALPHA, pb,
                        ALU.mult, ALU.add), reads=[bb, brs[tq]], writes=[bxs[tq]])
            for tq in range(4):
                i = ti % 2
                ti += 1
                layer_norm_tile(xs[tq], bxs[tq], lng, lnb, h1[:, tq, :], bh1[tq], stat[i], bstat[i])
                kb.op("act", lambda e: e.copy(xb[i], h1[:, tq, :]), reads=[bh1[tq]], writes=[bxb[i]])
                pb, bb = bank()
                pv = pb[:].bitcast(BF16)
                for kt in range(KT):
                    kb.op("pe", lambda e: e.transpose(pv[:, kt * 128:(kt + 1) * 128],
                                                      xb[i][:, kt * 128:(kt + 1) * 128], ident_bf),
                          reads=[bxb[i], b_const], writes=[bb])
                kb.op("act", lambda e: e.copy(h1T[:, :, tq * 128:(tq + 1) * 128],
                                              pv.rearrange("p (k t) -> p k t", k=KT)),
                      reads=[bb], writes=[bh1T])
            for f4 in range(6):
                nf = 4 if f4 < 5 else 2
                slab, bslab = ws_get()
                for fl in range(nf):
                    pbg, bbg = bank()
                    for kt in range(KT):
                        kb.op("pe", lambda e: e.matmul(
                            pbg, slab[:, kt, fl * 128:(fl + 1) * 128], h1T[:, kt, :], start=(kt == 0),
                            stop=(kt == KT - 1)), reads=[bslab, bh1T], writes=[bbg])
                    kb.op("act", lambda e: e.activation(out=sgt[fl], in_=pbg, func=AF.Silu),
                          reads=[bbg], writes=[bsgt[fl]])
                slab, bslab = ws_get()
                for fl in range(nf):
                    ft = f4 * 4 + fl
                    pbu, bbu = bank()
                    for kt in range(KT):
                        kb.op("pe", lambda e: e.matmul(
                            pbu, slab[:, kt, fl * 128:(fl + 1) * 128], h1T[:, kt, :], start=(kt == 0),
                            stop=(kt == KT - 1)), reads=[bslab, bh1T], writes=[bbu])
                    kb.op("dve", lambda e: e.tensor_tensor(aT[:, ft, :], sgt[fl], pbu, ALU.mult),
                          reads=[bbu, bsgt[fl]], writes=[baT])
            load_ln("ln2_g", "ln2_b")
            for half in range(2):
                accs = [bank() for _ in range(4)]
                for k3 in range(3):
                    nk = 8 if k3 < 2 else 6
                    slab, bslab = ws_get()
                    for tq in range(4):
                        pb, bb = accs[tq]
                        for kk_ in range(nk):
                            ft = k3 * 8 + kk_
                            kb.op("pe", lambda e: e.matmul(
                                pb, aT[:, ft, tq * 128:(tq + 1) * 128], slab[:, kk_, 0:512], start=(ft == 0),
                                stop=(ft == NFT - 1)), reads=[bslab, baT], writes=[bb])
                for tq in range(4):
                    pb, bb = accs[tq]
                    kb.op("dve", lambda e: e.scalar_tensor_tensor(
                        h1[:, tq, half * 512:(half + 1) * 512], h1[:, tq, half * 512:(half + 1) * 512], ALPHA, pb,
                        ALU.mult, ALU.add), reads=[bb, bh1[tq]], writes=[bh1[tq]])
            for tq in range(4):
                tt = tb * 4 + tq
                i = ti % 2
                ti += 1
                layer_norm_tile(h1[:, tq, :], bh1[tq], lng, lnb, ho[i], bho[i], stat[i], bstat[i])
                kb.dma("sp", dst[tt * 128:(tt + 1) * 128, :], ho[i], reads=[bho[i]], writes=[b_dst])
                if not last:
                    kb.op("act", lambda e: e.copy(xb[i], ho[i]), reads=[bho[i]], writes=[bxb[i]])
                    transpose_to_hT(xb[i], bxb[i], tt)

    bounds = {'pe': [], 'dve': [], 'act': []}
    def mark(nm):
        for e in bounds: bounds[e].append((nm, kb.cnt[e]))
    kb.bounds = bounds
    setup_consts()
    load_input()
    mark('input')
    yT = [ar.alloc("Y", [4, L], BF16) for _ in range(3)]
    b_yT = [Buf("yT%d" % i) for i in range(3)]
    for l in range(n_layers):
        kb.barrier()
        hgrn2(l, yT[0], b_yT[0])
        mark('hgrn2_%d' % l)
        kb.barrier()
        fox(l, yT[1], b_yT[1])
        mark('fox_%d' % l)
        kb.barrier()
        s5(l, yT[2], b_yT[2])
        mark('s5_%d' % l)
        kb.barrier()
        if dbg and l == 0:
            for i, nm in enumerate(("dbg_ya", "dbg_yb", "dbg_yc")):
                dt_ = dbg_tensor(nm, [128, 4 * L])
                ar.reset("R")
                tmpf = ar.alloc("R", [4 * L], F32)
                bt_ = Buf("dbgt")
                kb.op("dve", lambda e, i=i: e.tensor_copy(tmpf, yT[i].rearrange("p a b -> p (a b)")),
                      reads=[b_yT[i]], writes=[bt_])
                kb.dma("sp", dt_, tmpf, reads=[bt_], writes=[Buf("dbgo")])
                kb.barrier()
        ar.reset("R")
        mergedT = ar.alloc("R", [KT, L], BF16)
        bmerged = Buf("merged")
        mark_m = ar.ptr["R"]
        merge(l, yT, b_yT, mergedT, bmerged)
        mark('merge_%d' % l)
        kb.barrier()
        ar.ptr["R"] = mark_m
        last = (l == n_layers - 1)
        res_src = dr["x"] if l == 0 else hscr
        b_res = Buf("xres") if l == 0 else b_hscr
        tail(l, mergedT, bmerged, res_src, b_res, out_d if last else hscr, b_out if last else b_hscr, last)
        mark('tail_%d' % l)
        ar.reset("Y")
        yT = [ar.alloc("Y", [4, L], BF16) for _ in range(3)]
    kb.barrier(engines=("sp",))
    return nc, kb, list(dbg_out.keys()), plan


_NC_CACHE = {}


def pack_weights(plan, inputs):
    wp = np.zeros((plan["size"],), dtype=np.float32)
    for key in plan["order"]:
        names, l, nk, ncols, pieces = key
        off = plan["off"][key]
        blk = wp[off:off + 128 * nk * ncols].reshape(128, nk, ncols)
        for nm, (row0, col0, n, dstc) in zip(names, pieces):
            w = inputs[nm][l]
            blk[:, :, dstc:dstc + n] = w[row0:row0 + nk * 128, col0:col0 + n].reshape(nk, 128, n).transpose(1, 0, 2)
    return wp


def get_program(n_layers=DEPTH, dbg=False):
    key = (n_layers, dbg)
    if key not in _NC_CACHE:
        _, _, _, plan0 = build(n_layers, dbg=dbg)
        nc, kb, dbgn, plan = build(n_layers, dbg=dbg, wp_size=plan0["size"])
        assert plan["size"] == plan0["size"]
        _NC_CACHE[key] = (nc, kb, dbgn, plan)
    return _NC_CACHE[key]


def make_in_maps(inputs, plan, batches):
    inputs = {k: np.asarray(v, dtype=np.float32) for k, v in inputs.items()}
    wp = pack_weights(plan, inputs)
    shared = {"wpack": wp}
    for name, shp in INPUT_SPECS:
        if name == "x" or name in PACKED:
            continue
        shared[name] = np.ascontiguousarray(inputs[name])
    maps = []
    for b in batches:
        m = dict(shared)
        m["x"] = np.ascontiguousarray(inputs["x"][b])
        maps.append(m)
    return maps


def kernel(**inputs):
    nc, kb, _, plan = get_program(DEPTH, False)
    in_maps = make_in_maps(inputs, plan, range(8))
    res = run_bass_kernel_spmd(nc, in_maps, core_ids=list(range(8)))
    out = np.stack([np.asarray(res.results[b]["out"], dtype=np.float32) for b in range(8)], axis=0)
    return out
```

```python
import math
import numpy as np
import concourse.bass as bass
import concourse.mybir as mybir
from concourse.bass_utils import run_bass_kernel_spmd

F32 = mybir.dt.float32
BF16 = mybir.dt.bfloat16
I32 = mybir.dt.int32
AF = mybir.ActivationFunctionType
ALU = mybir.AluOpType
AX = mybir.AxisListType

L = 2048
D = 1024
KT = 8
NTT = 16
DEPTH = 2
FFN = 2816
NFT = 22
ALPHA = (2 * DEPTH) ** 0.25
LN_EPS = 1e-5
RMS_EPS = 1e-6
C_HQ, C_HF, C_HI, C_HG = 0, 512, 1024, 1536
C_FQ, C_FK, C_FV, C_FF = 2048, 2560, 3072, 3584
C_SU = 3592
C_GT = 4104
TWO_PI = 2.0 * math.pi

INPUT_SPECS = [
    ("x", [L, D]), ("w_in", [2, 1024, 7176]), ("hg_lower_bounds", [2, 512]), ("hg_norm_w", [2, 128]),
    ("fox_b_f", [2, 8]), ("s5_lambda_re", [2, 32, 64]), ("s5_lambda_im", [2, 32, 64]),
    ("s5_log_step", [2, 32]), ("s5_b_re", [2, 32, 64, 16]), ("s5_b_im", [2, 32, 64, 16]),
    ("s5_c_re", [2, 32, 16, 64]), ("s5_c_im", [2, 32, 16, 64]), ("s5_d", [2, 32, 16]),
    ("s5_w_glu", [2, 512, 512]), ("w_br_a", [2, 512, 1024]), ("w_br_b", [2, 512, 1024]),
    ("w_br_c", [2, 512, 1024]), ("w_out", [2, 1024, 1024]), ("ln1_g", [2, 1024]), ("ln1_b", [2, 1024]),
    ("w_ffn_gate", [2, 1024, 2816]), ("w_ffn_up", [2, 1024, 2816]), ("w_ffn_down", [2, 2816, 1024]),
    ("ln2_g", [2, 1024]), ("ln2_b", [2, 1024]),
]


PACKED = ("w_in", "s5_w_glu", "w_br_a", "w_br_b", "w_br_c", "w_out", "w_ffn_gate", "w_ffn_up", "w_ffn_down")


class Buf:
    __slots__ = ("name", "w", "r")

    def __init__(self, name):
        self.name = name
        self.w = None
        self.r = {}


class KB:
    NDS = 12

    def __init__(self, nc):
        self.nc = nc
        self.E = {"pe": nc.tensor, "act": nc.scalar, "dve": nc.vector, "pool": nc.gpsimd, "sp": nc.sync}
        self.sem = {}
        self.cnt = {}
        self.seen = {e: {} for e in self.E}
        for e in self.E:
            self.sem[e] = nc.alloc_semaphore("s_" + e)
            self.cnt[e] = 0
        self.dq = {}
        for q in ("sp", "pool"):
            keys = []
            for i in range(self.NDS):
                k = "d_%s%d" % (q, i)
                self.sem[k] = nc.alloc_semaphore(k)
                self.cnt[k] = 0
                keys.append(k)
            self.dq[q] = [keys, 0]
        self.nins = 0

    def _wait(self, eng, deps):
        need = {}
        for (s, v) in deps:
            if v > need.get(s, 0):
                need[s] = v
        seen = self.seen[eng]
        for s, v in need.items():
            if eng == "pe" and s == "pe":
                continue
            if seen.get(s, 0) < v:
                self.E[eng].wait_ge(self.sem[s], v)
                seen[s] = v

    @staticmethod
    def _deps(reads, writes):
        deps = []
        for b in reads:
            if b.w is not None:
                deps.append(b.w)
        for b in writes:
            if b.w is not None:
                deps.append(b.w)
            deps.extend(b.r.items())
        return deps

    @staticmethod
    def _commit(ev, reads, writes):
        s, v = ev
        for b in writes:
            b.w = ev
            b.r = {}
        for b in reads:
            if b in writes:
                continue
            if b.r.get(s, 0) < v:
                b.r[s] = v

    def op(self, eng, fn, reads=(), writes=()):
        self._wait(eng, self._deps(reads, writes))
        ins = fn(self.E[eng])
        self.cnt[eng] += 1
        ins.then_inc(self.sem[eng], 1)
        self._commit((eng, self.cnt[eng]), reads, writes)
        self.nins += 1

    def dma(self, q, out, in_, reads=(), writes=(), slow=False):
        keys, i = self.dq[q]
        k = keys[i % len(keys)]
        self.dq[q][1] = i + 1
        deps = self._deps(reads, writes)
        if self.cnt[k] > 0:
            deps.append((k, self.cnt[k]))
        self._wait(q, deps)
        if slow:
            ins = self.E[q].dma_start(out=out, in_=in_, allow_slow_non_contiguous=True)
        else:
            ins = self.E[q].dma_start(out=out, in_=in_)
        self.cnt[k] += 16
        ins.then_inc(self.sem[k], 16)
        self._commit((k, self.cnt[k]), reads, writes)
        self.nins += 1

    def barrier(self, engines=("pe", "act", "dve", "pool", "sp")):
        deps = [(s, v) for s, v in self.cnt.items() if v > 0]
        for e in engines:
            self._wait(e, deps)


class Arena:
    def __init__(self, nc, nbytes):
        self.n = nbytes // 2
        self.t = nc.alloc_sbuf_tensor("arena", [128, self.n], BF16)
        self.ptr = {}
        self.base = {}
        self.lim = {}

    def region(self, name, start, end):
        self.base[name] = start
        self.ptr[name] = start
        self.lim[name] = end

    def reset(self, name):
        self.ptr[name] = self.base[name]

    def alloc(self, region, shape, dtype, parts=128):
        esz = 4 if dtype in (F32, I32) else 2
        n = 1
        for s in shape:
            n *= s
        nb16 = (n * esz + 1) // 2
        nb16 = (nb16 + 15) // 16 * 16
        off = self.ptr[region]
        assert off + nb16 <= self.lim[region], "arena region %s overflow: %d + %d > %d" % (
            region, off, nb16, self.lim[region])
        self.ptr[region] = off + nb16
        v = self.t[0:parts, off:off + (n * esz) // 2]
        if esz == 4:
            v = v.bitcast(dtype)
        if len(shape) == 2:
            v = v.rearrange("p (a b) -> p a b", a=shape[0])
        elif len(shape) == 3:
            v = v.rearrange("p (a b c) -> p a b c", a=shape[0], b=shape[1])
        elif len(shape) == 4:
            v = v.rearrange("p (a b c d) -> p a b c d", a=shape[0], b=shape[1], c=shape[2])
        return v


def build(n_layers=DEPTH, dbg=False, wp_size=None):
    nc = bass.Bass("TRN2", target_bir_lowering=False)
    kb = KB(nc)
    dr = {}
    plan = {"size": 0, "off": {}, "order": []}
    wp = nc.dram_tensor("wpack", [wp_size if wp_size else 48 * 1024 * 1024], F32, kind="ExternalInput").ap()
    for name, shp in INPUT_SPECS:
        if name in PACKED:
            continue
        dr[name] = nc.dram_tensor(name, shp, F32, kind="ExternalInput").ap()
    out_d = nc.dram_tensor("out", [L, D], F32, kind="ExternalOutput").ap()
    hscr = nc.dram_tensor("hscr", [L, D], F32, kind="Internal").ap()
    cumscr = nc.dram_tensor("cumscr", [8, L], F32, kind="Internal").ap()
    b_hscr = Buf("hscr")
    b_cumscr = Buf("cumscr")
    b_out = Buf("out")
    dbg_out = {}

    def dbg_tensor(name, shape):
        t = nc.dram_tensor(name, shape, F32, kind="ExternalOutput").ap()
        dbg_out[name] = t
        return t

    total = nc.sbuf_bytes_remaining - 1024
    total = total // 64 * 64
    ar = Arena(nc, total)
    NEL = ar.n
    P_END = 16384 + 3 * 4096 + 6144
    ar.region("P", 0, P_END)
    Y0 = P_END
    Y_END = Y0 + 3 * 8192
    ar.region("Y", Y0, Y_END)
    ar.region("R", Y_END, NEL)

    banks = []
    bankbufs = []
    for i in range(8):
        banks.append(nc.alloc_psum_tensor("pb%d" % i, [128, 512], F32)[:])
        bankbufs.append(Buf("pb%d" % i))
    rot = {"list": list(range(8)), "i": 0}

    def bank():
        lst = rot["list"]
        i = lst[rot["i"] % len(lst)]
        rot["i"] += 1
        return banks[i], bankbufs[i]

    hT = ar.alloc("P", [KT, L], BF16)
    b_hT = [[Buf("hT%d_%d" % (k, t)) for t in range(4)] for k in range(KT)]

    def hT_bufs(tb=None):
        if tb is None:
            return [b for row in b_hT for b in row]
        return [b_hT[k][tb] for k in range(KT)]

    wslab = [ar.alloc("P", [KT * 512], BF16) for _ in range(3)]
    b_wslab = [Buf("wslab%d" % i) for i in range(3)]
    ident_bf = ar.alloc("P", [128], BF16)
    maskU_bf = ar.alloc("P", [128], BF16)
    ones_bf = ar.alloc("P", [128], BF16)
    ident_f = ar.alloc("P", [128], F32)
    lng = ar.alloc("P", [D], F32)
    lnb = ar.alloc("P", [D], F32)
    b_lng, b_lnb = Buf("lng"), Buf("lnb")
    kk = ar.alloc("P", [33], F32)
    kk_i = ar.alloc("P", [33], I32)
    rowmask = ar.alloc("P", [8], F32)
    negmask = ar.alloc("P", [128], F32)
    eps_ln = ar.alloc("P", [1], F32)
    eps_rms = ar.alloc("P", [1], F32)
    b_const = Buf("const")

    ws = {"req": [], "issued": 0, "next": 0}

    def ws_request(name, l, nk, ncols, pieces):
        names = name if isinstance(name, tuple) else (name,) * len(pieces)
        key = (names, l, nk, ncols, tuple(pieces))
        if key not in plan["off"]:
            plan["off"][key] = plan["size"]
            plan["size"] += 128 * nk * ncols
            plan["order"].append(key)
        off = plan["off"][key]
        n = nk * ncols

        def fn(dst, bdst):
            kb.dma("pool", dst[:, 0:n], wp[off:off + 128 * n].rearrange("(p f) -> p f", p=128), writes=[bdst])
        ws["req"].append((fn, nk, ncols))

    def ws_get(k=1):
        n = ws["next"]
        while ws["issued"] < min(len(ws["req"]), n + 3):
            i = ws["issued"]
            ws["req"][i][0](wslab[i % 3], b_wslab[i % 3])
            ws["issued"] += 1
        ws["next"] = n + k
        outl = []
        for j in range(k):
            _, nk, ncols = ws["req"][n + j]
            v = wslab[(n + j) % 3][:, 0:nk * ncols].rearrange("p (k c) -> p k c", k=nk)
            outl.append((v, b_wslab[(n + j) % 3]))
        return outl[0] if k == 1 else outl

    def wload(dst, bdst, src):
        kb.dma("pool", dst, src, writes=[bdst])

    def setup_consts():
        P = kb.op
        P("pool", lambda e: e.memset(ones_bf, 1.0), writes=[b_const])
        P("pool", lambda e: e.affine_select(out=ident_bf, in_=ones_bf, pattern=[[1, 128]],
                                            compare_op=ALU.is_equal, fill=0.0, base=0, channel_multiplier=-1),
          reads=[b_const], writes=[b_const])
        P("pool", lambda e: e.affine_select(out=maskU_bf, in_=ones_bf, pattern=[[1, 128]],
                                            compare_op=ALU.is_ge, fill=0.0, base=0, channel_multiplier=-1),
          reads=[b_const], writes=[b_const])
        P("dve", lambda e: e.tensor_copy(ident_f, ident_bf), reads=[b_const], writes=[b_const])
        P("dve", lambda e: e.tensor_scalar(negmask, maskU_bf, 30000.0, -30000.0, ALU.mult, ALU.add),
          reads=[b_const], writes=[b_const])
        P("pool", lambda e: e.iota(kk_i, pattern=[[1, 33]], base=0, channel_multiplier=0), writes=[b_const])
        P("pool", lambda e: e.memset(eps_ln, LN_EPS), reads=[b_const], writes=[b_const])
        P("pool", lambda e: e.memset(eps_rms, RMS_EPS), reads=[b_const], writes=[b_const])
        P("dve", lambda e: e.tensor_copy(kk, kk_i), reads=[b_const], writes=[b_const])
        P("pool", lambda e: e.memset(rowmask, 1.0), reads=[b_const], writes=[b_const])
        P("pool", lambda e: e.affine_select(out=rowmask, in_=rowmask, pattern=[[-16, 8]],
                                            compare_op=ALU.is_ge, fill=0.0, base=0, channel_multiplier=1),
          reads=[b_const], writes=[b_const])
        P("pool", lambda e: e.affine_select(out=rowmask, in_=rowmask, pattern=[[16, 8]],
                                            compare_op=ALU.is_ge, fill=0.0, base=15, channel_multiplier=-1),
          reads=[b_const], writes=[b_const])

    def proj_fm(slab, bslab, c0, ncols, tb, kts=KT, rhs_src=None, rhs_bufs=None):
        pb, bb = bank()
        for kt in range(kts):
            if rhs_src is None:
                rhs = hT[:, kt, tb * 512:(tb + 1) * 512]
                rb = [b_hT[kt][tb]]
            else:
                rhs = rhs_src[:, kt, tb * 512:(tb + 1) * 512]
                rb = rhs_bufs
            kb.op("pe", lambda e, kt=kt, rhs=rhs: e.matmul(pb[0:ncols, :], slab[:, kt, c0:c0 + ncols], rhs,
                                                           start=(kt == 0), stop=(kt == kts - 1)),
                  reads=[bslab] + rb, writes=[bb])
        return pb, bb

    def w_in_cols(l, c0, n):
        return dr["w_in"][l, :, c0:c0 + n].rearrange("(kt p) e -> p kt e", p=128)

    def transpose_to_hT(src_bf, bsrc, tt):
        pb, bb = bank()
        pv = pb[:].bitcast(BF16)
        for kt in range(KT):
            kb.op("pe", lambda e, kt=kt: e.transpose(pv[:, kt * 128:(kt + 1) * 128],
                                                     src_bf[:, kt * 128:(kt + 1) * 128], ident_bf),
                  reads=[bsrc, b_const], writes=[bb])
        kb.op("act", lambda e: e.copy(hT[:, :, tt * 128:(tt + 1) * 128],
                                      pv.rearrange("p (k t) -> p k t", k=KT)),
              reads=[bb], writes=[b_hT[k][tt // 4] for k in range(KT)])

    def layer_norm_tile(xs, bxs, g_t, b_t, out_f32, bout, stat, bstat):
        kb.op("dve", lambda e: e.bn_stats(stat[:, 0:6], xs[:, 0:512]), reads=[bxs], writes=[bstat])
        kb.op("dve", lambda e: e.bn_stats(stat[:, 6:12], xs[:, 512:1024]), reads=[bxs, bstat], writes=[bstat])
        kb.op("dve", lambda e: e.bn_aggr(stat[:, 12:14], stat[:, 0:12]), reads=[bstat], writes=[bstat])
        kb.op("act", lambda e: e.activation(out=stat[:, 14:15], in_=stat[:, 13:14], func=AF.Sqrt, bias=eps_ln[:, 0:1]),
              reads=[bstat, b_const], writes=[bstat])
        kb.op("dve", lambda e: e.reciprocal(stat[:, 14:15], stat[:, 14:15]), reads=[bstat], writes=[bstat])
        kb.op("dve", lambda e: e.scalar_tensor_tensor(stat[:, 15:16], stat[:, 12:13], -1.0, stat[:, 14:15],
                                                      ALU.mult, ALU.mult), reads=[bstat], writes=[bstat])
        kb.op("act", lambda e: e.activation(out=xs, in_=xs, func=AF.Identity, bias=stat[:, 15:16],
                                            scale=stat[:, 14:15]), reads=[bxs, bstat], writes=[bxs])
        kb.op("dve", lambda e: e.tensor_tensor(xs, xs, g_t, ALU.mult), reads=[bxs, b_lng], writes=[bxs])
        kb.op("dve", lambda e: e.tensor_tensor(out_f32, xs, b_t, ALU.add), reads=[bxs, b_lnb], writes=[bout])

    def load_input():
        ar.reset("R")
        xs = [ar.alloc("R", [D], F32) for _ in range(2)]
        xb = [ar.alloc("R", [D], BF16) for _ in range(2)]
        bxs = [Buf("xs0"), Buf("xs1")]
        bxb = [Buf("xb0"), Buf("xb1")]
        for tt in range(NTT):
            i = tt % 2
            kb.dma("sp", xs[i], dr["x"][tt * 128:(tt + 1) * 128, :], writes=[bxs[i]])
            kb.op("dve", lambda e, i=i: e.tensor_copy(xb[i], xs[i]), reads=[bxs[i]], writes=[bxb[i]])
            transpose_to_hT(xb[i], bxb[i], tt)

    def hgrn2(l, yT_a, b_yT_a):
        ar.reset("R")
        R = lambda shape, dt, parts=128: ar.alloc("R", shape, dt, parts)
        lbraw = R([2, 4], F32)
        lb = R([4], F32)
        oml = R([4], F32)
        normw = R([1], F32)
        b_sm = Buf("hg_small")
        scanmask = R([L], F32)
        A = R([L], F32)
        B = R([L], F32)
        G = R([L], F32)
        bA, bB, bG = Buf("A"), Buf("B"), Buf("G")
        qg = R([L], BF16)
        kg = R([L], BF16)
        kte = R([L], BF16)
        bqg, bkg, bkte = Buf("qg"), Buf("kg"), Buf("kte")
        vtok = R([32, 128], BF16)
        kteT = R([32, 128], BF16)
        sTm = R([32, 64], BF16)
        states_bf = R([32, 128], BF16)
        st_pp = [R([128], F32), R([128], F32)]
        egl = R([32], F32)
        bvtok, bkteT, bsTm, bstbf, begl = Buf("vtok"), Buf("kteT"), Buf("sTm"), Buf("stbf"), Buf("egl")
        bstpp = [Buf("stpp0"), Buf("stpp1")]

        kb.dma("sp", lbraw, dr["hg_lower_bounds"].rearrange("l (h k) -> k l h", k=128), writes=[b_sm], slow=True)
        kb.dma("sp", normw, dr["hg_norm_w"][l].rearrange("(p o) -> p o", o=1), writes=[b_sm])
        if l == 0:
            kb.op("dve", lambda e: e.memset(lb, 0.0), reads=[b_sm], writes=[b_sm])
        else:
            kb.op("dve", lambda e: e.tensor_tensor(lb, lbraw[:, 1, :], lbraw[:, 0, :], ALU.subtract),
                  reads=[b_sm], writes=[b_sm])
            kb.op("act", lambda e: e.activation(out=lb, in_=lb, func=AF.Sigmoid), reads=[b_sm], writes=[b_sm])
        kb.op("dve", lambda e: e.tensor_scalar(oml, lb, -1.0, 1.0, ALU.mult, ALU.add), reads=[b_sm], writes=[b_sm])
        kb.op("pool", lambda e: e.memset(scanmask, 1.0), writes=[b_sm])
        kb.op("pool", lambda e: e.memset(scanmask.rearrange("p (c s) -> p c s", s=64)[:, :, 0:1], 0.0),
              reads=[b_sm], writes=[b_sm])

        for h in range(4):
            ws_request("w_in", l, KT, 512, [(0, c0 + h * 128, 128, j * 128)
                                             for j, c0 in enumerate((C_HQ, C_HF, C_HI, C_HG))])
        for h in range(4):
            slab, bslab = ws_get()
            G3 = G.rearrange("p (c s) -> p c s", s=64)
            B3 = B.rearrange("p (c s) -> p c s", s=64)
            for tb in range(4):
                pb, bb = proj_fm(slab, bslab, 128, 128, tb)
                kb.op("act", lambda e, tb=tb, pb=pb: e.activation(out=A[:, tb * 512:(tb + 1) * 512], in_=pb,
                                                                  func=AF.Sigmoid), reads=[bb], writes=[bA])
            kb.op("dve", lambda e, h=h: e.tensor_scalar(A, A, oml[:, h:h + 1], lb[:, h:h + 1], ALU.mult, ALU.add),
                  reads=[bA, b_sm], writes=[bA])
            kb.op("act", lambda e: e.activation(out=B, in_=A, func=AF.Ln), reads=[bA], writes=[bB])
            kb.op("dve", lambda e: e.tensor_scalar(A, A, -1.0, 1.0, ALU.mult, ALU.add), reads=[bA], writes=[bA])
            kb.op("dve", lambda e: e.tensor_tensor_scan(G, scanmask, B, 0.0, ALU.mult, ALU.add),
                  reads=[bB, b_sm], writes=[bG])
            kb.op("act", lambda e: e.activation(out=B, in_=G, func=AF.Exp), reads=[bG], writes=[bB])
            for tb in range(4):
                pb, bb = proj_fm(slab, bslab, 0, 128, tb)
                kb.op("dve", lambda e, tb=tb, pb=pb: e.tensor_tensor(qg[:, tb * 512:(tb + 1) * 512], pb,
                                                                     B[:, tb * 512:(tb + 1) * 512], ALU.mult),
                      reads=[bb, bB], writes=[bqg])
            kb.op("act", lambda e: e.activation(out=B, in_=G, func=AF.Exp, scale=-1.0), reads=[bG], writes=[bB])
            kb.op("dve", lambda e: e.tensor_tensor(kg, A, B, ALU.mult), reads=[bA, bB], writes=[bkg])
            kb.op("dve", lambda e: e.tensor_tensor(B3, G3[:, :, 63:64].to_broadcast([128, 32, 64]), G3,
                                                   ALU.subtract), reads=[bG], writes=[bB])
            kb.op("act", lambda e: e.activation(out=B, in_=B, func=AF.Exp), reads=[bB], writes=[bB])
            kb.op("dve", lambda e: e.tensor_tensor(kte, A, B, ALU.mult), reads=[bA, bB], writes=[bkte])
            kb.op("act", lambda e: e.activation(out=egl, in_=G3[:, :, 63], func=AF.Exp), reads=[bG], writes=[begl])
            for c4 in range(8):
                pb, bb = bank()
                for cc in range(4):
                    c = c4 * 4 + cc
                    for kt in range(KT):
                        kb.op("pe", lambda e, kt=kt, c=c, cc=cc, pb=pb: e.matmul(
                            pb[0:64, cc * 128:(cc + 1) * 128], hT[:, kt, c * 64:(c + 1) * 64],
                            slab[:, kt, 256:384], start=(kt == 0), stop=(kt == KT - 1)),
                            reads=[bslab, b_hT[kt][c // 8]], writes=[bb])
                kb.op("act", lambda e, c4=c4, pb=pb: e.copy(vtok[0:64, c4 * 4:(c4 + 1) * 4, :],
                                                            pb[0:64, :].rearrange("p (a b) -> p a b", a=4)),
                      reads=[bb], writes=[bvtok])
            for tb in range(4):
                pb, bb = proj_fm(slab, bslab, 384, 128, tb)
                kb.op("act", lambda e, tb=tb, pb=pb: e.activation(out=A[:, tb * 512:(tb + 1) * 512], in_=pb,
                                                                  func=AF.Silu), reads=[bb], writes=[bA])
            for c8 in range(4):
                pb, bb = bank()
                for cc in range(8):
                    c = c8 * 8 + cc
                    kb.op("pe", lambda e, c=c, cc=cc, pb=pb: e.matmul(
                        pb[0:64, cc * 64:(cc + 1) * 64], kg[:, c * 64:(c + 1) * 64], qg[:, c * 64:(c + 1) * 64],
                        start=True, stop=True), reads=[bkg, bqg], writes=[bb])
                kb.op("dve", lambda e, c8=c8, pb=pb: e.tensor_tensor(
                    sTm[0:64, c8 * 8:(c8 + 1) * 8, :], pb[0:64, :].rearrange("p (a b) -> p a b", a=8),
                    maskU_bf[0:64, 0:64].unsqueeze(1).to_broadcast([64, 8, 64]), ALU.mult),
                    reads=[bb, b_const], writes=[bsTm])
            for c8 in range(4):
                pb, bb = bank()
                pv = pb[:].bitcast(BF16)
                for cc in range(8):
                    c = c8 * 8 + cc
                    kb.op("pe", lambda e, c=c, cc=cc, pv=pv: e.transpose(
                        pv[0:64, cc * 128:(cc + 1) * 128], kte[:, c * 64:(c + 1) * 64], ident_bf),
                        reads=[bkte, b_const], writes=[bb])
                kb.op("act", lambda e, c8=c8, pv=pv: e.copy(kteT[0:64, c8 * 8:(c8 + 1) * 8, :],
                                                            pv[0:64, :].rearrange("p (a b) -> p a b", a=8)),
                      reads=[bb], writes=[bkteT])
            for c4 in range(8):
                pb, bb = bank()
                for cc in range(4):
                    c = c4 * 4 + cc
                    kb.op("pe", lambda e, c=c, cc=cc, pb=pb: e.matmul(
                        pb[:, cc * 128:(cc + 1) * 128], kteT[0:64, c, :], vtok[0:64, c, :], start=True, stop=True),
                        reads=[bkteT, bvtok], writes=[bb])
                for cc in range(4):
                    c = c4 * 4 + cc
                    cur, prv = st_pp[c % 2], st_pp[(c + 1) % 2]
                    bcur, bprv = bstpp[c % 2], bstpp[(c + 1) % 2]
                    if c == 0:
                        kb.op("dve", lambda e, pb=pb, cur=cur: e.tensor_copy(cur, pb[:, 0:128]),
                              reads=[bb], writes=[bcur])
                    else:
                        kb.op("dve", lambda e, pb=pb, cur=cur, prv=prv, c=c, cc=cc: e.scalar_tensor_tensor(
                            cur, prv, egl[:, c:c + 1], pb[:, cc * 128:(cc + 1) * 128], ALU.mult, ALU.add),
                            reads=[bb, bprv, begl], writes=[bcur])
                    kb.op("act", lambda e, cur=cur, c=c: e.copy(states_bf[:, c, :], cur), reads=[bcur],
                          writes=[bstbf])
            for c8 in range(4):
                pb, bb = bank()
                for cc in range(8):
                    c = c8 * 8 + cc
                    if c >= 1:
                        kb.op("pe", lambda e, c=c, cc=cc, pb=pb: e.matmul(
                            pb[:, cc * 64:(cc + 1) * 64], states_bf[:, c - 1, :], qg[:, c * 64:(c + 1) * 64],
                            start=True, stop=False), reads=[bstbf, bqg], writes=[bb])
                    kb.op("pe", lambda e, c=c, cc=cc, pb=pb: e.matmul(
                        pb[:, cc * 64:(cc + 1) * 64], vtok[0:64, c, :], sTm[0:64, c, :],
                        start=(c == 0), stop=True), reads=[bvtok, bsTm], writes=[bb])
                kb.op("act", lambda e, c8=c8, pb=pb: e.copy(B[:, c8 * 512:(c8 + 1) * 512], pb), reads=[bb],
                      writes=[bB])
                kb.op("act", lambda e, c8=c8, pb=pb: e.activation(out=kg[:, c8 * 512:(c8 + 1) * 512], in_=pb,
                                                                  func=AF.Square), reads=[bb], writes=[bkg])
            for tb in range(4):
                pb, bb = bank()
                kb.op("pe", lambda e, tb=tb, pb=pb: e.matmul(pb, ones_bf, kg[:, tb * 512:(tb + 1) * 512],
                                                             start=True, stop=True), reads=[bkg, b_const],
                      writes=[bb])
                kb.op("act", lambda e, tb=tb, pb=pb: e.activation(out=G[:, tb * 512:(tb + 1) * 512], in_=pb,
                                                                  func=AF.Sqrt, bias=eps_rms[:, 0:1],
                                                                  scale=1.0 / 128.0),
                      reads=[bb, b_const], writes=[bG])
            kb.op("dve", lambda e: e.reciprocal(G, G), reads=[bG], writes=[bG])
            kb.op("dve", lambda e: e.tensor_tensor(B, B, G, ALU.mult), reads=[bB, bG], writes=[bB])
            kb.op("dve", lambda e, h=h: e.scalar_tensor_tensor(yT_a[:, h, :], B, normw[:, 0:1], A, ALU.mult,
                                                               ALU.mult), reads=[bB, bA, b_sm], writes=[b_yT_a])

    def fox(l, yT_b, b_yT_b):
        ar.reset("R")
        R = lambda shape, dt, parts=128: ar.alloc("R", shape, dt, parts)
        qT = R([L], BF16)
        kT = R([L], BF16)
        vaug = R([NTT, 2, 65], BF16)
        bqT, bkT, bvaug = Buf("qT"), Buf("kT"), Buf("vaug")
        cum = R([L], F32, 8)
        cum2 = R([L], F32, 8)
        bcum = Buf("cum")
        bfb = R([1], F32, 8)
        ncum = R([NTT, 8], F32)
        bncum = Buf("ncum")
        cumbc = [R([L], F32), R([L], F32)]
        bcumbc = [Buf("cumbc0"), Buf("cumbc1")]
        tmp = [R([512], F32), R([512], F32)]
        btmp = [Buf("tmp0"), Buf("tmp1")]
        pT = [R([512], BF16), R([512], BF16), R([512], BF16)]
        bpT = [Buf("pT0"), Buf("pT1"), Buf("pT2")]
        yb = R([NTT, 512], BF16)
        byb = Buf("yb")
        rl = R([NTT], F32)
        brl = Buf("rl")
        ones8 = R([L], F32, 8)

        ws_request("w_in", l, KT, 8, [(0, C_FF, 8, 0)])
        for hp in range(4):
            ws_request("w_in", l, KT, 384, [(0, c0 + hp * 128, 128, j * 128)
                                             for j, c0 in enumerate((C_FQ, C_FK, C_FV))])
        slab, bslab = ws_get()
        kb.dma("sp", bfb, dr["fox_b_f"][l].rearrange("(p o) -> p o", o=1), writes=[bcum])
        kb.op("dve", lambda e: e.memset(ones8, 1.0), writes=[bcum])
        for tb in range(4):
            pb, bb = proj_fm(slab, bslab, 0, 8, tb)
            kb.op("act", lambda e, tb=tb, pb=pb: e.activation(out=cum[:, tb * 512:(tb + 1) * 512], in_=pb[0:8, :],
                                                              func=AF.Sigmoid, bias=bfb[:, 0:1]),
                  reads=[bb, bcum], writes=[bcum])
        kb.op("act", lambda e: e.activation(out=cum, in_=cum, func=AF.Ln), reads=[bcum], writes=[bcum])
        kb.op("dve", lambda e: e.tensor_tensor_scan(cum2, ones8, cum, 0.0, ALU.mult, ALU.add),
              reads=[bcum], writes=[bcum])
        kb.dma("sp", cumscr, cum2, reads=[bcum], writes=[b_cumscr])
        if dbg and l == 0:
            kb.dma("sp", dbg_tensor("dbg_cum", [8, L]), cum2, reads=[bcum], writes=[Buf("dbgcum")])
        pb, bb = bank()
        for S in range(NTT):
            kb.op("pe", lambda e, S=S, pb=pb: e.transpose(pb[:, S * 8:(S + 1) * 8], cum2[:, S * 128:(S + 1) * 128],
                                                          ident_f[0:8, 0:8]), reads=[bcum, b_const], writes=[bb])
        kb.op("dve", lambda e, pb=pb: e.tensor_scalar(ncum, pb[:, 0:128].rearrange("p (a b) -> p a b", a=NTT), -1.0,
                                                      None, ALU.mult), reads=[bb], writes=[bncum])
        kb.op("pool", lambda e: e.memset(vaug[:, :, :, 64:65], 1.0), writes=[bvaug])

        rot["list"] = [0, 1, 2, 3, 4]
        rot["i"] = 0

        def oacc(T):
            bi = 5 + T // 6
            j = T % 6
            return banks[bi][:, j * 65:(j + 1) * 65], bankbufs[bi]

        gi = 0
        for hp in range(4):
            slab, bslab = ws_get()
            for tb in range(4):
                pb, bb = proj_fm(slab, bslab, 0, 128, tb)
                kb.op("act", lambda e, tb=tb, pb=pb: e.activation(out=qT[:, tb * 512:(tb + 1) * 512], in_=pb,
                                                                  func=AF.Copy, scale=0.125), reads=[bb],
                      writes=[bqT])
                pb, bb = proj_fm(slab, bslab, 128, 128, tb)
                kb.op("act", lambda e, tb=tb, pb=pb: e.copy(kT[:, tb * 512:(tb + 1) * 512], pb), reads=[bb],
                      writes=[bkT])
            for t4 in range(4):
                pb, bb = bank()
                for tq in range(4):
                    tt = t4 * 4 + tq
                    for kt in range(KT):
                        kb.op("pe", lambda e, kt=kt, tt=tt, tq=tq, pb=pb: e.matmul(
                            pb[:, tq * 128:(tq + 1) * 128], hT[:, kt, tt * 128:(tt + 1) * 128],
                            slab[:, kt, 256:384], start=(kt == 0), stop=(kt == KT - 1)),
                            reads=[bslab, b_hT[kt][tt // 4]], writes=[bb])
                kb.op("act", lambda e, t4=t4, pb=pb: e.copy(
                    vaug[:, t4 * 4:(t4 + 1) * 4, :, 0:64],
                    pb[:].rearrange("p (a h d) -> p a h d", a=4, h=2)), reads=[bb], writes=[bvaug])
            for hh in range(2):
                h = hp * 2 + hh
                hb = hh * 64
                cb, bcb = cumbc[h % 2], bcumbc[h % 2]
                kb.dma("sp", cb, cumscr[h].partition_broadcast(128), reads=[b_cumscr], writes=[bcb])
                groups = []
                for S in range(NTT):
                    t0 = S * 128
                    first = True
                    while t0 < L:
                        n = min(512, L - t0)
                        groups.append((S, t0, n, first))
                        first = False
                        t0 += n
                LOOK = 2
                sc_banks = {}

                def emit_scores(gidx):
                    S, t0, n, first = groups[gidx]
                    pb, bb = bank()
                    kb.op("pe", lambda e: e.matmul(
                        pb[:, 0:n], kT[hb:hb + 64, S * 128:(S + 1) * 128], qT[hb:hb + 64, t0:t0 + n],
                        start=True, stop=True), reads=[bkT, bqT], writes=[bb])
                    sc_banks[gidx] = (pb, bb)

                for gidx in range(min(LOOK, len(groups))):
                    emit_scores(gidx)
                for gidx, (S, t0, n, first) in enumerate(groups):
                    if gidx + LOOK < len(groups):
                        emit_scores(gidx + LOOK)
                    pb, bb = sc_banks.pop(gidx)
                    tm, btm = tmp[gi % 2], btmp[gi % 2]
                    pt, bpt = pT[gi % 3], bpT[gi % 3]
                    gi += 1
                    kb.op("dve", lambda e: e.scalar_tensor_tensor(
                        tm[:, 0:n], cb[:, t0:t0 + n], ncum[:, S, h:h + 1], pb[:, 0:n], ALU.add, ALU.add),
                        reads=[bb, bcb, bncum], writes=[btm])
                    if first:
                        kb.op("dve", lambda e: e.tensor_tensor(tm[:, 0:128], tm[:, 0:128], negmask, ALU.add),
                              reads=[btm, b_const], writes=[btm])
                    kb.op("act", lambda e: e.activation(out=pt[:, 0:n], in_=tm[:, 0:n], func=AF.Exp),
                          reads=[btm], writes=[bpt])
                    for j in range(n // 128):
                        T = t0 // 128 + j
                        oa, boa = oacc(T)
                        kb.op("pe", lambda e: e.matmul(
                            oa, pt[:, j * 128:(j + 1) * 128], vaug[:, S, hh, :],
                            start=(S == 0 and T % 6 == 0), stop=(S == T), skip_group_check=True),
                            reads=[bpt, bvaug], writes=[boa])
                for bi, (Ta, Tn) in enumerate(((0, 6), (6, 6), (12, 4))):
                    pbk, bbk = banks[5 + bi], bankbufs[5 + bi]
                    v3 = pbk[:, 0:Tn * 65].rearrange("p (a b) -> p a b", b=65)
                    kb.op("dve", lambda e, Ta=Ta, Tn=Tn, v3=v3: e.reciprocal(rl[:, Ta:Ta + Tn], v3[:, :, 64]),
                          reads=[bbk], writes=[brl])
                    kb.op("dve", lambda e, Ta=Ta, Tn=Tn, v3=v3, h=h: e.tensor_tensor(
                        yb[:, Ta:Ta + Tn, h * 64:(h + 1) * 64], v3[:, :, 0:64],
                        rl[:, Ta:Ta + Tn].unsqueeze(2).to_broadcast([128, Tn, 64]), ALU.mult),
                        reads=[bbk, brl], writes=[byb])
        rot["list"] = list(range(8))
        for tt in range(NTT):
            pb, bb = bank()
            pv = pb[:].bitcast(BF16)
            for ct in range(4):
                kb.op("pe", lambda e, ct=ct, tt=tt, pv=pv: e.transpose(pv[:, ct * 128:(ct + 1) * 128],
                                                                       yb[:, tt, ct * 128:(ct + 1) * 128], ident_bf),
                      reads=[byb, b_const], writes=[bb])
            kb.op("act", lambda e, tt=tt, pv=pv: e.copy(yT_b[:, :, tt * 128:(tt + 1) * 128],
                                                        pv[:, 0:512].rearrange("p (c t) -> p c t", c=4)),
                  reads=[bb], writes=[b_yT_b])

    def s5(l, yT_c, b_yT_c):
        ar.reset("R")
        R = lambda shape, dt, parts=128: ar.alloc("R", shape, dt, parts)
        V = kb.op
        lo, hi = slice(0, 64), slice(64, 128)
        lre = R([8], F32)
        lim = R([8], F32)
        dts = R([8], F32)
        sm = R([8, 8], F32)
        bsm = Buf("s5small")
        Lre = R([8, 33], F32)
        Lim = R([8, 33], F32)
        LA = R([8, 33], F32)
        LB = R([8, 33], F32)
        LA2 = R([8, 33], F32)
        LB2 = R([8, 33], F32)
        ang = R([8, 33], F32)
        mag = R([8, 33], F32)
        angi = R([8, 33], I32)
        angf = R([8, 33], F32)
        angi2 = R([8, 33], I32)
        angf2 = R([8, 33], F32)
        bL = Buf("Ltab")
        bLre, bLim, bang, bts, btc = Buf("Lre"), Buf("Lim"), Buf("ang"), Buf("ts"), Buf("tc")
        Bre = R([8, 16], F32)
        Bim = R([8, 16], F32)
        Bbr = R([8, 16], F32)
        Bbi = R([8, 16], F32)
        bB_ = Buf("Btab")
        Cnat = R([2, 2, 64], F32)
        CA = R([8, 16], F32)
        CB = R([8, 16], F32)
        Cst = R([8, 16], BF16)
        bC = Buf("Ctab")
        dcol = R([1], F32)
        tt1 = [R([4, 128], F32), R([4, 128], F32)]
        tt2 = [R([4, 128], F32), R([4, 128], F32)]
        btt1 = [Buf("tt1a"), Buf("tt1b")]
        btt2 = [Buf("tt2a"), Buf("tt2b")]
        BLd = R([33, 128], BF16)
        bBLd = Buf("BLd")
        pad = R([4, 8, 128], BF16)
        bpad = Buf("pad")
        Ktab = R([32, 128], BF16)
        bK = Buf("Ktab")
        uc = R([L], BF16)
        buc = Buf("uc")
        Abf = R([8, 96], BF16)
        bAbf = Buf("Abf")
        g2 = R([512], F32)
        bg = Buf("gelu")
        sig = R([4, 512], F32)
        bsig = Buf("sig")
        mark = ar.ptr["R"]
        Wd = R([32, 128], BF16)
        Wds = R([32, 128], BF16)
        um = R([L], BF16)
        bWd = bum = bX = Buf("s5alias")
        ar.ptr["R"] = mark
        XA = [R([8, 96], F32), R([8, 96], F32)]
        XB = [R([8, 96], F32), R([8, 96], F32)]
        xt1 = R([8, 64], F32)
        xt2 = R([8, 64], F32)
        xt3 = R([8, 64], F32)
        xt4 = R([8, 64], F32)
        cpow = R([4, 8], F32)
        bXA = [Buf("XA0"), Buf("XA1")]
        bXB = [Buf("XB0"), Buf("XB1")]
        bx1, bx2, bx3, bx4, bcp = Buf("x1"), Buf("x2"), Buf("x3"), Buf("x4"), Buf("cpow")
        Gd = BLd
        bGd = bBLd

        ws_request("w_in", l, KT, 512, [(0, C_SU, 512, 0)])
        ws_request("s5_w_glu", l, 4, 512, [(0, 0, 512, 0)])
        slab, bslab = ws_get()

        def padded(Tpad):
            return bass.AP(Tpad.tensor, Tpad.offset, [list(Tpad.ap[0]), [8 * 128, 4], [144, 8], [1, 16]])

        for ct in range(4):
            g0 = ct * 8
            for tb in range(4):
                pb, bb = proj_fm(slab, bslab, ct * 128, 128, tb)
                V("act", lambda e, tb=tb, pb=pb: e.copy(uc[:, tb * 512:(tb + 1) * 512], pb), reads=[bb],
                  writes=[buc])
            for half in range(2):
                hs = slice(half * 64, half * 64 + 64)
                kb.dma("sp", lre[hs, :], dr["s5_lambda_re"][l, g0:g0 + 8, :].rearrange("g p -> p g"),
                       writes=[bsm], slow=True)
                kb.dma("sp", lim[hs, :], dr["s5_lambda_im"][l, g0:g0 + 8, :].rearrange("g p -> p g"),
                       writes=[bsm], slow=True)
                kb.dma("sp", Bre[hs, :, :], dr["s5_b_re"][l, g0:g0 + 8].rearrange("g p h -> p g h"),
                       writes=[bB_])
                kb.dma("sp", Bim[hs, :, :], dr["s5_b_im"][l, g0:g0 + 8].rearrange("g p h -> p g h"),
                       writes=[bB_])
            kb.dma("sp", dts, dr["s5_log_step"][l, g0:g0 + 8].partition_broadcast(128), writes=[bsm])
            for ri, nm in enumerate(("s5_c_re", "s5_c_im")):
                for dup in range(2):
                    kb.dma("sp", Cnat[:, ri, dup, :],
                           dr[nm][l, g0:g0 + 8].rearrange("g h p -> (g h) p"), writes=[bC])
            kb.dma("sp", dcol, dr["s5_d"][l, g0:g0 + 8].rearrange("g (h o) -> (g h) o", o=1), writes=[bC])
            S = lambda i: sm[:, i, :]
            V("act", lambda e: e.activation(out=dts, in_=dts, func=AF.Exp), reads=[bsm], writes=[bsm])
            V("dve", lambda e: e.tensor_tensor(S(0), lre, dts, ALU.mult), reads=[bsm], writes=[bsm])
            V("dve", lambda e: e.tensor_tensor(S(1), lim, dts, ALU.mult), reads=[bsm], writes=[bsm])
            kkb = kk.unsqueeze(1).to_broadcast([128, 8, 33])
            V("dve", lambda e: e.tensor_tensor(mag, S(0).unsqueeze(2).to_broadcast([128, 8, 33]), kkb, ALU.mult),
              reads=[bsm, b_const], writes=[bL])
            V("act", lambda e: e.activation(out=mag, in_=mag, func=AF.Exp), reads=[bL], writes=[bL])
            V("dve", lambda e: e.tensor_tensor(ang, S(1).unsqueeze(2).to_broadcast([128, 8, 33]), kkb, ALU.mult),
              reads=[bsm, b_const, bLim, bLre, bts, btc], writes=[bL, bang])
            def sin_ops(dst, bdst, ti, tf, btmp_, shift):
                C1 = 6.28125
                C2 = TWO_PI - C1
                return [
                    ("dve", lambda e: e.tensor_scalar(dst, ang, shift, 1.0 / TWO_PI, ALU.add, ALU.mult),
                     [bang], [bdst]),
                    ("dve", lambda e: e.tensor_copy(ti, dst), [bdst], [btmp_]),
                    ("dve", lambda e: e.tensor_copy(tf, ti), [btmp_], [btmp_]),
                    ("dve", lambda e: e.tensor_scalar(dst, ang, shift, None, ALU.add), [bang], [bdst]),
                    ("dve", lambda e: e.scalar_tensor_tensor(dst, tf, -C1, dst, ALU.mult, ALU.add),
                     [btmp_, bdst], [bdst]),
                    ("dve", lambda e: e.scalar_tensor_tensor(dst, tf, -C2, dst, ALU.mult, ALU.add),
                     [btmp_, bdst], [bdst]),
                    ("dve", lambda e: e.tensor_scalar(tf, dst, math.pi, -TWO_PI, ALU.is_gt, ALU.mult),
                     [bdst], [btmp_]),
                    ("dve", lambda e: e.tensor_tensor(dst, dst, tf, ALU.add), [bdst, btmp_], [bdst]),
                    ("dve", lambda e: e.tensor_scalar(tf, dst, -math.pi, TWO_PI, ALU.is_lt, ALU.mult),
                     [bdst], [btmp_]),
                    ("dve", lambda e: e.tensor_tensor(dst, dst, tf, ALU.add), [bdst, btmp_], [bdst]),
                    ("act", lambda e: e.activation(out=dst, in_=dst, func=AF.Sin), [bdst], [bdst]),
                ]
            chain_s = sin_ops(Lim, bLim, angi, angf, bts, 0.0)
            chain_c = sin_ops(Lre, bLre, angi2, angf2, btc, 0.5 * math.pi)
            for o1, o2 in zip(chain_s, chain_c):
                V(o1[0], o1[1], reads=o1[2], writes=o1[3])
                V(o2[0], o2[1], reads=o2[2], writes=o2[3])
            V("dve", lambda e: e.tensor_tensor(Lim, Lim, mag, ALU.mult), reads=[bLim, bL], writes=[bLim, bL])
            V("dve", lambda e: e.tensor_tensor(Lre, Lre, mag, ALU.mult), reads=[bLre, bL], writes=[bLre, bL])
            V("dve", lambda e: e.tensor_copy(LA[lo], Lre[lo]), reads=[bL], writes=[bL])
            V("dve", lambda e: e.tensor_copy(LA[hi], Lim[hi]), reads=[bL], writes=[bL])
            V("dve", lambda e: e.tensor_scalar(LB[lo], Lim[lo], -1.0, None, ALU.mult), reads=[bL], writes=[bL])
            V("dve", lambda e: e.tensor_copy(LB[hi], Lre[hi]), reads=[bL], writes=[bL])
            V("dve", lambda e: e.tensor_copy(LA2[lo], Lre[lo]), reads=[bL], writes=[bL])
            V("dve", lambda e: e.tensor_scalar(LA2[hi], Lim[hi], -1.0, None, ALU.mult), reads=[bL], writes=[bL])
            V("dve", lambda e: e.tensor_scalar(LB2[lo], Lim[lo], -1.0, None, ALU.mult), reads=[bL], writes=[bL])
            V("dve", lambda e: e.tensor_scalar(LB2[hi], Lre[hi], -1.0, None, ALU.mult), reads=[bL], writes=[bL])
            V("dve", lambda e: e.tensor_scalar(S(2), Lre[:, :, 1], -1.0, None, ALU.add), reads=[bL, bsm],
              writes=[bsm])
            V("dve", lambda e: e.tensor_tensor(S(3), lre, lre, ALU.mult), reads=[bsm], writes=[bsm])
            V("dve", lambda e: e.tensor_tensor(S(4), lim, lim, ALU.mult), reads=[bsm], writes=[bsm])
            V("dve", lambda e: e.tensor_tensor(S(3), S(3), S(4), ALU.add), reads=[bsm], writes=[bsm])
            V("dve", lambda e: e.reciprocal(S(3), S(3)), reads=[bsm], writes=[bsm])
            V("dve", lambda e: e.tensor_tensor(S(4), S(2), lre, ALU.mult), reads=[bsm], writes=[bsm])
            V("dve", lambda e: e.tensor_tensor(S(5), Lim[:, :, 1], lim, ALU.mult), reads=[bsm, bL], writes=[bsm])
            V("dve", lambda e: e.tensor_tensor(S(4), S(4), S(5), ALU.add), reads=[bsm], writes=[bsm])
            V("dve", lambda e: e.tensor_tensor(S(6), S(4), S(3), ALU.mult), reads=[bsm], writes=[bsm])
            V("dve", lambda e: e.tensor_tensor(S(4), Lim[:, :, 1], lre, ALU.mult), reads=[bsm, bL], writes=[bsm])
            V("dve", lambda e: e.tensor_tensor(S(5), S(2), lim, ALU.mult), reads=[bsm], writes=[bsm])
            V("dve", lambda e: e.tensor_tensor(S(4), S(4), S(5), ALU.subtract), reads=[bsm], writes=[bsm])
            V("dve", lambda e: e.tensor_tensor(S(7), S(4), S(3), ALU.mult), reads=[bsm], writes=[bsm])
            crb = S(6).unsqueeze(2).to_broadcast([128, 8, 16])
            cib = S(7).unsqueeze(2).to_broadcast([128, 8, 16])
            V("dve", lambda e: e.tensor_tensor(Bbr, Bre, crb, ALU.mult), reads=[bsm, bB_], writes=[bB_])
            V("dve", lambda e: e.tensor_tensor(Bbi, Bim, cib, ALU.mult), reads=[bsm, bB_], writes=[bB_])
            V("dve", lambda e: e.tensor_tensor(Bbr, Bbr, Bbi, ALU.subtract), reads=[bB_], writes=[bB_])
            V("dve", lambda e: e.tensor_tensor(Bbi, Bim, crb, ALU.mult), reads=[bsm, bB_], writes=[bB_])
            V("dve", lambda e: e.tensor_tensor(Bre, Bre, cib, ALU.mult), reads=[bsm, bB_], writes=[bB_])
            V("dve", lambda e: e.tensor_tensor(Bbi, Bbi, Bre, ALU.add), reads=[bB_], writes=[bB_])
            pb, bb = bank()
            for ri in range(2):
                V("pe", lambda e, ri=ri, pb=pb: e.transpose(
                    pb[:, ri * 128:(ri + 1) * 128], Cnat[:, ri, :, :].rearrange("p a b -> p (a b)"), ident_f),
                    reads=[bC, b_const], writes=[bb])
            V("dve", lambda e, pb=pb: e.tensor_copy(CA, pb[:, 0:128].rearrange("p (g h) -> p g h", g=8)),
              reads=[bb], writes=[bC])
            V("dve", lambda e, pb=pb: e.tensor_copy(CB, pb[:, 128:256].rearrange("p (g h) -> p g h", g=8)),
              reads=[bb], writes=[bC])
            V("dve", lambda e: e.tensor_copy(Cst[lo], CA[lo]), reads=[bC], writes=[bC])
            V("dve", lambda e: e.tensor_scalar(Cst[hi], CB[hi], -1.0, None, ALU.mult), reads=[bC], writes=[bC])

            def dense_tab(dst, bdst, Ta, Tb, La, Lb_, bsrc):
                pend = None
                for gi_, k0 in enumerate(range(0, 33, 4)):
                    kn = min(4, 33 - k0)
                    ta, tb_ = tt1[gi_ % 2], tt2[gi_ % 2]
                    bta, btb = btt1[gi_ % 2], btt2[gi_ % 2]

                    def bc_tab(T):
                        return T.unsqueeze(1).to_broadcast([128, kn, 8, 16])

                    def bc_pow(Lt):
                        return Lt[:, :, k0:k0 + kn].rearrange("p g k -> p k g").unsqueeze(3).to_broadcast(
                            [128, kn, 8, 16])
                    t1v = ta[:, 0:kn, :].rearrange("p k (g h) -> p k g h", g=8)
                    t2v = tb_[:, 0:kn, :].rearrange("p k (g h) -> p k g h", g=8)
                    V("dve", lambda e: e.tensor_tensor(t1v, bc_tab(Ta), bc_pow(La), ALU.mult),
                      reads=[bsrc, bL], writes=[bta])
                    V("dve", lambda e: e.tensor_tensor(t2v, bc_tab(Tb), bc_pow(Lb_), ALU.mult),
                      reads=[bsrc, bL], writes=[btb])
                    if pend is not None:
                        pend()
                    pend = (lambda ta=ta, tb_=tb_, bta=bta, btb=btb, k0=k0, kn=kn: V(
                        "dve", lambda e: e.tensor_tensor(dst[:, k0:k0 + kn, :], ta[:, 0:kn, :], tb_[:, 0:kn, :],
                                                         ALU.add), reads=[bta, btb], writes=[bdst]))
                pend()

            dense_tab(BLd, bBLd, Bbr, Bbi, LA, LB, bB_)
            V("pool", lambda e: e.memset(pad, 0.0), writes=[bpad])
            bdm = rowmask.unsqueeze(1).unsqueeze(3).to_broadcast([128, 4, 8, 16])
            Cflat = Cst.rearrange("p g h -> p (g h)")
            for d4 in range(8):
                pb, bb = bank()
                for dd in range(4):
                    V("pe", lambda e: e.matmul(pb[:, dd * 128:(dd + 1) * 128], BLd[:, d4 * 4 + dd, :], Cflat,
                                               start=True, stop=True), reads=[bBLd, bC], writes=[bb])
                pb4 = pb[:].rearrange("p (a g h) -> p a g h", a=4, g=8)
                kt4 = Ktab[:, d4 * 4:d4 * 4 + 4, :].rearrange("p a (g h) -> p a g h", g=8)
                V("dve", lambda e: e.tensor_tensor(kt4, pb4, bdm, ALU.mult), reads=[bb, b_const], writes=[bK])
                if d4 == 0:
                    V("dve", lambda e: e.scalar_tensor_tensor(Ktab[:, 0, :], ident_f, dcol[:, 0:1], Ktab[:, 0, :],
                                                              ALU.mult, ALU.add),
                      reads=[bC, b_const, bK], writes=[bK])
            for s8 in range(4):
                pb, bb = bank()
                pv = pb[:].bitcast(BF16)
                for ss in range(8):
                    sg = s8 * 8 + ss
                    V("pe", lambda e, sg=sg, ss=ss, pv=pv: e.transpose(pv[:, ss * 128:(ss + 1) * 128],
                                                                       BLd[:, 31 - sg, :], ident_bf),
                      reads=[bBLd, b_const], writes=[bb])
                pv3 = pv.rearrange("p (a b) -> p a b", a=8)
                V("act", lambda e, s8=s8, pv3=pv3: e.copy(Wd[:, s8 * 8:(s8 + 1) * 8, :], pv3), reads=[bb],
                  writes=[bWd])
                V("dve", lambda e, s8=s8, pv3=pv3: e.tensor_copy(Wds[:, s8 * 8:(s8 + 1) * 8, 0:64],
                                                                 pv3[:, :, 64:128]), reads=[bb], writes=[bWd])
                V("dve", lambda e, s8=s8, pv3=pv3: e.tensor_copy(Wds[:, s8 * 8:(s8 + 1) * 8, 64:128],
                                                                 pv3[:, :, 0:64]), reads=[bb], writes=[bWd])
            pbA, bbA = bank()
            pbB, bbB = bank()
            u3 = um.rearrange("p (c s) -> p c s", s=32)
            for gl in range(8):
                V("dve", lambda e, gl=gl: e.tensor_scalar(um, uc, rowmask[:, gl:gl + 1], None, ALU.mult),
                  reads=[buc, b_const], writes=[bum])
                for sg in range(32):
                    V("pe", lambda e, gl=gl, sg=sg: e.matmul(pbA[:, gl * 64:(gl + 1) * 64], Wd[:, sg, :],
                                                             u3[:, :, sg], start=(sg == 0), stop=(sg == 31)),
                      reads=[bWd, bum], writes=[bbA])
                for sg in range(32):
                    V("pe", lambda e, gl=gl, sg=sg: e.matmul(pbB[:, gl * 64:(gl + 1) * 64], Wds[:, sg, :],
                                                             u3[:, :, sg], start=(sg == 0), stop=(sg == 31)),
                      reads=[bWd, bum], writes=[bbB])
            fine = [bXA[0], bXA[1], bXB[0], bXB[1], bx1, bx2, bx3, bx4, bcp]
            V("pool", lambda e: e.memset(XA[0][:, :, 0:32], 0.0), writes=[bX] + fine)
            V("pool", lambda e: e.memset(XA[1][:, :, 0:32], 0.0), reads=[bX], writes=[bX])
            V("pool", lambda e: e.memset(XB[0][:, :, 0:32], 0.0), reads=[bX], writes=[bX])
            V("pool", lambda e: e.memset(XB[1][:, :, 0:32], 0.0), reads=[bX], writes=[bX] + fine)
            V("dve", lambda e: e.tensor_copy(XA[0][:, :, 32:96], pbA[:].rearrange("p (g c) -> p g c", g=8)),
              reads=[bbA, bX], writes=[bXA[0]])
            V("dve", lambda e: e.tensor_copy(XB[0][:, :, 32:96], pbB[:].rearrange("p (g c) -> p g c", g=8)),
              reads=[bbB, bX], writes=[bXB[0]])
            V("dve", lambda e: e.tensor_copy(cpow[:, 0, :], Lre[:, :, 32]), reads=[bL, bX], writes=[bcp])
            V("dve", lambda e: e.tensor_scalar(cpow[lo, 1, :], Lim[lo, :, 32], -1.0, None, ALU.mult),
              reads=[bL, bX], writes=[bcp])
            V("dve", lambda e: e.tensor_copy(cpow[hi, 1, :], Lim[hi, :, 32]), reads=[bL, bX], writes=[bcp])
            cur = 0
            for m in range(6):
                s_ = 1 << m
                A0, B0, A1, B1 = XA[cur], XB[cur], XA[1 - cur], XB[1 - cur]
                bA0, bB0, bA1, bB1 = bXA[cur], bXB[cur], bXA[1 - cur], bXB[1 - cur]
                cab = cpow[:, 0, :].unsqueeze(2).to_broadcast([128, 8, 64])
                cbb = cpow[:, 1, :].unsqueeze(2).to_broadcast([128, 8, 64])
                As, Bs = A0[:, :, 32 - s_:96 - s_], B0[:, :, 32 - s_:96 - s_]
                V("dve", lambda e: e.tensor_tensor(xt1, As, cab, ALU.mult), reads=[bA0, bcp], writes=[bx1])
                V("dve", lambda e: e.tensor_tensor(xt3, Bs, cab, ALU.mult), reads=[bB0, bcp], writes=[bx3])
                V("dve", lambda e: e.tensor_tensor(xt2, Bs, cbb, ALU.mult), reads=[bB0, bcp], writes=[bx2])
                V("dve", lambda e: e.tensor_tensor(xt4, As, cbb, ALU.mult), reads=[bA0, bcp], writes=[bx4])
                V("dve", lambda e: e.tensor_tensor(xt1, xt1, xt2, ALU.add), reads=[bx1, bx2], writes=[bx1])
                V("dve", lambda e: e.tensor_tensor(xt3, xt3, xt4, ALU.subtract), reads=[bx3, bx4], writes=[bx3])
                V("dve", lambda e: e.tensor_tensor(A1[:, :, 32:96], A0[:, :, 32:96], xt1, ALU.add),
                  reads=[bA0, bx1], writes=[bA1])
                V("dve", lambda e: e.tensor_tensor(B1[:, :, 32:96], B0[:, :, 32:96], xt3, ALU.add),
                  reads=[bB0, bx3], writes=[bB1])
                cur = 1 - cur
                if m < 5:
                    V("dve", lambda e: e.tensor_tensor(cpow[:, 2, :], cpow[:, 0, :], cpow[:, 0, :], ALU.mult),
                      reads=[bcp], writes=[bcp])
                    V("dve", lambda e: e.tensor_tensor(cpow[:, 3, :], cpow[:, 1, :], cpow[:, 1, :], ALU.mult),
                      reads=[bcp], writes=[bcp])
                    V("dve", lambda e: e.scalar_tensor_tensor(cpow[:, 1, :], cpow[:, 0, :], 2.0, cpow[:, 1, :],
                                                              ALU.mult, ALU.mult), reads=[bcp], writes=[bcp])
                    V("dve", lambda e: e.tensor_tensor(cpow[:, 0, :], cpow[:, 2, :], cpow[:, 3, :], ALU.subtract),
                      reads=[bcp], writes=[bcp])
            V("dve", lambda e: e.memset(cpow[:, 2, 0:1], 0.0),
              reads=[bXA[0], bXA[1], bXB[0], bXB[1], bx1, bx2, bx3, bx4, bcp], writes=[bX, bcp])
            V("act", lambda e, cur=cur: e.copy(Abf, XA[cur]), reads=[bX], writes=[bAbf])
            dense_tab(Gd, bGd, CA, CB, LA2, LB2, bC)
            rot["list"] = [0, 1, 2, 3]
            rot["i"] = 0
            accb = [4, 5, 6, 7]
            u3c = uc.rearrange("p (c s) -> p s c", s=32)
            for bi in range(4):
                pb, bb = banks[accb[bi]], bankbufs[accb[bi]]
                pb3 = pb[:].rearrange("p (s c) -> p s c", c=64)
                for d in range(0, bi * 8 + 8):
                    tl0 = max(bi * 8, d)
                    nt = bi * 8 + 8 - tl0
                    V("pe", lambda e, d=d, tl0=tl0, nt=nt, pb3=pb3, bi=bi: e.matmul(
                        pb3[:, tl0 - bi * 8:tl0 - bi * 8 + nt, :], Ktab[:, d, :], u3c[:, tl0 - d:tl0 - d + nt, :],
                        start=(d == 0), stop=False), reads=[bK, buc], writes=[bb])
            for k4 in range(8):
                V("dve", lambda e, k4=k4: e.tensor_copy(
                    padded(pad), Gd[:, 1 + k4 * 4:1 + (k4 + 1) * 4, :].rearrange("p k (g h) -> p k g h", g=8)),
                    reads=[bGd], writes=[bpad])
                pb, bb = banks[accb[k4 // 2]], bankbufs[accb[k4 // 2]]
                for kl in range(4):
                    col = ((k4 % 2) * 4 + kl) * 64
                    for gl in range(8):
                        V("pe", lambda e, kl=kl, gl=gl, pb=pb, col=col: e.matmul(
                            pb[:, col:col + 64], pad[:, kl, gl, :], Abf[:, gl, 31:95],
                            start=False, stop=(gl == 7)), reads=[bpad, bAbf], writes=[bb])
            yc3 = yT_c[:, ct, :].rearrange("p (c s) -> p s c", s=32)
            for bi in range(4):
                pb, bb = banks[accb[bi]], bankbufs[accb[bi]]
                V("act", lambda e, pb=pb: e.activation(out=g2, in_=pb, func=AF.Square), reads=[bb], writes=[bg])
                V("dve", lambda e: e.tensor_scalar(g2, g2, 0.044715, 1.0, ALU.mult, ALU.add), reads=[bg],
                  writes=[bg])
                V("dve", lambda e, pb=pb: e.tensor_tensor(g2, g2, pb, ALU.mult), reads=[bg, bb], writes=[bg])
                V("act", lambda e: e.activation(out=g2, in_=g2, func=AF.Sigmoid,
                                                scale=2.0 * math.sqrt(2.0 / math.pi)), reads=[bg], writes=[bg])
                V("dve", lambda e, bi=bi, pb=pb: e.tensor_tensor(
                    yc3[:, bi * 8:(bi + 1) * 8, :], g2.rearrange("p (s c) -> p s c", c=64),
                    pb[:].rearrange("p (s c) -> p s c", c=64), ALU.mult), reads=[bg, bb], writes=[b_yT_c])
            rot["list"] = list(range(8))
            rot["i"] = 0
        slab, bslab = ws_get()
        for tb in range(4):
            for ct in range(4):
                pb, bb = proj_fm(slab, bslab, ct * 128, 128, tb, kts=4, rhs_src=yT_c, rhs_bufs=[b_yT_c])
                V("act", lambda e, ct=ct, pb=pb: e.activation(out=sig[:, ct, :], in_=pb, func=AF.Sigmoid),
                  reads=[bb], writes=[bsig])
            V("dve", lambda e, tb=tb: e.tensor_tensor(yT_c[:, :, tb * 512:(tb + 1) * 512],
                                                      yT_c[:, :, tb * 512:(tb + 1) * 512], sig, ALU.mult),
              reads=[bsig, b_yT_c], writes=[b_yT_c])

    def merge(l, yT, b_yT, mergedT, bmerged):
        sg = [ar.alloc("R", [512], F32) for _ in range(3)]
        bsg = [Buf("sg%d" % i) for i in range(3)]
        acc = ar.alloc("R", [512], F32)
        bacc = Buf("macc")
        brn = ("w_br_a", "w_br_b", "w_br_c")
        for et in range(KT):
            ws_request("w_in", l, KT, 384, [(0, C_GT + i * 1024 + et * 128, 128, i * 128) for i in range(3)])
            ws_request(brn, l, 4, 384, [(0, et * 128, 128, i * 128) for i in range(3)])
        for et in range(KT):
            (sg_slab, bsg_slab), (sb_slab, bsb_slab) = ws_get(2)
            for tb in range(4):
                for i in range(3):
                    pb, bb = proj_fm(sg_slab, bsg_slab, i * 128, 128, tb)
                    kb.op("act", lambda e, i=i, pb=pb: e.activation(out=sg[i], in_=pb, func=AF.Sigmoid),
                          reads=[bb], writes=[bsg[i]])
                for i in range(3):
                    pb, bb = proj_fm(sb_slab, bsb_slab, i * 128, 128, tb, kts=4, rhs_src=yT[i], rhs_bufs=[b_yT[i]])
                    if i == 0:
                        kb.op("dve", lambda e, pb=pb: e.tensor_tensor(acc, sg[0], pb, ALU.mult),
                              reads=[bb, bsg[0]], writes=[bacc])
                    else:
                        kb.op("dve", lambda e, pb=pb, i=i: e.tensor_tensor(sg[i], sg[i], pb, ALU.mult),
                              reads=[bb, bsg[i]], writes=[bsg[i]])
                        if i == 1:
                            kb.op("dve", lambda e: e.tensor_tensor(acc, acc, sg[1], ALU.add),
                                  reads=[bacc, bsg[1]], writes=[bacc])
                        else:
                            kb.op("dve", lambda e, et=et, tb=tb: e.tensor_tensor(
                                mergedT[:, et, tb * 512:(tb + 1) * 512], acc, sg[2], ALU.add),
                                reads=[bacc, bsg[2]], writes=[bmerged])

    def tail(l, mergedT, bmerged, res_src, b_res, dst, b_dst, last):
        ar.reset("Y")
        Yr = lambda shape, dt: ar.alloc("Y", shape, dt)
        Rr = lambda shape, dt: ar.alloc("R", shape, dt)
        h1 = Yr([4, D], F32)
        bh1 = [Buf("h1_%d" % i) for i in range(4)]
        h1T = Yr([KT, 512], BF16)
        bh1T = Buf("h1T")
        aT = Yr([NFT, 512], BF16)
        baT = Buf("aT")
        xs = [Rr([D], F32) for _ in range(4)]
        bxs = [Buf("txs%d" % i) for i in range(4)]
        rs = [Rr([D], F32) for _ in range(4)]
        brs = [Buf("rs%d" % i) for i in range(4)]
        xb = [Rr([D], BF16), Rr([D], BF16)]
        bxb = [Buf("txb0"), Buf("txb1")]
        stat = [Rr([16], F32), Rr([16], F32)]
        bstat = [Buf("st0"), Buf("st1")]
        sgt = [Rr([512], F32) for _ in range(4)]
        bsgt = [Buf("sgt%d" % i) for i in range(4)]
        ho = [Rr([D], F32), Rr([D], F32)]
        bho = [Buf("ho0"), Buf("ho1")]

        def load_ln(gname, bname):
            kb.dma("sp", lng, dr[gname][l].partition_broadcast(128), writes=[b_lng])
            kb.dma("sp", lnb, dr[bname][l].partition_broadcast(128), writes=[b_lnb])

        for tb in range(4):
            for half in range(2):
                ws_request("w_out", l, KT, 512, [(0, half * 512, 512, 0)])
            for f4 in range(6):
                nf = 4 if f4 < 5 else 2
                for nm in ("w_ffn_gate", "w_ffn_up"):
                    ws_request(nm, l, KT, nf * 128, [(0, f4 * 512, nf * 128, 0)])
            for half in range(2):
                for k3 in range(3):
                    nk = 8 if k3 < 2 else 6
                    ws_request("w_ffn_down", l, nk, 512, [(k3 * 1024, half * 512, 512, 0)])
        ti = 0
        for tb in range(4):
            load_ln("ln1_g", "ln1_b")
            for tq in range(4):
                tt = tb * 4 + tq
                kb.dma("sp", rs[tq], res_src[tt * 128:(tt + 1) * 128, :], reads=[b_res], writes=[brs[tq]])
            for half in range(2):
                slab, bslab = ws_get()
                for tq in range(4):
                    tt = tb * 4 + tq
                    pb, bb = bank()
                    for kt in range(KT):
                        kb.op("pe", lambda e: e.matmul(
                            pb, mergedT[:, kt, tt * 128:(tt + 1) * 128], slab[:, kt, 0:512], start=(kt == 0),
                            stop=(kt == KT - 1)), reads=[bslab, bmerged], writes=[bb])
                    kb.op("dve", lambda e: e.scalar_tensor_tensor(
                        xs[tq][:, half * 512:(half + 1) * 512], rs[tq][:, half * 512:(half + 1) * 512], ALPHA, pb,
                        ALU.mult, ALU.add), reads=[bb, brs[tq]], writes=[bxs[tq]])
            for tq in range(4):
                i = ti % 2
                ti += 1
                layer_norm_tile(xs[tq], bxs[tq], lng, lnb, h1[:, tq, :], bh1[tq], stat[i], bstat[i])
                kb.op("act", lambda e: e.copy(xb[i], h1[:, tq, :]), reads=[bh1[tq]], writes=[bxb[i]])
                pb, bb = bank()
                pv = pb[:].bitcast(BF16)
                for kt in range(KT):
                    kb.op("pe", lambda e: e.transpose(pv[:, kt * 128:(kt + 1) * 128],
                                                      xb[i][:, kt * 128:(kt + 1) * 128], ident_bf),
                          reads=[bxb[i], b_const], writes=[bb])
                kb.op("act", lambda e: e.copy(h1T[:, :, tq * 128:(tq + 1) * 128],
                                              pv.rearrange("p (k t) -> p k t", k=KT)),
                      reads=[bb], writes=[bh1T])
            for f4 in range(6):
                nf = 4 if f4 < 5 else 2
                slab, bslab = ws_get()
                for fl in range(nf):
                    pbg, bbg = bank()
                    for kt in range(KT):
                        kb.op("pe", lambda e: e.matmul(
                            pbg, slab[:, kt, fl * 128:(fl + 1) * 128], h1T[:, kt, :], start=(kt == 0),
                            stop=(kt == KT - 1)), reads=[bslab, bh1T], writes=[bbg])
                    kb.op("act", lambda e: e.activation(out=sgt[fl], in_=pbg, func=AF.Silu),
                          reads=[bbg], writes=[bsgt[fl]])
                slab, bslab = ws_get()
                for fl in range(nf):
                    ft = f4 * 4 + fl
                    pbu, bbu = bank()
                    for kt in range(KT):
                        kb.op("pe", lambda e: e.matmul(
                            pbu, slab[:, kt, fl * 128:(fl + 1) * 128], h1T[:, kt, :], start=(kt == 0),
                            stop=(kt == KT - 1)), reads=[bslab, bh1T], writes=[bbu])
                    kb.op("dve", lambda e: e.tensor_tensor(aT[:, ft, :], sgt[fl], pbu, ALU.mult),
                          reads=[bbu, bsgt[fl]], writes=[baT])
            load_ln("ln2_g", "ln2_b")
            for half in range(2):
                accs = [bank() for _ in range(4)]
                for k3 in range(3):
                    nk = 8 if k3 < 2 else 6
                    slab, bslab = ws_get()
                    for tq in range(4):
                        pb, bb = accs[tq]
                        for kk_ in range(nk):
                            ft = k3 * 8 + kk_
                            kb.op("pe", lambda e: e.matmul(
                                pb, aT[:, ft, tq * 128:(tq + 1) * 128], slab[:, kk_, 0:512], start=(ft == 0),
                                stop=(ft == NFT - 1)), reads=[bslab, baT], writes=[bb])
                for tq in range(4):
                    pb, bb = accs[tq]
                    kb.op("dve", lambda e: e.scalar_tensor_tensor(
                        h1[:, tq, half * 512:(half + 1) * 512], h1[:, tq, half * 512:(half + 1) * 512], ALPHA, pb,
                        ALU.mult, ALU.add), reads=[bb, bh1[tq]], writes=[bh1[tq]])
            for tq in range(4):
                tt = tb * 4 + tq
                i = ti % 2
                ti += 1
                layer_norm_tile(h1[:, tq, :], bh1[tq], lng, lnb, ho[i], bho[i], stat[i], bstat[i])
                kb.dma("sp", dst[tt * 128:(tt + 1) * 128, :], ho[i], reads=[bho[i]], writes=[b_dst])
                if not last:
                    kb.op("act", lambda e: e.copy(xb[i], ho[i]), reads=[bho[i]], writes=[bxb[i]])
                    transpose_to_hT(xb[i], bxb[i], tt)

    bounds = {'pe': [], 'dve': [], 'act': []}
    def mark(nm):
        for e in bounds: bounds[e].append((nm, kb.cnt[e]))
    kb.bounds = bounds
    setup_consts()
    load_input()
    mark('input')
    yT = [ar.alloc("Y", [4, L], BF16) for _ in range(3)]
    b_yT = [Buf("yT%d" % i) for i in range(3)]
    for l in range(n_layers):
        kb.barrier()
        hgrn2(l, yT[0], b_yT[0])
        mark('hgrn2_%d' % l)
        kb.barrier()
        fox(l, yT[1], b_yT[1])
        mark('fox_%d' % l)
        kb.barrier()
        s5(l, yT[2], b_yT[2])
        mark('s5_%d' % l)
        kb.barrier()
        if dbg and l == 0:
            for i, nm in enumerate(("dbg_ya", "dbg_yb", "dbg_yc")):
                dt_ = dbg_tensor(nm, [128, 4 * L])
                ar.reset("R")
                tmpf = ar.alloc("R", [4 * L], F32)
                bt_ = Buf("dbgt")
                kb.op("dve", lambda e, i=i: e.tensor_copy(tmpf, yT[i].rearrange("p a b -> p (a b)")),
                      reads=[b_yT[i]], writes=[bt_])
                kb.dma("sp", dt_, tmpf, reads=[bt_], writes=[Buf("dbgo")])
                kb.barrier()
        ar.reset("R")
        mergedT = ar.alloc("R", [KT, L], BF16)
        bmerged = Buf("merged")
        mark_m = ar.ptr["R"]
        merge(l, yT, b_yT, mergedT, bmerged)
        mark('merge_%d' % l)
        kb.barrier()
        ar.ptr["R"] = mark_m
        last = (l == n_layers - 1)
        res_src = dr["x"] if l == 0 else hscr
        b_res = Buf("xres") if l == 0 else b_hscr
        tail(l, mergedT, bmerged, res_src, b_res, out_d if last else hscr, b_out if last else b_hscr, last)
        mark('tail_%d' % l)
        ar.reset("Y")
        yT = [ar.alloc("Y", [4, L], BF16) for _ in range(3)]
    kb.barrier(engines=("sp",))
    return nc, kb, list(dbg_out.keys()), plan


_NC_CACHE = {}


def pack_weights(plan, inputs):
    wp = np.zeros((plan["size"],), dtype=np.float32)
    for key in plan["order"]:
        names, l, nk, ncols, pieces = key
        off = plan["off"][key]
        blk = wp[off:off + 128 * nk * ncols].reshape(128, nk, ncols)
        for nm, (row0, col0, n, dstc) in zip(names, pieces):
            w = inputs[nm][l]
            blk[:, :, dstc:dstc + n] = w[row0:row0 + nk * 128, col0:col0 + n].reshape(nk, 128, n).transpose(1, 0, 2)
    return wp


def get_program(n_layers=DEPTH, dbg=False):
    key = (n_layers, dbg)
    if key not in _NC_CACHE:
        _, _, _, plan0 = build(n_layers, dbg=dbg)
        nc, kb, dbgn, plan = build(n_layers, dbg=dbg, wp_size=plan0["size"])
        assert plan["size"] == plan0["size"]
        _NC_CACHE[key] = (nc, kb, dbgn, plan)
    return _NC_CACHE[key]


def make_in_maps(inputs, plan, batches):
    inputs = {k: np.asarray(v, dtype=np.float32) for k, v in inputs.items()}
    wp = pack_weights(plan, inputs)
    shared = {"wpack": wp}
    for name, shp in INPUT_SPECS:
        if name == "x" or name in PACKED:
            continue
        shared[name] = np.ascontiguousarray(inputs[name])
    maps = []
    for b in batches:
        m = dict(shared)
        m["x"] = np.ascontiguousarray(inputs["x"][b])
        maps.append(m)
    return maps


def kernel(**inputs):
    nc, kb, _, plan = get_program(DEPTH, False)
    in_maps = make_in_maps(inputs, plan, range(8))
    res = run_bass_kernel_spmd(nc, in_maps, core_ids=list(range(8)))
    out = np.stack([np.asarray(res.results[b]["out"], dtype=np.float32) for b in range(8)], axis=0)
    return out
```
